# Optimizing a Trainium2 kernel written in Bass

```python
import jax, jax.numpy as jnp
from jax import lax
import numpy as np

D_MODEL = 2048
BATCH = 8
SEQ = 4096
DEPTH = 2
DEC_BATCH = 2
DEC_SEQ = 8192
PAST_LEN = 128

GRID_W = 64
HEAD_DIM = 128
NA_HEADS = 8
NA_WIDTH = NA_HEADS * HEAD_DIM
NA_WIN_ROWS_MAX = 8
NA_WIN_COLS = 16
RPB_ROWS = 2 * NA_WIN_ROWS_MAX - 1
RPB_COLS = 2 * NA_WIN_COLS - 1
MEM_HEADS = 4
MEM_WIDTH = MEM_HEADS * HEAD_DIM
N_MEM = 256
CONV_CH = D_MODEL - NA_WIDTH - MEM_WIDTH
CONV_K = 31
D_FF = 5632
FFN_CONV_K = 3
IN_WIDTH = 3 * NA_WIDTH + MEM_WIDTH + 2 * CONV_CH
SPLITS = [NA_WIDTH, 2 * NA_WIDTH, 3 * NA_WIDTH, 3 * NA_WIDTH + MEM_WIDTH, 3 * NA_WIDTH + MEM_WIDTH + CONV_CH]
EPS = 1e-6

kernel_name = "hybrid_natten_conformer_mem_encoder"


def rms_norm(x, g):
    xf = x.astype(jnp.float32)
    y = xf * lax.rsqrt(jnp.mean(xf * xf, axis=-1, keepdims=True) + EPS)
    return (y * g.astype(jnp.float32)).astype(x.dtype)


def layer_norm(x, g, b):
    xf = x.astype(jnp.float32)
    mu = jnp.mean(xf, axis=-1, keepdims=True)
    var = jnp.mean(jnp.square(xf - mu), axis=-1, keepdims=True)
    y = (xf - mu) * lax.rsqrt(var + EPS)
    return (y * g.astype(jnp.float32) + b.astype(jnp.float32)).astype(x.dtype)


def dwconv1d(x, w, b):
    k = w.shape[0]
    y = lax.conv_general_dilated(
        x, w[:, None, :], window_strides=(1,), padding=[((k - 1) // 2, k // 2)],
        dimension_numbers=('NWC', 'WIO', 'NWC'), feature_group_count=x.shape[-1])
    return y + b


def heads(a, n):
    return a.reshape(a.shape[0], a.shape[1], n, HEAD_DIM)


def neighbourhood_attention(q, k, v, rpb):
    b, t, h, dh = q.shape
    rows = t // GRID_W
    wr = min(NA_WIN_ROWS_MAX, rows)
    to_grid = lambda a: a.reshape(b, rows, GRID_W, h, dh).transpose(1, 0, 3, 2, 4)
    qg, kg, vg = to_grid(q), to_grid(k), to_grid(v)
    qcol = np.arange(GRID_W)
    col_start = np.clip(qcol - NA_WIN_COLS // 2, 0, GRID_W - NA_WIN_COLS)
    cols = col_start[:, None] + np.arange(NA_WIN_COLS)[None, :]
    dc_idx = cols - qcol[:, None] + (NA_WIN_COLS - 1)
    scale = dh ** -0.5

    def row_block(args):
        r, q_r = args
        rs = jnp.clip(r - wr // 2, 0, rows - wr)
        k_r = lax.dynamic_slice_in_dim(kg, rs, wr, axis=0)
        v_r = lax.dynamic_slice_in_dim(vg, rs, wr, axis=0)
        k_sel = k_r[:, :, :, cols, :]
        v_sel = v_r[:, :, :, cols, :]
        s = jnp.einsum('bhqd,wbhqcd->bhqwc', q_r, k_sel,
                       preferred_element_type=jnp.float32) * scale
        dr_idx = rs + jnp.arange(wr) - r + (NA_WIN_ROWS_MAX - 1)
        bias = rpb[:, dr_idx][:, :, dc_idx]
        s = s + bias.transpose(0, 2, 1, 3).astype(jnp.float32)[None]
        p = jax.nn.softmax(s.reshape(b, h, GRID_W, wr * NA_WIN_COLS), axis=-1)
        p = p.reshape(s.shape).astype(v.dtype)
        return jnp.einsum('bhqwc,wbhqcd->bhqd', p, v_sel)

    out = lax.map(row_block, (jnp.arange(rows), qg))
    return out.transpose(1, 0, 3, 2, 4).reshape(b, t, h * dh)


def memory_attention(q, k, v):
    s = jnp.einsum('bthd,bmhd->bhtm', q, k, preferred_element_type=jnp.float32) * (HEAD_DIM ** -0.5)
    p = jax.nn.softmax(s, axis=-1).astype(v.dtype)
    o = jnp.einsum('bhtm,bmhd->bthd', p, v)
    return o.reshape(o.shape[0], o.shape[1], MEM_WIDTH)


def encoder_layer(x, mem, norm_mix_g, w_in, na_q_norm_g, na_k_norm_g, na_rpb, mem_norm_g, w_mem_kv,
                  mem_q_norm_g, mem_k_norm_g, conv_dw_w, conv_dw_b, conv_ln_g, conv_ln_b, w_out,
                  norm_ffn_g, w_up, ffn_dw_w, ffn_dw_b, w_down):
    h = rms_norm(x, norm_mix_g)
    z = h @ w_in
    q_na, k_na, v_na, q_m, u_c, g_c = jnp.split(z, SPLITS, axis=-1)
    q_na = rms_norm(heads(q_na, NA_HEADS), na_q_norm_g)
    k_na = rms_norm(heads(k_na, NA_HEADS), na_k_norm_g)
    o_na = neighbourhood_attention(q_na, k_na, heads(v_na, NA_HEADS), na_rpb)
    kv_m = rms_norm(mem, mem_norm_g) @ w_mem_kv
    k_m, v_m = jnp.split(kv_m, 2, axis=-1)
    o_m = memory_attention(rms_norm(heads(q_m, MEM_HEADS), mem_q_norm_g),
                           rms_norm(heads(k_m, MEM_HEADS), mem_k_norm_g),
                           heads(v_m, MEM_HEADS))
    c = u_c * jax.nn.sigmoid(g_c)
    c = dwconv1d(c, conv_dw_w, conv_dw_b)
    c = jax.nn.silu(layer_norm(c, conv_ln_g, conv_ln_b))
    x = x + jnp.concatenate([o_na, o_m, c], axis=-1) @ w_out
    up = rms_norm(x, norm_ffn_g) @ w_up
    up = dwconv1d(up, ffn_dw_w, ffn_dw_b)
    gate, val = jnp.split(up, 2, axis=-1)
    return x + (jax.nn.silu(gate) * val) @ w_down


def setup_inputs(seed: int = 0) -> dict:
    key = jax.random.key(seed)
    ks = jax.random.split(key, 24)
    n = lambda k, s, sc: jax.random.normal(k, s, jnp.float32) * sc
    gain = lambda k, s: 1.0 + 0.02 * jax.random.normal(k, s, jnp.float32)
    L = DEPTH
    return {
        "x_prompt": n(ks[0], (BATCH, SEQ, D_MODEL), 1.0),
        "x_sample": n(ks[1], (DEC_BATCH, DEC_SEQ, D_MODEL), 1.0),
        "mem_prompt": n(ks[2], (BATCH, N_MEM, D_MODEL), 1.0),
        "mem_sample": n(ks[3], (DEC_BATCH, N_MEM, D_MODEL), 1.0),
        "norm_mix_g": gain(ks[4], (L, D_MODEL)),
        "w_in": n(ks[5], (L, D_MODEL, IN_WIDTH), D_MODEL ** -0.5),
        "na_q_norm_g": gain(ks[6], (L, HEAD_DIM)),
        "na_k_norm_g": gain(ks[7], (L, HEAD_DIM)),
        "na_rpb": n(ks[8], (L, NA_HEADS, RPB_ROWS, RPB_COLS), 0.1),
        "mem_norm_g": gain(ks[9], (L, D_MODEL)),
        "w_mem_kv": n(ks[10], (L, D_MODEL, 2 * MEM_WIDTH), D_MODEL ** -0.5),
        "mem_q_norm_g": gain(ks[11], (L, HEAD_DIM)),
        "mem_k_norm_g": gain(ks[12], (L, HEAD_DIM)),
        "conv_dw_w": n(ks[13], (L, CONV_K, CONV_CH), CONV_K ** -0.5),
        "conv_dw_b": n(ks[14], (L, CONV_CH), 0.02),
        "conv_ln_g": gain(ks[15], (L, CONV_CH)),
        "conv_ln_b": n(ks[16], (L, CONV_CH), 0.02),
        "w_out": n(ks[17], (L, D_MODEL, D_MODEL), D_MODEL ** -0.5),
        "norm_ffn_g": gain(ks[18], (L, D_MODEL)),
        "w_up": n(ks[19], (L, D_MODEL, 2 * D_FF), D_MODEL ** -0.5),
        "ffn_dw_w": n(ks[20], (L, FFN_CONV_K, 2 * D_FF), FFN_CONV_K ** -0.5),
        "ffn_dw_b": n(ks[21], (L, 2 * D_FF), 0.02),
        "w_down": n(ks[22], (L, D_FF, D_MODEL), D_FF ** -0.5),
    }


def reference(x_prompt, x_sample, mem_prompt, mem_sample, norm_mix_g, w_in, na_q_norm_g, na_k_norm_g,
              na_rpb, mem_norm_g, w_mem_kv, mem_q_norm_g, mem_k_norm_g, conv_dw_w, conv_dw_b, conv_ln_g,
              conv_ln_b, w_out, norm_ffn_g, w_up, ffn_dw_w, ffn_dw_b, w_down):
    params = (norm_mix_g, w_in, na_q_norm_g, na_k_norm_g, na_rpb, mem_norm_g, w_mem_kv, mem_q_norm_g,
              mem_k_norm_g, conv_dw_w, conv_dw_b, conv_ln_g, conv_ln_b, w_out, norm_ffn_g, w_up,
              ffn_dw_w, ffn_dw_b, w_down)

    def trunk(x, mem):
        for l in range(DEPTH):
            x = encoder_layer(x, mem, *[p[l] for p in params])
        return x

    y_prompt = trunk(x_prompt, mem_prompt)
    y_sample = trunk(x_sample, mem_sample)
    return (y_prompt, y_sample)
```

```python
import math
from collections import defaultdict
import numpy as np
import concourse.bass as bass
import concourse.mybir as mybir
from concourse.bass_utils import run_bass_kernel_spmd

F32 = mybir.dt.float32
BF16 = mybir.dt.bfloat16
AF = mybir.ActivationFunctionType
ALU = mybir.AluOpType

D = 2048
KC = 16
T = 512
NAW = 1024
MW = 512
CC = 512
DFF = 5632
HC = 44
INW = 4608
NMEM = 256
EPS = 1e-6
GW = 64
NEG = -30000.0
L = 2
NP = 540
P_G1, P_GM, P_G2, P_GQ, P_GK, P_GQM, P_GKM, P_CB, P_LNG, P_LNB, P_CW, P_FW, P_FB = \
    0, 16, 32, 48, 49, 50, 51, 52, 56, 60, 64, 188, 452
ARENA_WORDS = 43520


def _tw(r, ty, R):
    if ty == 'A':
        return int(np.clip(r - 4, 0, R - 8))
    half = R // 2
    base = 0 if r < half else half
    return base + int(np.clip(r - base - 4, 0, half - 8))


def geometry(NT):
    R = 8 * NT
    rows = []
    for r in range(R):
        a, b = _tw(r, 'A', R), _tw(r, 'B', R)
        lo, hi = min(a, b), max(a, b) + 8
        start = lo - (lo % 2)
        nch = (hi - start + 1) // 2
        assert start + 2 * nch <= R
        rows.append((start, nch, a == b))
    tiles = []
    packs = {}
    for i in range(NT):
        keys = []
        slotmap = {}
        for rl in range(8):
            r = 8 * i + rl
            start, nch, same = rows[r]
            rs = _tw(r, 'A', R)
            for ci in range(nch):
                kr = start + 2 * ci
                if same:
                    key = ('g', kr - r + 7, rs <= kr < rs + 8, rs <= kr + 1 < rs + 8)
                else:
                    key = ('s', rl, ci)
                if key not in keys:
                    keys.append(key)
                slotmap[(rl, ci)] = keys.index(key)
        sig = tuple(keys)
        if sig not in packs:
            packs[sig] = (len(packs), i)
        w0 = min(rows[8 * i + rl][0] for rl in range(8))
        w1 = max(rows[8 * i + rl][0] + 2 * rows[8 * i + rl][1] for rl in range(8))
        assert (w1 - w0) <= 16 and w0 % 2 == 0
        tiles.append(dict(keys=keys, slotmap=slotmap, pack=packs[sig][0], w0=w0, w1=w1))
    nslot = max(len(t['keys']) for t in tiles)
    packlist = sorted(packs.values())
    return dict(R=R, rows=rows, tiles=tiles, nslot=nslot, npack=len(packlist),
                pack_rep=[p[1] for p in packlist])


def build_gtab(rpb, ty, NT, geo):
    R = geo['R']
    qc = np.arange(GW)[:, None]
    kc = np.arange(GW)[None, :]
    cs = np.clip(qc - 8, 0, GW - 16)
    valid = (kc >= cs) & (kc < cs + 16)
    dc = np.clip(kc - qc + 15, 0, 30)
    Tf = np.where(valid[None, None, None], rpb[:, :, :, dc], np.float32(NEG)).astype(np.float32)
    negt = np.full((rpb.shape[0], 8, GW, GW), NEG, np.float32)
    out = np.full((rpb.shape[0], geo['npack'], 8, geo['nslot'], GW, 2 * GW), NEG, np.float32)
    for p, i in enumerate(geo['pack_rep']):
        t = geo['tiles'][i]
        for k, key in enumerate(t['keys']):
            halves = []
            if key[0] == 'g':
                _, dA, vA, vB = key
                for d, v in ((dA, vA), (dA + 1, vB)):
                    halves.append(Tf[:, :, d] if (v and 0 <= d <= 14) else negt)
            else:
                _, rl, ci = key
                r = 8 * i + rl
                start = geo['rows'][r][0]
                rs = _tw(r, ty, R)
                for hf in range(2):
                    kr = start + 2 * ci + hf
                    d = kr - r + 7
                    v = (rs <= kr < rs + 8)
                    halves.append(Tf[:, :, d] if (v and 0 <= d <= 14) else negt)
            out[:, p, :, k, :, 0:GW] = halves[0]
            out[:, p, :, k, :, GW:] = halves[1]
    return out


def build_pvec(inp):
    pv = np.zeros((L, 128, NP), np.float32)
    fm = lambda a, n: a.reshape(L, n, 128).transpose(0, 2, 1)
    pv[:, :, P_G1:P_G1 + 16] = fm(inp['norm_mix_g'], 16)
    pv[:, :, P_GM:P_GM + 16] = fm(inp['mem_norm_g'], 16)
    pv[:, :, P_G2:P_G2 + 16] = fm(inp['norm_ffn_g'], 16)
    pv[:, :, P_GQ] = inp['na_q_norm_g']
    pv[:, :, P_GK] = inp['na_k_norm_g']
    pv[:, :, P_GQM] = inp['mem_q_norm_g']
    pv[:, :, P_GKM] = inp['mem_k_norm_g']
    pv[:, :, P_CB:P_CB + 4] = fm(inp['conv_dw_b'], 4)
    pv[:, :, P_LNG:P_LNG + 4] = fm(inp['conv_ln_g'], 4)
    pv[:, :, P_LNB:P_LNB + 4] = fm(inp['conv_ln_b'], 4)
    cw = inp['conv_dw_w'].reshape(L, 31, 4, 128).transpose(0, 3, 2, 1)
    pv[:, :, P_CW:P_CW + 124] = cw.reshape(L, 128, 124)
    fw = inp['ffn_dw_w'].reshape(L, 3, 88, 128).transpose(0, 3, 2, 1)
    pv[:, :, P_FW:P_FW + 264] = fw.reshape(L, 128, 264)
    pv[:, :, P_FB:P_FB + 88] = fm(inp['ffn_dw_b'], 88)
    return pv


ENGS = ['pe', 'act', 'dve', 'pool', 'sp']
ENGATTR = {'pe': 'tensor', 'act': 'scalar', 'dve': 'vector', 'pool': 'gpsimd', 'sp': 'sync'}


class Prog:
    def __init__(self):
        self.cnt = defaultdict(int)
        self.ops = {e: [] for e in ENGS}
        self.seen = {e: defaultdict(int) for e in ENGS}
        self.lastw = {}
        self.readers = defaultdict(dict)
        self.dmasems = set()
        self.nobarrier = set()
        import os
        self.limit = int(os.environ.get('KLIMIT', '0')) or None
        self.nops = 0

    def _skip(self):
        self.nops += 1
        return self.limit is not None and self.nops > self.limit

    def _deps(self, reads, writes):
        need = {}

        def add(ev):
            s, v = ev
            if need.get(s, 0) < v:
                need[s] = v
        for k in reads:
            if k in self.lastw:
                add(self.lastw[k])
        for k in writes:
            if k in self.lastw:
                add(self.lastw[k])
            for s, v in self.readers[k].items():
                add((s, v))
        return need

    def _commit(self, ev, reads, writes):
        s, v = ev
        for k in reads:
            rd = self.readers[k]
            if rd.get(s, 0) < v:
                rd[s] = v
        for k in writes:
            self.lastw[k] = ev
            self.readers[k] = {}

    def _waits(self, eng, need):
        for s, v in need.items():
            if eng == 'pe' and s == 'pe':
                continue
            if self.seen[eng][s] >= v:
                continue
            self.seen[eng][s] = v
            self.ops[eng].append(('wait', s, v))

    def op(self, eng, fn, reads=(), writes=()):
        if self._skip():
            return None
        self._waits(eng, self._deps(reads, writes))
        self.cnt[eng] += 1
        ev = (eng, self.cnt[eng])
        self.ops[eng].append(('op', fn, True))
        self._commit(ev, reads, writes)
        return ev

    def group(self, eng, fns, reads=(), writes=()):
        if self._skip():
            return None
        self._waits(eng, self._deps(reads, writes))
        for f in fns[:-1]:
            self.ops[eng].append(('op', f, False))
        self.cnt[eng] += 1
        ev = (eng, self.cnt[eng])
        self.ops[eng].append(('op', fns[-1], True))
        self._commit(ev, reads, writes)
        return ev

    def dma(self, q, out, in_, sem, reads=(), writes=(), **kw):
        if self._skip():
            return None
        self._waits(q, self._deps(reads, writes))
        self.dmasems.add(sem)
        self.cnt[sem] += 16
        ev = (sem, self.cnt[sem])
        self.ops[q].append(('dma', out, in_, sem, kw))
        self._commit(ev, reads, writes)
        return ev

    def barrier(self):
        if self.limit is not None and self.nops > self.limit:
            return
        for e in ENGS:
            self._waits(e, {s_: c for s_, c in self.cnt.items() if c > 0 and s_ not in self.nobarrier})

    def wait_all(self, eng, keys):
        self._waits(eng, self._deps(keys, ()))


class Rot:
    def __init__(self, items):
        self.items = list(items)
        self.i = 0

    def next(self):
        v = self.items[self.i % len(self.items)]
        self.i += 1
        return v


class Arena:
    def __init__(self, ap, nwords):
        self.ap = ap
        self.n = nwords
        self.off = 0

    def _shape(self, v, shape):
        if len(shape) == 1:
            return v
        if len(shape) == 2:
            return v.rearrange("p (a b) -> p a b", a=shape[0])
        if len(shape) == 3:
            return v.rearrange("p (a b c) -> p a b c", a=shape[0], b=shape[1])
        raise ValueError

    def f32(self, *shape):
        n = int(np.prod(shape))
        v = self.ap[:, self.off:self.off + n]
        self.off += n
        assert self.off <= self.n, (self.off, self.n)
        return self._shape(v, shape)

    def bf16(self, *shape):
        n = int(np.prod(shape))
        w = (n + 1) // 2
        w += w % 2
        v = self.ap[:, self.off:self.off + w].bitcast(BF16)[:, 0:n]
        self.off += w
        assert self.off <= self.n, (self.off, self.n)
        return self._shape(v, shape)


def MM(out, lhsT, rhs, start, stop):
    return lambda e: e.matmul(out, lhsT, rhs, start=start, stop=stop)


def TR(out, in_, ident):
    return lambda e: e.transpose(out, in_, ident)


def ACTF(out, in_, func, scale=None, bias=None):
    kw = {}
    if scale is not None:
        kw['scale'] = scale
    if bias is not None:
        kw['bias'] = bias
    return lambda e: e.activation(out=out, in_=in_, func=func, **kw)


def TT(out, a, b, op):
    return lambda e: e.tensor_tensor(out=out, in0=a, in1=b, op=op)


def TS(out, a, s1, op0, s2=None, op1=None):
    if op1 is None:
        return lambda e: e.tensor_scalar(out=out, in0=a, scalar1=s1, scalar2=None, op0=op0)
    return lambda e: e.tensor_scalar(out=out, in0=a, scalar1=s1, scalar2=s2, op0=op0, op1=op1)


def STT(out, a, s, b, op0, op1):
    return lambda e: e.scalar_tensor_tensor(out=out, in0=a, scalar=s, in1=b, op0=op0, op1=op1)


def CP(out, in_):
    return lambda e: e.tensor_copy(out=out, in_=in_)


def RCP(out, in_):
    return lambda e: e.reciprocal(out=out, in_=in_)


def MSET(ap, c):
    return lambda e: e.memset(ap, c)


def build_program(NT, layers=L, stop_after=None, debug=False):
    geo = geometry(NT)
    LT = NT * T
    NSLOT, NPACK = geo['nslot'], geo['npack']
    nc = bass.Bass("TRN2", target_bir_lowering=False)

    def dt_(name, shape, dtype, kind):
        return nc.dram_tensor(name, list(shape), dtype, kind=kind).ap()

    xin = dt_("xin", [LT, D], F32, "ExternalInput")
    memin = dt_("memin", [2 * NMEM, D], F32, "ExternalInput")
    w_in = dt_("w_in", [L, D, INW], F32, "ExternalInput")
    w_kv = dt_("w_kv", [L, D, 2 * MW], F32, "ExternalInput")
    w_out = dt_("w_out", [L, D, D], F32, "ExternalInput")
    w_up = dt_("w_up", [L, D, 2 * DFF], F32, "ExternalInput")
    w_down = dt_("w_down", [L, DFF, D], F32, "ExternalInput")
    pvec = dt_("pvec", [L, 128, NP], F32, "ExternalInput")
    NGR = -(-(L * NPACK * 8 * NSLOT * GW * 2 * GW) // (2048 * 128)) * 128
    gtab = dt_("gtab", [NGR, 2048], F32, "ExternalInput")
    flagin = dt_("flagin", [128, 1], F32, "ExternalInput")
    identin = dt_("identin", [128, 128], F32, "ExternalInput")
    yout = dt_("yout", [LT, D], F32, "ExternalOutput")

    wb_in = dt_("wb_in", [L, D, INW], BF16, "Internal")
    wb_kv = dt_("wb_kv", [L, D, 2 * MW], BF16, "Internal")
    wb_out = dt_("wb_out", [L, D, D], BF16, "Internal")
    wb_up = dt_("wb_up", [L, D, 2 * DFF], BF16, "Internal")
    wb_down = dt_("wb_down", [L, DFF, D], BF16, "Internal")
    gtb = dt_("gtb", [NGR, 2048], BF16, "Internal")
    IK = "ExternalOutput" if debug else "Internal"
    XT = dt_("XT", [D, LT], F32, IK)
    QT = dt_("QT", [NAW, LT], BF16, IK)
    KT = dt_("KT", [NAW, LT], BF16, IK)
    VV = dt_("VV", [LT, NAW], BF16, IK)
    QMT = dt_("QMT", [MW, LT], BF16, IK)
    CT = dt_("CT", [CC, LT], BF16, IK)
    XM = dt_("XM", [D, LT], F32, IK)
    H2T = dt_("H2T", [D, LT], BF16, IK)
    X1T = dt_("X1T", [D, LT], F32, IK)

    P = Prog()
    fm = lambda ap: ap.rearrange("(c p) t -> p c t", p=128)

    ctx_arena = nc.sbuf_tensor("arena", [128, ARENA_WORDS], F32)
    ctx_psum = nc.psum_tensor("psum", [128, 8, 512], F32)
    arena_t = ctx_arena.__enter__()
    ps = ctx_psum.__enter__()
    A = Arena(arena_t, ARENA_WORDS)

    pv = A.f32(NP)
    ident_f = A.f32(128)
    ones_f = A.f32(128)
    neghalf = A.f32(512)
    flag = A.f32(1)
    epsc = A.f32(1)
    ident_b = A.bf16(128)
    ones_b = A.bf16(128)
    Km = A.bf16(2, 4, NMEM)
    Vm = A.bf16(2, 2, MW)
    h2halo = A.bf16(KC, 2 * NT)
    base_off = A.off

    def col(off, n=1):
        return pv[:, off:off + n]

    P.dma('sp', ident_f, identin, 'c0', writes=['ident_f'])
    P.dma('sp', flag, flagin, 'c1', writes=['flag'])
    P.op('dve', CP(ident_b, ident_f), reads=['ident_f'], writes=['ident_b'])
    P.op('pool', MSET(ones_b, 1.0), writes=['ones_b'])
    P.op('pool', MSET(ones_f, 1.0), writes=['ones_f'])
    P.op('pool', MSET(neghalf, -0.5), writes=['neghalf'])
    P.op('pool', MSET(epsc, EPS), writes=['epsc'])

    cast_jobs = []

    def plan_cast(dst2d, src2d, key, sem):
        rows, cols = src2d.shape
        assert rows % 128 == 0
        for r in range(0, rows, 128):
            cast_jobs.append((dst2d[r:r + 128, :], src2d[r:r + 128, :], sem))
        P.lastw[key] = (sem, 16 * (rows // 128))
        P.nobarrier.add(sem)

    def issue_casts(n):
        for _ in range(min(n, len(cast_jobs))):
            d_, s_, sem = cast_jobs.pop(0)
            P.dma('pool', d_, s_, sem, writes=[], max_dma_last_dim=4096)

    plan_cast(wb_kv[0], w_kv[0], ('wb_kv', 0), 'pkv0')
    plan_cast(wb_in[0], w_in[0], ('wb_in', 0), 'pin0')
    n_first = len(cast_jobs)
    plan_cast(wb_out[0], w_out[0], ('wb_out', 0), 'pout0')
    plan_cast(gtb, gtab, 'gtb', 'pgtb')
    n_second = len(cast_jobs) - n_first
    plan_cast(wb_up[0], w_up[0], ('wb_up', 0), 'pup0')
    plan_cast(wb_down[0], w_down[0], ('wb_down', 0), 'pdn0')
    if layers > 1:
        plan_cast(wb_kv[1], w_kv[1], ('wb_kv', 1), 'pkv1')
        plan_cast(wb_in[1], w_in[1], ('wb_in', 1), 'pin1')
        plan_cast(wb_out[1], w_out[1], ('wb_out', 1), 'pout1')
        plan_cast(wb_up[1], w_up[1], ('wb_up', 1), 'pup1')
        plan_cast(wb_down[1], w_down[1], ('wb_down', 1), 'pdn1')
    n_rest = len(cast_jobs) - n_first - n_second
    issue_casts(n_first)

    def rstd_chain(sumbank, inv_n, msb, rsb, kms, krs):
        P.op('act', ACTF(msb, ps[:, sumbank, :], AF.Sqrt, scale=inv_n, bias=epsc),
             reads=[('ps', sumbank), 'epsc'], writes=[kms])
        P.op('dve', RCP(rsb, msb), reads=[kms], writes=[krs])

    def sweepA(l, mode):
        A.off = base_off
        P.barrier()
        xtok = [A.f32(D), A.f32(D)] if mode != 'x1' else None
        xT = A.f32(KC, T)
        sq = [A.bf16(T), A.bf16(T), A.bf16(T)]
        msb = [A.f32(T), A.f32(T), A.f32(T)]
        rsb = [A.f32(T), A.f32(T), A.f32(T)]
        pfs = [A.f32(T), A.f32(T), A.f32(T)]
        h = A.bf16(KC, T)
        wblk = [A.bf16(KC, 256), A.bf16(KC, 256), A.bf16(KC, 256)]
        qstage = A.bf16(8, T)
        kstage = A.bf16(8, T)
        qmstage = A.bf16(4, T)
        vstage = A.bf16(4, NAW)
        ustage = A.f32(4, T)
        cstage = A.bf16(4, T)
        thb = [A.f32(T), A.f32(T)]
        PB = Rot([0, 1, 2, 3])
        SB = Rot([4, 5])
        VB = Rot([6, 7])
        tag = f"A{l}{mode}"
        ntile = NT if mode != 'mem' else 1
        gcol = P_GM if mode == 'mem' else P_G1
        wsrc = (wb_kv if mode == 'mem' else wb_in)[l]
        wkey = ('wb_kv', l) if mode == 'mem' else ('wb_in', l)
        nblk = (2 * MW if mode == 'mem' else INW) // 256
        wv = wsrc.rearrange("(c p) n -> p c n", p=128)
        wrot = Rot([0, 1, 2])
        slot2 = Rot([0, 1, 2])
        thr = Rot([0, 1])

        for i in range(ntile):
            t0, t1 = i * T, (i + 1) * T
            if mode == 'x1':
                P.dma('sp', xT, fm(X1T)[:, :, t0:t1], 'xT', reads=[('X1T', i, oc_) for oc_ in range(KC)],
                      writes=[('xT', kc) for kc in range(KC)])
            else:
                src = xin if mode == 'x0' else memin
                for s in range(4):
                    sl = s % 2
                    P.dma('sp', xtok[sl], src[t0 + s * 128:t0 + (s + 1) * 128, :], f'xtok{sl}',
                          writes=[('xtok', sl)])
                    for g in range(4):
                        b = VB.next()
                        P.group('pe', [TR(ps[:, b, q * 128:(q + 1) * 128],
                                          xtok[sl][:, (g * 4 + q) * 128:(g * 4 + q + 1) * 128], ident_f)
                                       for q in range(4)],
                                reads=[('xtok', sl), 'ident_f'], writes=[('ps', b)])
                        P.op('dve', CP(xT[:, g * 4:(g + 1) * 4, s * 128:(s + 1) * 128],
                                       ps[:, b, :].rearrange("p (a b) -> p a b", a=4)),
                             reads=[('ps', b)], writes=[('xT', g * 4 + q) for q in range(4)])
                if mode == 'x0':
                    P.dma('pool', fm(XT)[:, :, t0:t1], xT, 'stXT', reads=[('xT', kc) for kc in range(KC)],
                          writes=[('XT', i)])
            sb = SB.next()
            for kc in range(KC):
                sl = kc % 2
                P.op('act', ACTF(sq[sl], xT[:, kc, :], AF.Square), reads=[('xT', kc)], writes=[('sq', sl)])
                P.op('pe', MM(ps[:, sb, :], ones_b, sq[sl], kc == 0, kc == KC - 1),
                     reads=[('sq', sl), 'ones_b'], writes=[('ps', sb)])
            rstd_chain(sb, 1.0 / D, msb[0], rsb[0], ('ms', 0), ('rs', 0))
            for kc in range(KC):
                P.op('dve', STT(h[:, kc, :], xT[:, kc, :], col(gcol + kc), rsb[0], ALU.mult, ALU.mult),
                     reads=[('xT', kc), ('rs', 0), 'pv'], writes=[('h', kc)])
            hkeys = [('h', kc) for kc in range(KC)]

            pending = []

            def flush():
                for f in pending:
                    f()
                pending.clear()

            def qk_stage1(bank, dst, gofs):
                s2 = slot2.next()
                P.op('dve', CP(pfs[s2], ps[:, bank, :]), reads=[('ps', bank)], writes=[('pf', s2)])
                P.op('act', ACTF(sq[s2], pfs[s2], AF.Square), reads=[('pf', s2)], writes=[('sq', s2)])

                def stage2():
                    sb2 = SB.next()
                    P.op('pe', MM(ps[:, sb2, :], ones_b, sq[s2], True, True),
                         reads=[('sq', s2), 'ones_b'], writes=[('ps', sb2)])
                    P.op('act', ACTF(msb[s2], ps[:, sb2, :], AF.Sqrt, scale=1.0 / 128, bias=epsc),
                         reads=[('ps', sb2), 'epsc'], writes=[('ms', s2)])
                    P.op('dve', RCP(rsb[s2], msb[s2]), reads=[('ms', s2)], writes=[('rs', s2)])
                    a_, b_ = pfs[s2], rsb[s2]
                    if len(dst[0].shape) == 3:
                        a_ = a_.rearrange("p (a b) -> p a b", a=2)
                        b_ = b_.rearrange("p (a b) -> p a b", a=2)
                    P.op('dve', STT(dst[0], a_, col(gofs), b_, ALU.mult, ALU.mult),
                         reads=[('pf', s2), ('rs', s2), 'pv'], writes=[dst[1]])
                pending.append(stage2)

            for b in range(nblk):
                ws = wrot.next()
                P.dma('sp', wblk[ws], wv[:, :, b * 256:(b + 1) * 256], f'wA{ws}', reads=[wkey],
                      writes=[('wA', ws)])
                col0 = b * 256
                if mode == 'mem':
                    kind = 'km' if col0 < MW else 'vm'
                else:
                    kind = ('q' if col0 < 1024 else 'k' if col0 < 2048 else 'v' if col0 < 3072
                            else 'qm' if col0 < 3584 else 'u' if col0 < 4096 else 'g')
                if kind in ('v', 'vm'):
                    for s in range(4):
                        bank = VB.next()
                        P.group('pe', [MM(ps[:, bank, 0:256], h[:, kc, s * 128:(s + 1) * 128], wblk[ws][:, kc, :],
                                          kc == 0, kc == KC - 1) for kc in range(KC)],
                                reads=hkeys + [('wA', ws)], writes=[('ps', bank)])
                        flush()
                        if kind == 'v':
                            vc = col0 - 2048
                            dst, dkey = vstage[:, s, vc:vc + 256], ('vst', s)
                        else:
                            vc = col0 - MW
                            dst, dkey = Vm[:, s // 2, s % 2, vc:vc + 256], ('Vm', s)
                        eng = 'dve' if (s % 2 == 0) else 'act'
                        fn = CP(dst, ps[:, bank, 0:256]) if eng == 'dve' else ACTF(dst, ps[:, bank, 0:256], AF.Copy)
                        P.op(eng, fn, reads=[('ps', bank)], writes=[dkey])
                    continue
                for o in range(2):
                    oc = (col0 + o * 128) // 128
                    bank = PB.next()
                    P.group('pe', [MM(ps[:, bank, :], wblk[ws][:, kc, o * 128:(o + 1) * 128], h[:, kc, :],
                                      kc == 0, kc == KC - 1) for kc in range(KC)],
                            reads=hkeys + [('wA', ws)], writes=[('ps', bank)])
                    flush()
                    if kind == 'q':
                        qk_stage1(bank, (qstage[:, oc, :], ('qst', oc)), P_GQ)
                    elif kind == 'k':
                        qk_stage1(bank, (kstage[:, oc - 8, :], ('kst', oc - 8)), P_GK)
                    elif kind == 'qm':
                        qk_stage1(bank, (qmstage[:, oc - 24, :], ('qmst', oc - 24)), P_GQM)
                    elif kind == 'km':
                        qk_stage1(bank, (Km[:, :, oc, :], ('Km', oc)), P_GKM)
                    elif kind == 'u':
                        j = oc - 28
                        P.op('dve', CP(ustage[:, j, :], ps[:, bank, :]), reads=[('ps', bank)], writes=[('ust', j)])
                    elif kind == 'g':
                        j = oc - 32
                        s2 = thr.next()
                        P.op('act', ACTF(thb[s2], ps[:, bank, :], AF.Tanh, scale=0.5),
                             reads=[('ps', bank)], writes=[('th', s2)])
                        P.op('dve', STT(cstage[:, j, :], thb[s2], 1.0, ustage[:, j, :], ALU.add, ALU.mult),
                             reads=[('th', s2), ('ust', j)], writes=[('cst', j)])
            flush()
            if mode != 'mem':
                P.dma('pool', fm(QT)[:, :, t0:t1], qstage, 'stQ', reads=[('qst', j) for j in range(8)],
                      writes=[('QT', i)])
                P.dma('pool', fm(KT)[:, :, t0:t1], kstage, 'stK', reads=[('kst', j) for j in range(8)],
                      writes=[('KT', i)])
                P.dma('pool', fm(QMT)[:, :, t0:t1], qmstage, 'stQM', reads=[('qmst', j) for j in range(4)],
                      writes=[('QMT', i)])
                P.dma('pool', VV[t0:t1, :].rearrange("(s p) c -> p s c", p=128), vstage, 'stV',
                      reads=[('vst', s) for s in range(4)], writes=[('VV', i)])
                P.dma('pool', fm(CT)[:, :, t0:t1], cstage, 'stC', reads=[('cst', j) for j in range(4)],
                      writes=[('CT', i)])
            if mode == 'x0':
                issue_casts(-(-n_rest // NT))

    def sweepB(l):
        A.off = base_off
        P.barrier()
        qt = A.bf16(8, T)
        qmt = A.bf16(4, T)
        Kw = [A.bf16(1024), A.bf16(1024)]
        Vw = [A.bf16(8, 128), A.bf16(8, 128)]
        Gp = [A.bf16(NSLOT, 128), A.bf16(NSLOT, 128)]
        Pt = [A.bf16(6, 64), A.bf16(6, 64), A.bf16(6, 64)]
        Ptm = [A.bf16(T), A.bf16(T)]
        rden = [A.f32(T), A.f32(T)]
        cat = A.bf16(KC, T)
        cwin = A.bf16(4, T + 30)
        dg = [A.bf16(128) for _ in range(8)]
        cv = A.f32(4, T)
        sqf = [A.f32(T), A.f32(T)]
        msb = A.f32(T)
        rsb = A.f32(T)
        thb = [A.f32(T), A.f32(T)]
        xch = [A.f32(T), A.f32(T)]
        xmst = [A.f32(T), A.f32(T)]
        sq = [A.bf16(T), A.bf16(T)]
        h2 = A.bf16(KC, T)
        wblk = [A.bf16(KC, 256), A.bf16(KC, 256), A.bf16(KC, 256)]
        SC = Rot([0, 1, 6])
        OB = Rot([2, 3])
        DBk = Rot([4, 5])
        P2 = Rot([7])
        hrot = Rot([0, 1])
        prot = Rot([0, 1, 2])
        wrot = Rot([0, 1, 2])
        r2 = Rot([0, 1])
        dgr = Rot(range(8))
        wv = wb_out[l].rearrange("(c p) n -> p c n", p=128)
        KTv, QTv, QMTv, CTv = fm(KT), fm(QT), fm(QMT), fm(CT)
        gtv = gtb.rearrange("a b -> (a b)")[0:L * NPACK * 8 * NSLOT * GW * 2 * GW].rearrange(
            "(l k h s q c) -> l k h q s c", l=L, k=NPACK, h=8, s=NSLOT, q=GW)
        Xsrc = XT if l == 0 else X1T
        xkey = 'XT' if l == 0 else 'X1T'
        mid = NT // 2

        for i in range(NT):
            t0, t1 = i * T, (i + 1) * T
            tg = geo['tiles'][i]
            w0, w1 = tg['w0'], tg['w1']
            nwin = (w1 - w0) * GW
            ktiles = sorted(set(range((w0 * GW) // T, ((w1 * GW) - 1) // T + 1)))
            P.dma('sp', qt, QTv[:, :, t0:t1], 'ldq', reads=[('QT', i)], writes=['qt'])
            P.dma('sp', qmt, QMTv[:, :, t0:t1], 'ldqm', reads=[('QMT', i)], writes=['qmt'])
            lo, hi = t0 - 15, t1 + 15
            clo, chi = max(lo, 0), min(hi, LT)
            if clo > lo:
                P.op('pool', MSET(cwin[:, :, 0:15], 0.0), writes=['cwin'])
            if chi < hi:
                P.op('pool', MSET(cwin[:, :, T + 15:T + 30], 0.0), writes=['cwin'])
            P.dma('sp', cwin[:, :, clo - lo:chi - lo], CTv[:, :, clo:chi], 'ldc',
                  reads=[('CT', j) for j in range(max(i - 1, 0), min(i + 2, NT))], writes=['cwin'])
            if i == mid:
                P.op('dve', TS(cwin[:, :, 0:15], cwin[:, :, 0:15], flag[:, 0:1], ALU.mult),
                     reads=['cwin', 'flag'], writes=['cwin'])
            if i == mid - 1:
                P.op('dve', TS(cwin[:, :, T + 15:T + 30], cwin[:, :, T + 15:T + 30], flag[:, 0:1], ALU.mult),
                     reads=['cwin', 'flag'], writes=['cwin'])

            def na_head(hd):
                    hs = hrot.next()
                    P.dma('sp', Kw[hs][:, 0:nwin], KTv[:, hd, w0 * GW:w1 * GW], f'ldk{hs}',
                          reads=[('KT', j) for j in ktiles], writes=[('Kw', hs)])
                    P.dma('sp', Vw[hs][:, 0:(w1 - w0) // 2, :],
                          VV[w0 * GW:w1 * GW, hd * 128:(hd + 1) * 128].rearrange("(c p) d -> p c d", p=128),
                          f'ldv{hs}', reads=[('VV', j) for j in ktiles], writes=[('Vw', hs)])
                    P.dma('sp', Gp[hs][0:GW, :, :], gtv[l, tg['pack'], hd], f'ldg{hs}', reads=['gtb'],
                          writes=[('Gp', hs)])
                    ob, db = OB.next(), DBk.next()
                    for rl in range(8):
                        r = 8 * i + rl
                        start, nch, _ = geo['rows'][r]
                        sc = SC.next()
                        pp = prot.next()
                        mms = []
                        for ci in range(nch):
                            ko = (start + 2 * ci - w0) * GW
                            o_ = ps[:, sc, ci * 64:(ci + 1) * 64]
                            mms.append(MM(o_, Kw[hs][:, ko:ko + 128], qt[:, hd, rl * 64:(rl + 1) * 64], True, False))
                            mms.append(MM(o_, Gp[hs][0:GW, tg['slotmap'][(rl, ci)], :], ident_b[0:GW, 0:GW],
                                          False, True))
                        P.group('pe', mms, reads=[('Kw', hs), ('Gp', hs), 'qt', 'ident_b'], writes=[('ps', sc)])
                        P.op('act', ACTF(Pt[pp][:, 0:nch, :],
                                         ps[:, sc, 0:nch * 64].rearrange("p (a b) -> p a b", a=nch), AF.Exp),
                             reads=[('ps', sc)], writes=[('Pt', pp)])
                        mms = []
                        for ci in range(nch):
                            vi = (start + 2 * ci - w0) // 2
                            mms.append(MM(ps[:, ob, rl * 64:(rl + 1) * 64], Vw[hs][:, vi, :], Pt[pp][:, ci, :],
                                          ci == 0, ci == nch - 1))
                        for ci in range(nch):
                            mms.append(MM(ps[:, db, rl * 64:(rl + 1) * 64], ones_b, Pt[pp][:, ci, :],
                                          ci == 0, ci == nch - 1))
                        P.group('pe', mms, reads=[('Vw', hs), ('Pt', pp), 'ones_b'],
                                writes=[('ps', ob), ('ps', db)])
                    rr = r2.next()
                    P.op('dve', RCP(rden[rr], ps[:, db, :]), reads=[('ps', db)], writes=[('rden', rr)])
                    P.op('dve', TT(cat[:, hd, :], ps[:, ob, :], rden[rr], ALU.mult),
                         reads=[('ps', ob), ('rden', rr)], writes=[('cat', hd)])


            def conv_part():
                for j in range(4):
                    cbk = P2.next()
                    for k in range(31):
                        ds = dgr.next()
                        P.op('dve', TS(dg[ds], ident_b, col(P_CW + j * 31 + k), ALU.mult),
                             reads=['ident_b', 'pv'], writes=[('dg', ds)])
                        P.op('pe', MM(ps[:, cbk, :], dg[ds], cwin[:, j, k:k + T], k == 0, k == 30),
                             reads=[('dg', ds), 'cwin'], writes=[('ps', cbk)])
                    P.op('act', ACTF(cv[:, j, :], ps[:, cbk, :], AF.Identity, bias=col(P_CB + j)),
                         reads=[('ps', cbk), 'pv'], writes=[('cv', j)])

            def ln_part1():
                mb = P2.next()
                P.group('pe', [MM(ps[:, mb, :], ones_f, cv[:, j, :], j == 0, j == 3) for j in range(4)],
                        reads=[('cv', j) for j in range(4)] + ['ones_f'], writes=[('ps', mb)])
                for j in range(4):
                    P.op('dve', STT(cv[:, j, :], ps[:, mb, :], -1.0 / CC, cv[:, j, :], ALU.mult, ALU.add),
                         reads=[('ps', mb), ('cv', j)], writes=[('cv', j)])

            def ln_part2():
                vb = P2.next()
                for j in range(4):
                    s2 = j % 2
                    P.op('act', ACTF(sqf[s2], cv[:, j, :], AF.Square), reads=[('cv', j)], writes=[('sqf', s2)])
                    P.op('pe', MM(ps[:, vb, :], ones_f, sqf[s2], j == 0, j == 3),
                         reads=[('sqf', s2), 'ones_f'], writes=[('ps', vb)])
                P.op('act', ACTF(msb, ps[:, vb, :], AF.Sqrt, scale=1.0 / CC, bias=epsc),
                     reads=[('ps', vb), 'epsc'], writes=['msB'])
                P.op('dve', RCP(rsb, msb), reads=['msB'], writes=['rsB'])
                for j in range(4):
                    P.op('dve', TT(cv[:, j, :], cv[:, j, :], rsb, ALU.mult), reads=[('cv', j), 'rsB'],
                         writes=[('cv', j)])
                    P.op('dve', TS(cv[:, j, :], cv[:, j, :], col(P_LNG + j), ALU.mult, col(P_LNB + j), ALU.add),
                         reads=[('cv', j), 'pv'], writes=[('cv', j)])
                    s2 = j % 2
                    P.op('act', ACTF(thb[s2], cv[:, j, :], AF.Tanh), reads=[('cv', j)], writes=[('thB', s2)])
                    P.op('dve', STT(cat[:, 12 + j, :], thb[s2], 1.0, cv[:, j, :], ALU.add, ALU.mult),
                         reads=[('thB', s2), ('cv', j)], writes=[('cat', 12 + j)])

            conv_part()
            na_head(0)
            na_head(1)
            ln_part1()
            na_head(2)
            na_head(3)
            ln_part2()
            for hd in range(4, 8):
                na_head(hd)

            mi = 0 if i < mid else 1
            for hm in range(4):
                ob, db = OB.next(), DBk.next()
                scs = []
                for ch in range(2):
                    sc = SC.next()
                    scs.append(sc)
                    P.group('pe', [MM(ps[:, sc, :], Km[:, mi, hm, ch * 128:(ch + 1) * 128], qmt[:, hm, :], True, True)],
                            reads=[('Km', hm), 'qmt'], writes=[('ps', sc)])
                    P.op('act', ACTF(Ptm[ch], ps[:, sc, :], AF.Exp), reads=[('ps', sc)], writes=[('Ptm', ch)])
                mms = [MM(ps[:, ob, :], Vm[:, mi, ch, hm * 128:(hm + 1) * 128], Ptm[ch], ch == 0, ch == 1)
                       for ch in range(2)]
                mms += [MM(ps[:, db, :], ones_b, Ptm[ch], ch == 0, ch == 1) for ch in range(2)]
                P.group('pe', mms, reads=[('Vm', s) for s in range(4)] + [('Ptm', 0), ('Ptm', 1), 'ones_b'],
                        writes=[('ps', ob), ('ps', db)])
                rr = r2.next()
                P.op('dve', RCP(rden[rr], ps[:, db, :]), reads=[('ps', db)], writes=[('rden', rr)])
                P.op('dve', TT(cat[:, 8 + hm, :], ps[:, ob, :], rden[rr], ALU.mult),
                     reads=[('ps', ob), ('rden', rr)], writes=[('cat', 8 + hm)])

            catkeys = [('cat', k) for k in range(KC)]
            sb = P2.next()
            for b in range(8):
                ws = wrot.next()
                P.dma('sp', wblk[ws], wv[:, :, b * 256:(b + 1) * 256], f'wB{ws}', reads=[('wb_out', l)],
                      writes=[('wB', ws)])
                for o in range(2):
                    oc = b * 2 + o
                    xs = oc % 2
                    P.dma('sp', xch[xs], fm(Xsrc)[:, oc, t0:t1], f'ldx{xs}', reads=[('XT', i) if l == 0 else ('X1T', i, oc)], writes=[('xch', xs)])
                    bank = SC.next()
                    P.group('pe', [MM(ps[:, bank, :], wblk[ws][:, kc, o * 128:(o + 1) * 128], cat[:, kc, :],
                                      kc == 0, kc == KC - 1) for kc in range(KC)],
                            reads=catkeys + [('wB', ws)], writes=[('ps', bank)])
                    P.op('dve', TT(xmst[xs], ps[:, bank, :], xch[xs], ALU.add),
                         reads=[('ps', bank), ('xch', xs)], writes=[('xmst', xs)])
                    P.dma('pool', fm(XM)[:, oc, t0:t1], xmst[xs], f'stxm{xs}', reads=[('xmst', xs)], writes=[('XM', i, oc)])
                    P.op('act', ACTF(sq[xs], xmst[xs], AF.Square), reads=[('xmst', xs)], writes=[('sqB', xs)])
                    P.op('pe', MM(ps[:, sb, :], ones_b, sq[xs], oc == 0, oc == KC - 1),
                         reads=[('sqB', xs), 'ones_b'], writes=[('ps', sb)])
                    P.op('pool', TS(h2[:, oc, :], xmst[xs], col(P_G2 + oc), ALU.mult, 0.0, ALU.add),
                         reads=[('xmst', xs), 'pv'], writes=[('h2', oc)])
            P.op('act', ACTF(msb, ps[:, sb, :], AF.Sqrt, scale=1.0 / D, bias=epsc),
                 reads=[('ps', sb), 'epsc'], writes=['msB'])
            P.op('dve', RCP(rsb, msb), reads=['msB'], writes=['rsB'])
            for oc in range(KC):
                P.op('dve', TT(h2[:, oc, :], h2[:, oc, :], rsb, ALU.mult), reads=[('h2', oc), 'rsB'],
                     writes=[('h2', oc)])
            h2keys = [('h2', k) for k in range(KC)]
            P.op('dve', CP(h2halo[:, :, 2 * i:2 * i + 1], h2[:, :, 0:1]), reads=h2keys, writes=[('h2halo', 2 * i)])
            P.op('dve', CP(h2halo[:, :, 2 * i + 1:2 * i + 2], h2[:, :, T - 1:T]), reads=h2keys,
                 writes=[('h2halo', 2 * i + 1)])
            P.dma('pool', fm(H2T)[:, :, t0:t1], h2, 'sth2', reads=h2keys, writes=[('H2T', i)])

    def sweepC(l):
        A.off = base_off
        P.barrier()
        h2 = A.bf16(KC, T)
        hid = A.bf16(HC, T)
        wu = [(A.bf16(KC, 256), A.bf16(KC, 256)) for _ in range(2)]
        wd = [A.bf16(HC, 128) for _ in range(3)]
        ag = [A.f32(T), A.f32(T)]
        av = [A.f32(T), A.f32(T)]
        th = [A.f32(T), A.f32(T)]
        xmc = [A.f32(T), A.f32(T)]
        xo = [A.f32(T), A.f32(T)]
        uph = A.bf16(88, 2 * NT)
        edge = A.f32(88, 2)
        GBk = Rot([0, 1])
        VBk = Rot([2, 3])
        OP = Rot([4, 5])
        HB = Rot([6, 7])
        wurot = Rot([0, 1])
        wdrot = Rot([0, 1, 2])
        r2 = Rot([0, 1])
        wuv = wb_up[l].rearrange("(c p) n -> p c n", p=128)
        wdv = wb_down[l].rearrange("(c p) n -> p c n", p=128)
        mid = NT // 2
        Xdst = X1T
        hkeys = [('h2c', k) for k in range(KC)]
        halokeys = [('h2halo', k) for k in range(2 * NT)]
        fwc = lambda c, k: col(P_FW + c * 3 + k)

        for b in range(2 * DFF // 256):
            ws = wurot.next()
            P.dma('sp', wu[ws][0], wuv[:, :, b * 256:(b + 1) * 256], f'wu{ws}', reads=[('wb_up', l)],
                  writes=[('wu', ws)])
            for o in range(2):
                c = b * 2 + o
                hb = HB.next()
                P.group('pe', [MM(ps[:, hb, 0:2 * NT], wu[ws][0][:, kc, o * 128:(o + 1) * 128], h2halo[:, kc, :],
                                  kc == 0, kc == KC - 1) for kc in range(KC)],
                        reads=halokeys + [('wu', ws)], writes=[('ps', hb)])
                P.op('act', ACTF(uph[:, c, :], ps[:, hb, 0:2 * NT], AF.Copy), reads=[('ps', hb)], writes=['uph'])
        P.op('dve', TS(uph[:, :, 2 * mid - 1:2 * mid + 1], uph[:, :, 2 * mid - 1:2 * mid + 1], flag[:, 0:1], ALU.mult),
             reads=['uph', 'flag'], writes=['uph'])

        for i in range(NT):
            t0, t1 = i * T, (i + 1) * T
            P.dma('sp', h2, fm(H2T)[:, :, t0:t1], 'ldh2', reads=[('H2T', i)], writes=hkeys)
            fw3 = pv[:, P_FW:P_FW + 264].rearrange("p (c k) -> p c k", k=3)
            if i > 0:
                P.op('dve', TT(edge[:, :, 0:1], uph[:, :, 2 * i - 1:2 * i], fw3[:, :, 0:1], ALU.mult),
                     reads=['uph', 'pv'], writes=['edge'])
            else:
                P.op('dve', MSET(edge[:, :, 0:1], 0.0), writes=['edge'])
            if i < NT - 1:
                P.op('dve', TT(edge[:, :, 1:2], uph[:, :, 2 * i + 2:2 * i + 3], fw3[:, :, 2:3], ALU.mult),
                     reads=['uph', 'pv'], writes=['edge'])
            else:
                P.op('dve', MSET(edge[:, :, 1:2], 0.0), writes=['edge'])

            for jb in range(HC // 2):
                ws = wurot.next()
                P.dma('sp', wu[ws][0], wuv[:, :, jb * 256:(jb + 1) * 256], f'wu{ws}', reads=[('wb_up', l)],
                      writes=[('wu', ws)])
                P.dma('sp', wu[ws][1], wuv[:, :, DFF + jb * 256:DFF + (jb + 1) * 256], f'wu{ws}',
                      reads=[('wb_up', l)], writes=[('wu', ws)])
                for o in range(2):
                    j = jb * 2 + o
                    gb, vb = GBk.next(), VBk.next()
                    P.group('pe', [MM(ps[:, gb, :], wu[ws][0][:, kc, o * 128:(o + 1) * 128], h2[:, kc, :],
                                      kc == 0, kc == KC - 1) for kc in range(KC)],
                            reads=hkeys + [('wu', ws)], writes=[('ps', gb)])
                    P.group('pe', [MM(ps[:, vb, :], wu[ws][1][:, kc, o * 128:(o + 1) * 128], h2[:, kc, :],
                                      kc == 0, kc == KC - 1) for kc in range(KC)],
                            reads=hkeys + [('wu', ws)], writes=[('ps', vb)])
                    s2 = r2.next()
                    for (bank, dst, c, key) in ((gb, ag[s2], j, 'ag'), (vb, av[s2], HC + j, 'av')):
                        P.op('act', ACTF(dst, ps[:, bank, :], AF.Identity, scale=fwc(c, 1), bias=col(P_FB + c)),
                             reads=[('ps', bank), 'pv'], writes=[(key, s2)])
                        P.op('dve', STT(dst[:, 1:T], ps[:, bank, 0:T - 1], fwc(c, 0), dst[:, 1:T], ALU.mult, ALU.add),
                             reads=[('ps', bank), 'pv', (key, s2)], writes=[(key, s2)])
                        P.op('dve', STT(dst[:, 0:T - 1], ps[:, bank, 1:T], fwc(c, 2), dst[:, 0:T - 1], ALU.mult, ALU.add),
                             reads=[('ps', bank), 'pv', (key, s2)], writes=[(key, s2)])
                        P.op('pool', TT(dst[:, 0:1], dst[:, 0:1], edge[:, c, 0:1], ALU.add),
                             reads=[(key, s2), 'edge'], writes=[(key, s2)])
                        P.op('pool', TT(dst[:, T - 1:T], dst[:, T - 1:T], edge[:, c, 1:2], ALU.add),
                             reads=[(key, s2), 'edge'], writes=[(key, s2)])
                    P.op('act', ACTF(th[s2], ag[s2], AF.Tanh, scale=0.5), reads=[('ag', s2)], writes=[('thC', s2)])
                    P.op('dve', STT(th[s2], th[s2], 1.0, ag[s2], ALU.add, ALU.mult),
                         reads=[('thC', s2), ('ag', s2)], writes=[('thC', s2)])
                    P.op('pool', TT(hid[:, j, :], th[s2], av[s2], ALU.mult),
                         reads=[('thC', s2), ('av', s2)], writes=[('hid', j)])
            hidkeys = [('hid', j) for j in range(HC)]
            for oc in range(KC):
                ws = wdrot.next()
                P.dma('sp', wd[ws], wdv[:, :, oc * 128:(oc + 1) * 128], f'wd{ws}', reads=[('wb_down', l)],
                      writes=[('wd', ws)])
                xs = oc % 2
                P.dma('sp', xmc[xs], fm(XM)[:, oc, t0:t1], f'ldxm{xs}', reads=[('XM', i, oc)], writes=[('xmc', xs)])
                bank = OP.next()
                P.group('pe', [MM(ps[:, bank, :], wd[ws][:, k, :], hid[:, k, :], k == 0, k == HC - 1)
                               for k in range(HC)],
                        reads=hidkeys + [('wd', ws)], writes=[('ps', bank)])
                P.op('dve', TT(xo[xs], ps[:, bank, :], xmc[xs], ALU.add), reads=[('ps', bank), ('xmc', xs)],
                     writes=[('xo', xs)])
                P.dma('pool', fm(Xdst)[:, oc, t0:t1], xo[xs], f'stxo{xs}', reads=[('xo', xs)], writes=[('X1T', i, oc)])

    def sweepD():
        A.off = base_off
        P.barrier()
        xT = [A.f32(KC, T), A.f32(KC, T)]
        yt = [A.f32(D), A.f32(D)]
        TB = Rot(range(8))
        for i in range(NT):
            t0, t1 = i * T, (i + 1) * T
            xs = i % 2
            P.dma('sp', xT[xs], fm(X1T)[:, :, t0:t1], f'ldD{xs}', reads=[('X1T', i, oc_) for oc_ in range(KC)], writes=[('xTD', xs)])
            for s in range(4):
                ys = s % 2
                for g in range(4):
                    b = TB.next()
                    P.group('pe', [TR(ps[:, b, q * 128:(q + 1) * 128], xT[xs][:, g * 4 + q, s * 128:(s + 1) * 128], ident_f)
                                   for q in range(4)], reads=[('xTD', xs), 'ident_f'], writes=[('ps', b)])
                    eng = 'dve' if g % 2 == 0 else 'act'
                    dst = yt[ys][:, g * 512:(g + 1) * 512]
                    fn = CP(dst, ps[:, b, :]) if eng == 'dve' else ACTF(dst, ps[:, b, :], AF.Copy)
                    P.op(eng, fn, reads=[('ps', b)], writes=[('yt', ys)])
                P.dma('sp', yout[t0 + s * 128:t0 + (s + 1) * 128, :], yt[ys], f'stD{ys}', reads=[('yt', ys)],
                      writes=[('yout', i, s)])

    for l in range(layers):
        P.dma('sp', pv, pvec[l], 'pv', writes=['pv'])
        sc = 128.0 ** -0.5
        for (off, n, c) in ((P_GQ, 1, sc), (P_GQM, 1, sc), (P_CW, 124, 0.5), (P_LNG, 8, 0.5),
                            (P_FW + HC * 3, HC * 3, 0.5), (P_FB + HC, HC, 0.5)):
            P.op('dve', TS(col(off, n), col(off, n), c, ALU.mult), reads=['pv'], writes=['pv'])
        if stop_after == ('P', l):
            break
        sweepA(l, 'mem')
        if l == 0:
            issue_casts(n_second)
        if stop_after == ('M', l):
            break
        sweepA(l, 'x0' if l == 0 else 'x1')
        if stop_after == ('A', l):
            break
        issue_casts(len(cast_jobs))
        sweepB(l)
        if stop_after == ('B', l):
            break
        sweepC(l)
    if stop_after is None:
        sweepD()
        P.wait_all('sp', [('yout', i, s) for i in range(NT) for s in range(4)])
    else:
        P.wait_all('sp', list(P.lastw.keys()))

    semnames = ENGS + sorted(P.dmasems)
    ctxs = [nc.semaphore(f"s_{n}") for n in semnames]
    sems = {n: c.__enter__() for n, c in zip(semnames, ctxs)}
    with nc.Block() as block:
        for e in ENGS:
            ops = P.ops[e]

            def body(eng, ops=ops, e=e):
                for o in ops:
                    if o[0] == 'wait':
                        eng.wait_ge(sems[o[1]], o[2])
                    elif o[0] == 'op':
                        ins = o[1](eng)
                        if o[2]:
                            ins.then_inc(sems[e], 1)
                    else:
                        eng.dma_start(out=o[1], in_=o[2], **o[4]).then_inc(sems[o[3]], 16)
            getattr(block, ENGATTR[e])(body)
    for c in reversed(ctxs):
        c.__exit__(None, None, None)
    ctx_psum.__exit__(None, None, None)
    ctx_arena.__exit__(None, None, None)
    stats = {e: len(P.ops[e]) for e in ENGS}
    stats['nops'] = P.nops
    return nc, geo, stats


def core_inputs(xseq, mem2, ty, shared, NT, geo, inp):
    m = dict(shared)
    m['xin'] = np.ascontiguousarray(xseq, dtype=np.float32)
    m['memin'] = np.ascontiguousarray(mem2.reshape(2 * NMEM, D), dtype=np.float32)
    g = build_gtab(inp['na_rpb'], ty, NT, geo)
    flat = g.reshape(-1)
    ngr = -(-flat.size // (2048 * 128)) * 128
    buf = np.zeros((ngr * 2048,), np.float32)
    buf[:flat.size] = flat
    m['gtab'] = buf.reshape(ngr, 2048)
    m['flagin'] = np.full((128, 1), 1.0 if ty == 'A' else 0.0, np.float32)
    return m


def shared_inputs(inp):
    return {
        'w_in': np.ascontiguousarray(inp['w_in'], dtype=np.float32),
        'w_kv': np.ascontiguousarray(inp['w_mem_kv'], dtype=np.float32),
        'w_out': np.ascontiguousarray(inp['w_out'], dtype=np.float32),
        'w_up': np.ascontiguousarray(inp['w_up'], dtype=np.float32),
        'w_down': np.ascontiguousarray(inp['w_down'], dtype=np.float32),
        'pvec': build_pvec(inp),
        'identin': np.eye(128, dtype=np.float32),
    }


_CACHE = {}


def kernel(**inputs):
    inp = {k: np.asarray(v) for k, v in inputs.items()}
    NT = 16
    if NT not in _CACHE:
        _CACHE[NT] = build_program(NT)
    nc, geo, _ = _CACHE[NT]
    shared = shared_inputs(inp)
    xp, xs, mp, ms = inp['x_prompt'], inp['x_sample'], inp['mem_prompt'], inp['mem_sample']
    maps = []
    for b in range(2):
        maps.append(core_inputs(xs[b], np.stack([ms[b], ms[b]]), 'A', shared, NT, geo, inp))
    for k in range(4):
        maps.append(core_inputs(np.concatenate([xp[2 * k], xp[2 * k + 1]], axis=0),
                                np.stack([mp[2 * k], mp[2 * k + 1]]), 'B', shared, NT, geo, inp))
    for k in range(2):
        maps.append(core_inputs(np.zeros((NT * T, D), np.float32), np.zeros((2, NMEM, D), np.float32), 'B',
                                shared, NT, geo, inp))
    res = run_bass_kernel_spmd(nc, maps, core_ids=list(range(8)))
    outs = [np.asarray(r['yout']) for r in res.results]
    y_sample = np.stack([outs[0], outs[1]]).astype(np.float32)
    y_prompt = np.stack([outs[2 + k // 2][(k % 2) * 4096:(k % 2 + 1) * 4096] for k in range(8)]).astype(np.float32)
    return (y_prompt, y_sample)
```

```python
import math
from collections import defaultdict
import numpy as np
import concourse.bass as bass
import concourse.mybir as mybir
from concourse.bass_utils import run_bass_kernel_spmd

F32 = mybir.dt.float32
BF16 = mybir.dt.bfloat16
AF = mybir.ActivationFunctionType
ALU = mybir.AluOpType

D = 2048
KC = 16
T = 512
NAW = 1024
MW = 512
CC = 512
DFF = 5632
HC = 44
INW = 4608
NMEM = 256
EPS = 1e-6
GW = 64
NEG = -30000.0
L = 2
NP = 540
P_G1, P_GM, P_G2, P_GQ, P_GK, P_GQM, P_GKM, P_CB, P_LNG, P_LNB, P_CW, P_FW, P_FB = \
    0, 16, 32, 48, 49, 50, 51, 52, 56, 60, 64, 188, 452
ARENA_WORDS = 43520


def _tw(r, ty, R):
    if ty == 'A':
        return int(np.clip(r - 4, 0, R - 8))
    half = R // 2
    base = 0 if r < half else half
    return base + int(np.clip(r - base - 4, 0, half - 8))


def geometry(NT):
    R = 8 * NT
    rows = []
    for r in range(R):
        a, b = _tw(r, 'A', R), _tw(r, 'B', R)
        lo, hi = min(a, b), max(a, b) + 8
        start = lo - (lo % 2)
        nch = (hi - start + 1) // 2
        assert start + 2 * nch <= R
        rows.append((start, nch, a == b))
    tiles = []
    packs = {}
    for i in range(NT):
        keys = []
        slotmap = {}
        for rl in range(8):
            r = 8 * i + rl
            start, nch, same = rows[r]
            rs = _tw(r, 'A', R)
            for ci in range(nch):
                kr = start + 2 * ci
                if same:
                    key = ('g', kr - r + 7, rs <= kr < rs + 8, rs <= kr + 1 < rs + 8)
                else:
                    key = ('s', rl, ci)
                if key not in keys:
                    keys.append(key)
                slotmap[(rl, ci)] = keys.index(key)
        sig = tuple(keys)
        if sig not in packs:
            packs[sig] = (len(packs), i)
        w0 = min(rows[8 * i + rl][0] for rl in range(8))
        w1 = max(rows[8 * i + rl][0] + 2 * rows[8 * i + rl][1] for rl in range(8))
        assert (w1 - w0) <= 16 and w0 % 2 == 0
        tiles.append(dict(keys=keys, slotmap=slotmap, pack=packs[sig][0], w0=w0, w1=w1))
    nslot = max(len(t['keys']) for t in tiles)
    packlist = sorted(packs.values())
    return dict(R=R, rows=rows, tiles=tiles, nslot=nslot, npack=len(packlist),
                pack_rep=[p[1] for p in packlist])


def build_gtab(rpb, ty, NT, geo):
    R = geo['R']
    qc = np.arange(GW)[:, None]
    kc = np.arange(GW)[None, :]
    cs = np.clip(qc - 8, 0, GW - 16)
    valid = (kc >= cs) & (kc < cs + 16)
    dc = np.clip(kc - qc + 15, 0, 30)
    Tf = np.where(valid[None, None, None], rpb[:, :, :, dc], np.float32(NEG)).astype(np.float32)
    negt = np.full((rpb.shape[0], 8, GW, GW), NEG, np.float32)
    out = np.full((rpb.shape[0], geo['npack'], 8, geo['nslot'], GW, 2 * GW), NEG, np.float32)
    for p, i in enumerate(geo['pack_rep']):
        t = geo['tiles'][i]
        for k, key in enumerate(t['keys']):
            halves = []
            if key[0] == 'g':
                _, dA, vA, vB = key
                for d, v in ((dA, vA), (dA + 1, vB)):
                    halves.append(Tf[:, :, d] if (v and 0 <= d <= 14) else negt)
            else:
                _, rl, ci = key
                r = 8 * i + rl
                start = geo['rows'][r][0]
                rs = _tw(r, ty, R)
                for hf in range(2):
                    kr = start + 2 * ci + hf
                    d = kr - r + 7
                    v = (rs <= kr < rs + 8)
                    halves.append(Tf[:, :, d] if (v and 0 <= d <= 14) else negt)
            out[:, p, :, k, :, 0:GW] = halves[0]
            out[:, p, :, k, :, GW:] = halves[1]
    return out


def build_pvec(inp):
    pv = np.zeros((L, 128, NP), np.float32)
    fm = lambda a, n: a.reshape(L, n, 128).transpose(0, 2, 1)
    pv[:, :, P_G1:P_G1 + 16] = fm(inp['norm_mix_g'], 16)
    pv[:, :, P_GM:P_GM + 16] = fm(inp['mem_norm_g'], 16)
    pv[:, :, P_G2:P_G2 + 16] = fm(inp['norm_ffn_g'], 16)
    pv[:, :, P_GQ] = inp['na_q_norm_g']
    pv[:, :, P_GK] = inp['na_k_norm_g']
    pv[:, :, P_GQM] = inp['mem_q_norm_g']
    pv[:, :, P_GKM] = inp['mem_k_norm_g']
    pv[:, :, P_CB:P_CB + 4] = fm(inp['conv_dw_b'], 4)
    pv[:, :, P_LNG:P_LNG + 4] = fm(inp['conv_ln_g'], 4)
    pv[:, :, P_LNB:P_LNB + 4] = fm(inp['conv_ln_b'], 4)
    cw = inp['conv_dw_w'].reshape(L, 31, 4, 128).transpose(0, 3, 2, 1)
    pv[:, :, P_CW:P_CW + 124] = cw.reshape(L, 128, 124)
    fw = inp['ffn_dw_w'].reshape(L, 3, 88, 128).transpose(0, 3, 2, 1)
    pv[:, :, P_FW:P_FW + 264] = fw.reshape(L, 128, 264)
    pv[:, :, P_FB:P_FB + 88] = fm(inp['ffn_dw_b'], 88)
    return pv


ENGS = ['pe', 'act', 'dve', 'pool', 'sp']
ENGATTR = {'pe': 'tensor', 'act': 'scalar', 'dve': 'vector', 'pool': 'gpsimd', 'sp': 'sync'}


class Prog:
    def __init__(self):
        self.cnt = defaultdict(int)
        self.ops = {e: [] for e in ENGS}
        self.seen = {e: defaultdict(int) for e in ENGS}
        self.lastw = {}
        self.readers = defaultdict(dict)
        self.dmasems = set()
        self.nobarrier = set()
        import os
        self.limit = int(os.environ.get('KLIMIT', '0')) or None
        self.nops = 0

    def _skip(self):
        self.nops += 1
        return self.limit is not None and self.nops > self.limit

    def _deps(self, reads, writes):
        need = {}

        def add(ev):
            s, v = ev
            if need.get(s, 0) < v:
                need[s] = v
        for k in reads:
            if k in self.lastw:
                add(self.lastw[k])
        for k in writes:
            if k in self.lastw:
                add(self.lastw[k])
            for s, v in self.readers[k].items():
                add((s, v))
        return need

    def _commit(self, ev, reads, writes):
        s, v = ev
        for k in reads:
            rd = self.readers[k]
            if rd.get(s, 0) < v:
                rd[s] = v
        for k in writes:
            self.lastw[k] = ev
            self.readers[k] = {}

    def _waits(self, eng, need):
        for s, v in need.items():
            if eng == 'pe' and s == 'pe':
                continue
            if self.seen[eng][s] >= v:
                continue
            self.seen[eng][s] = v
            self.ops[eng].append(('wait', s, v))

    def op(self, eng, fn, reads=(), writes=()):
        if self._skip():
            return None
        self._waits(eng, self._deps(reads, writes))
        self.cnt[eng] += 1
        ev = (eng, self.cnt[eng])
        self.ops[eng].append(('op', fn, True))
        self._commit(ev, reads, writes)
        return ev

    def group(self, eng, fns, reads=(), writes=()):
        if self._skip():
            return None
        self._waits(eng, self._deps(reads, writes))
        for f in fns[:-1]:
            self.ops[eng].append(('op', f, False))
        self.cnt[eng] += 1
        ev = (eng, self.cnt[eng])
        self.ops[eng].append(('op', fns[-1], True))
        self._commit(ev, reads, writes)
        return ev

    def dma(self, q, out, in_, sem, reads=(), writes=(), **kw):
        if self._skip():
            return None
        self._waits(q, self._deps(reads, writes))
        self.dmasems.add(sem)
        self.cnt[sem] += 16
        ev = (sem, self.cnt[sem])
        self.ops[q].append(('dma', out, in_, sem, kw))
        self._commit(ev, reads, writes)
        return ev

    def barrier(self):
        if self.limit is not None and self.nops > self.limit:
            return
        for e in ENGS:
            self._waits(e, {s_: c for s_, c in self.cnt.items() if c > 0 and s_ not in self.nobarrier})

    def wait_all(self, eng, keys):
        self._waits(eng, self._deps(keys, ()))


class Rot:
    def __init__(self, items):
        self.items = list(items)
        self.i = 0

    def next(self):
        v = self.items[self.i % len(self.items)]
        self.i += 1
        return v


class Arena:
    def __init__(self, ap, nwords):
        self.ap = ap
        self.n = nwords
        self.off = 0

    def _shape(self, v, shape):
        if len(shape) == 1:
            return v
        if len(shape) == 2:
            return v.rearrange("p (a b) -> p a b", a=shape[0])
        if len(shape) == 3:
            return v.rearrange("p (a b c) -> p a b c", a=shape[0], b=shape[1])
        raise ValueError

    def f32(self, *shape):
        n = int(np.prod(shape))
        v = self.ap[:, self.off:self.off + n]
        self.off += n
        assert self.off <= self.n, (self.off, self.n)
        return self._shape(v, shape)

    def bf16(self, *shape):
        n = int(np.prod(shape))
        w = (n + 1) // 2
        w += w % 2
        v = self.ap[:, self.off:self.off + w].bitcast(BF16)[:, 0:n]
        self.off += w
        assert self.off <= self.n, (self.off, self.n)
        return self._shape(v, shape)


def MM(out, lhsT, rhs, start, stop):
    return lambda e: e.matmul(out, lhsT, rhs, start=start, stop=stop)


def TR(out, in_, ident):
    return lambda e: e.transpose(out, in_, ident)


def ACTF(out, in_, func, scale=None, bias=None):
    kw = {}
    if scale is not None:
        kw['scale'] = scale
    if bias is not None:
        kw['bias'] = bias
    return lambda e: e.activation(out=out, in_=in_, func=func, **kw)


def TT(out, a, b, op):
    return lambda e: e.tensor_tensor(out=out, in0=a, in1=b, op=op)


def TS(out, a, s1, op0, s2=None, op1=None):
    if op1 is None:
        return lambda e: e.tensor_scalar(out=out, in0=a, scalar1=s1, scalar2=None, op0=op0)
    return lambda e: e.tensor_scalar(out=out, in0=a, scalar1=s1, scalar2=s2, op0=op0, op1=op1)


def STT(out, a, s, b, op0, op1):
    return lambda e: e.scalar_tensor_tensor(out=out, in0=a, scalar=s, in1=b, op0=op0, op1=op1)


def CP(out, in_):
    return lambda e: e.tensor_copy(out=out, in_=in_)


def RCP(out, in_):
    return lambda e: e.reciprocal(out=out, in_=in_)


def MSET(ap, c):
    return lambda e: e.memset(ap, c)


def build_program(NT, layers=L, stop_after=None, debug=False):
    geo = geometry(NT)
    LT = NT * T
    NSLOT, NPACK = geo['nslot'], geo['npack']
    nc = bass.Bass("TRN2", target_bir_lowering=False)

    def dt_(name, shape, dtype, kind):
        return nc.dram_tensor(name, list(shape), dtype, kind=kind).ap()

    xin = dt_("xin", [LT, D], F32, "ExternalInput")
    memin = dt_("memin", [2 * NMEM, D], F32, "ExternalInput")
    w_in = dt_("w_in", [L, D, INW], F32, "ExternalInput")
    w_kv = dt_("w_kv", [L, D, 2 * MW], F32, "ExternalInput")
    w_out = dt_("w_out", [L, D, D], F32, "ExternalInput")
    w_up = dt_("w_up", [L, D, 2 * DFF], F32, "ExternalInput")
    w_down = dt_("w_down", [L, DFF, D], F32, "ExternalInput")
    pvec = dt_("pvec", [L, 128, NP], F32, "ExternalInput")
    NGR = -(-(L * NPACK * 8 * NSLOT * GW * 2 * GW) // (2048 * 128)) * 128
    gtab = dt_("gtab", [NGR, 2048], F32, "ExternalInput")
    flagin = dt_("flagin", [128, 1], F32, "ExternalInput")
    identin = dt_("identin", [128, 128], F32, "ExternalInput")
    yout = dt_("yout", [LT, D], F32, "ExternalOutput")

    wb_in = dt_("wb_in", [L, D, INW], BF16, "Internal")
    wb_kv = dt_("wb_kv", [L, D, 2 * MW], BF16, "Internal")
    wb_out = dt_("wb_out", [L, D, D], BF16, "Internal")
    wb_up = dt_("wb_up", [L, D, 2 * DFF], BF16, "Internal")
    wb_down = dt_("wb_down", [L, DFF, D], BF16, "Internal")
    gtb = dt_("gtb", [NGR, 2048], BF16, "Internal")
    IK = "ExternalOutput" if debug else "Internal"
    XT = dt_("XT", [D, LT], F32, IK)
    QT = dt_("QT", [NAW, LT], BF16, IK)
    KT = dt_("KT", [NAW, LT], BF16, IK)
    VV = dt_("VV", [LT, NAW], BF16, IK)
    QMT = dt_("QMT", [MW, LT], BF16, IK)
    CT = dt_("CT", [CC, LT], BF16, IK)
    XM = dt_("XM", [D, LT], F32, IK)
    H2T = dt_("H2T", [D, LT], BF16, IK)
    X1T = dt_("X1T", [D, LT], F32, IK)

    P = Prog()
    fm = lambda ap: ap.rearrange("(c p) t -> p c t", p=128)

    ctx_arena = nc.sbuf_tensor("arena", [128, ARENA_WORDS], F32)
    ctx_psum = nc.psum_tensor("psum", [128, 8, 512], F32)
    arena_t = ctx_arena.__enter__()
    ps = ctx_psum.__enter__()
    A = Arena(arena_t, ARENA_WORDS)

    pv = A.f32(NP)
    ident_f = A.f32(128)
    ones_f = A.f32(128)
    neghalf = A.f32(512)
    flag = A.f32(1)
    epsc = A.f32(1)
    ident_b = A.bf16(128)
    ones_b = A.bf16(128)
    Km = A.bf16(2, 4, NMEM)
    Vm = A.bf16(2, 2, MW)
    h2halo = A.bf16(KC, 2 * NT)
    base_off = A.off

    def col(off, n=1):
        return pv[:, off:off + n]

    P.dma('sp', ident_f, identin, 'c0', writes=['ident_f'])
    P.dma('sp', flag, flagin, 'c1', writes=['flag'])
    P.op('dve', CP(ident_b, ident_f), reads=['ident_f'], writes=['ident_b'])
    P.op('pool', MSET(ones_b, 1.0), writes=['ones_b'])
    P.op('pool', MSET(ones_f, 1.0), writes=['ones_f'])
    P.op('pool', MSET(neghalf, -0.5), writes=['neghalf'])
    P.op('pool', MSET(epsc, EPS), writes=['epsc'])

    cast_jobs = []

    def plan_cast(dst2d, src2d, key, sem):
        rows, cols = src2d.shape
        assert rows % 128 == 0
        for r in range(0, rows, 128):
            cast_jobs.append((dst2d[r:r + 128, :], src2d[r:r + 128, :], sem))
        P.lastw[key] = (sem, 16 * (rows // 128))
        P.nobarrier.add(sem)

    def issue_casts(n):
        for _ in range(min(n, len(cast_jobs))):
            d_, s_, sem = cast_jobs.pop(0)
            P.dma('pool', d_, s_, sem, writes=[], max_dma_last_dim=4096)

    plan_cast(wb_kv[0], w_kv[0], ('wb_kv', 0), 'pkv0')
    plan_cast(wb_in[0], w_in[0], ('wb_in', 0), 'pin0')
    n_first = len(cast_jobs)
    plan_cast(wb_out[0], w_out[0], ('wb_out', 0), 'pout0')
    plan_cast(gtb, gtab, 'gtb', 'pgtb')
    n_second = len(cast_jobs) - n_first
    plan_cast(wb_up[0], w_up[0], ('wb_up', 0), 'pup0')
    plan_cast(wb_down[0], w_down[0], ('wb_down', 0), 'pdn0')
    if layers > 1:
        plan_cast(wb_kv[1], w_kv[1], ('wb_kv', 1), 'pkv1')
        plan_cast(wb_in[1], w_in[1], ('wb_in', 1), 'pin1')
        plan_cast(wb_out[1], w_out[1], ('wb_out', 1), 'pout1')
        plan_cast(wb_up[1], w_up[1], ('wb_up', 1), 'pup1')
        plan_cast(wb_down[1], w_down[1], ('wb_down', 1), 'pdn1')
    n_rest = len(cast_jobs) - n_first - n_second
    issue_casts(n_first)

    def rstd_chain(sumbank, inv_n, msb, rsb, kms, krs):
        P.op('act', ACTF(msb, ps[:, sumbank, :], AF.Sqrt, scale=inv_n, bias=epsc),
             reads=[('ps', sumbank), 'epsc'], writes=[kms])
        P.op('dve', RCP(rsb, msb), reads=[kms], writes=[krs])

    def sweepA(l, mode):
        A.off = base_off
        P.barrier()
        xtok = [A.f32(D), A.f32(D)] if mode != 'x1' else None
        xT = A.f32(KC, T)
        sq = [A.bf16(T), A.bf16(T), A.bf16(T)]
        msb = [A.f32(T), A.f32(T), A.f32(T)]
        rsb = [A.f32(T), A.f32(T), A.f32(T)]
        pfs = [A.f32(T), A.f32(T), A.f32(T)]
        h = A.bf16(KC, T)
        wblk = [A.bf16(KC, 256), A.bf16(KC, 256), A.bf16(KC, 256)]
        qstage = A.bf16(8, T)
        kstage = A.bf16(8, T)
        qmstage = A.bf16(4, T)
        vstage = A.bf16(4, NAW)
        ustage = A.f32(4, T)
        cstage = A.bf16(4, T)
        thb = [A.f32(T), A.f32(T)]
        PB = Rot([0, 1, 2, 3])
        SB = Rot([4, 5])
        VB = Rot([6, 7])
        tag = f"A{l}{mode}"
        ntile = NT if mode != 'mem' else 1
        gcol = P_GM if mode == 'mem' else P_G1
        wsrc = (wb_kv if mode == 'mem' else wb_in)[l]
        wkey = ('wb_kv', l) if mode == 'mem' else ('wb_in', l)
        nblk = (2 * MW if mode == 'mem' else INW) // 256
        wv = wsrc.rearrange("(c p) n -> p c n", p=128)
        wrot = Rot([0, 1, 2])
        slot2 = Rot([0, 1, 2])
        thr = Rot([0, 1])

        for i in range(ntile):
            t0, t1 = i * T, (i + 1) * T
            if mode == 'x1':
                P.dma('sp', xT, fm(X1T)[:, :, t0:t1], 'xT', reads=[('X1T', i, oc_) for oc_ in range(KC)],
                      writes=[('xT', kc) for kc in range(KC)])
            else:
                src = xin if mode == 'x0' else memin
                for s in range(4):
                    sl = s % 2
                    P.dma('sp', xtok[sl], src[t0 + s * 128:t0 + (s + 1) * 128, :], f'xtok{sl}',
                          writes=[('xtok', sl)])
                    for g in range(4):
                        b = VB.next()
                        P.group('pe', [TR(ps[:, b, q * 128:(q + 1) * 128],
                                          xtok[sl][:, (g * 4 + q) * 128:(g * 4 + q + 1) * 128], ident_f)
                                       for q in range(4)],
                                reads=[('xtok', sl), 'ident_f'], writes=[('ps', b)])
                        P.op('dve', CP(xT[:, g * 4:(g + 1) * 4, s * 128:(s + 1) * 128],
                                       ps[:, b, :].rearrange("p (a b) -> p a b", a=4)),
                             reads=[('ps', b)], writes=[('xT', g * 4 + q) for q in range(4)])
                if mode == 'x0':
                    P.dma('pool', fm(XT)[:, :, t0:t1], xT, 'stXT', reads=[('xT', kc) for kc in range(KC)],
                          writes=[('XT', i)])
            sb = SB.next()
            for kc in range(KC):
                sl = kc % 2
                P.op('act', ACTF(sq[sl], xT[:, kc, :], AF.Square), reads=[('xT', kc)], writes=[('sq', sl)])
                P.op('pe', MM(ps[:, sb, :], ones_b, sq[sl], kc == 0, kc == KC - 1),
                     reads=[('sq', sl), 'ones_b'], writes=[('ps', sb)])
            rstd_chain(sb, 1.0 / D, msb[0], rsb[0], ('ms', 0), ('rs', 0))
            for kc in range(KC):
                P.op('dve', STT(h[:, kc, :], xT[:, kc, :], col(gcol + kc), rsb[0], ALU.mult, ALU.mult),
                     reads=[('xT', kc), ('rs', 0), 'pv'], writes=[('h', kc)])
            hkeys = [('h', kc) for kc in range(KC)]

            pending = []

            def flush():
                for f in pending:
                    f()
                pending.clear()

            def qk_stage1(bank, dst, gofs):
                s2 = slot2.next()
                P.op('dve', CP(pfs[s2], ps[:, bank, :]), reads=[('ps', bank)], writes=[('pf', s2)])
                P.op('act', ACTF(sq[s2], pfs[s2], AF.Square), reads=[('pf', s2)], writes=[('sq', s2)])

                def stage2():
                    sb2 = SB.next()
                    P.op('pe', MM(ps[:, sb2, :], ones_b, sq[s2], True, True),
                         reads=[('sq', s2), 'ones_b'], writes=[('ps', sb2)])
                    P.op('act', ACTF(msb[s2], ps[:, sb2, :], AF.Sqrt, scale=1.0 / 128, bias=epsc),
                         reads=[('ps', sb2), 'epsc'], writes=[('ms', s2)])
                    P.op('dve', RCP(rsb[s2], msb[s2]), reads=[('ms', s2)], writes=[('rs', s2)])
                    a_, b_ = pfs[s2], rsb[s2]
                    if len(dst[0].shape) == 3:
                        a_ = a_.rearrange("p (a b) -> p a b", a=2)
                        b_ = b_.rearrange("p (a b) -> p a b", a=2)
                    P.op('dve', STT(dst[0], a_, col(gofs), b_, ALU.mult, ALU.mult),
                         reads=[('pf', s2), ('rs', s2), 'pv'], writes=[dst[1]])
                pending.append(stage2)

            for b in range(nblk):
                ws = wrot.next()
                P.dma('sp', wblk[ws], wv[:, :, b * 256:(b + 1) * 256], f'wA{ws}', reads=[wkey],
                      writes=[('wA', ws)])
                col0 = b * 256
                if mode == 'mem':
                    kind = 'km' if col0 < MW else 'vm'
                else:
                    kind = ('q' if col0 < 1024 else 'k' if col0 < 2048 else 'v' if col0 < 3072
                            else 'qm' if col0 < 3584 else 'u' if col0 < 4096 else 'g')
                if kind in ('v', 'vm'):
                    for s in range(4):
                        bank = VB.next()
                        P.group('pe', [MM(ps[:, bank, 0:256], h[:, kc, s * 128:(s + 1) * 128], wblk[ws][:, kc, :],
                                          kc == 0, kc == KC - 1) for kc in range(KC)],
                                reads=hkeys + [('wA', ws)], writes=[('ps', bank)])
                        flush()
                        if kind == 'v':
                            vc = col0 - 2048
                            dst, dkey = vstage[:, s, vc:vc + 256], ('vst', s)
                        else:
                            vc = col0 - MW
                            dst, dkey = Vm[:, s // 2, s % 2, vc:vc + 256], ('Vm', s)
                        eng = 'dve' if (s % 2 == 0) else 'act'
                        fn = CP(dst, ps[:, bank, 0:256]) if eng == 'dve' else ACTF(dst, ps[:, bank, 0:256], AF.Copy)
                        P.op(eng, fn, reads=[('ps', bank)], writes=[dkey])
                    continue
                for o in range(2):
                    oc = (col0 + o * 128) // 128
                    bank = PB.next()
                    P.group('pe', [MM(ps[:, bank, :], wblk[ws][:, kc, o * 128:(o + 1) * 128], h[:, kc, :],
                                      kc == 0, kc == KC - 1) for kc in range(KC)],
                            reads=hkeys + [('wA', ws)], writes=[('ps', bank)])
                    flush()
                    if kind == 'q':
                        qk_stage1(bank, (qstage[:, oc, :], ('qst', oc)), P_GQ)
                    elif kind == 'k':
                        qk_stage1(bank, (kstage[:, oc - 8, :], ('kst', oc - 8)), P_GK)
                    elif kind == 'qm':
                        qk_stage1(bank, (qmstage[:, oc - 24, :], ('qmst', oc - 24)), P_GQM)
                    elif kind == 'km':
                        qk_stage1(bank, (Km[:, :, oc, :], ('Km', oc)), P_GKM)
                    elif kind == 'u':
                        j = oc - 28
                        P.op('dve', CP(ustage[:, j, :], ps[:, bank, :]), reads=[('ps', bank)], writes=[('ust', j)])
                    elif kind == 'g':
                        j = oc - 32
                        s2 = thr.next()
                        P.op('act', ACTF(thb[s2], ps[:, bank, :], AF.Tanh, scale=0.5),
                             reads=[('ps', bank)], writes=[('th', s2)])
                        P.op('dve', STT(cstage[:, j, :], thb[s2], 1.0, ustage[:, j, :], ALU.add, ALU.mult),
                             reads=[('th', s2), ('ust', j)], writes=[('cst', j)])
            flush()
            if mode != 'mem':
                P.dma('pool', fm(QT)[:, :, t0:t1], qstage, 'stQ', reads=[('qst', j) for j in range(8)],
                      writes=[('QT', i)])
                P.dma('pool', fm(KT)[:, :, t0:t1], kstage, 'stK', reads=[('kst', j) for j in range(8)],
                      writes=[('KT', i)])
                P.dma('pool', fm(QMT)[:, :, t0:t1], qmstage, 'stQM', reads=[('qmst', j) for j in range(4)],
                      writes=[('QMT', i)])
                P.dma('pool', VV[t0:t1, :].rearrange("(s p) c -> p s c", p=128), vstage, 'stV',
                      reads=[('vst', s) for s in range(4)], writes=[('VV', i)])
                P.dma('pool', fm(CT)[:, :, t0:t1], cstage, 'stC', reads=[('cst', j) for j in range(4)],
                      writes=[('CT', i)])
            if mode == 'x0':
                issue_casts(-(-n_rest // NT))

    def sweepB(l):
        A.off = base_off
        P.barrier()
        qt = A.bf16(8, T)
        qmt = A.bf16(4, T)
        Kw = [A.bf16(1024), A.bf16(1024)]
        Vw = [A.bf16(8, 128), A.bf16(8, 128)]
        Gp = [A.bf16(NSLOT, 128), A.bf16(NSLOT, 128)]
        Pt = [A.bf16(6, 64), A.bf16(6, 64), A.bf16(6, 64)]
        Ptm = [[A.bf16(T), A.bf16(T)], [A.bf16(T), A.bf16(T)]]
        rden = [A.f32(T), A.f32(T)]
        cat = A.bf16(KC, T)
        cwin = A.bf16(4, T + 30)
        dg = [A.bf16(128) for _ in range(8)]
        cv = A.f32(4, T)
        sqf = [A.f32(T), A.f32(T)]
        msb = A.f32(T)
        rsb = A.f32(T)
        thb = [A.f32(T), A.f32(T)]
        xch = [A.f32(T), A.f32(T)]
        xmst = [A.f32(T), A.f32(T)]
        sq = [A.bf16(T), A.bf16(T)]
        h2 = A.bf16(KC, T)
        wblk = [A.bf16(KC, 256), A.bf16(KC, 256), A.bf16(KC, 256)]
        SC = Rot([0, 1, 6])
        OB = Rot([2, 3])
        DBk = Rot([4, 5])
        P2 = Rot([7])
        hrot = Rot([0, 1])
        prot = Rot([0, 1, 2])
        wrot = Rot([0, 1, 2])
        r2 = Rot([0, 1])
        dgr = Rot(range(8))
        wv = wb_out[l].rearrange("(c p) n -> p c n", p=128)
        KTv, QTv, QMTv, CTv = fm(KT), fm(QT), fm(QMT), fm(CT)
        gtv = gtb.rearrange("a b -> (a b)")[0:L * NPACK * 8 * NSLOT * GW * 2 * GW].rearrange(
            "(l k h s q c) -> l k h q s c", l=L, k=NPACK, h=8, s=NSLOT, q=GW)
        Xsrc = XT if l == 0 else X1T
        xkey = 'XT' if l == 0 else 'X1T'
        mid = NT // 2

        for i in range(NT):
            t0, t1 = i * T, (i + 1) * T
            tg = geo['tiles'][i]
            w0, w1 = tg['w0'], tg['w1']
            nwin = (w1 - w0) * GW
            ktiles = sorted(set(range((w0 * GW) // T, ((w1 * GW) - 1) // T + 1)))
            P.dma('sp', qt, QTv[:, :, t0:t1], 'ldq', reads=[('QT', i)], writes=['qt'])
            P.dma('sp', qmt, QMTv[:, :, t0:t1], 'ldqm', reads=[('QMT', i)], writes=['qmt'])
            lo, hi = t0 - 15, t1 + 15
            clo, chi = max(lo, 0), min(hi, LT)
            if clo > lo:
                P.op('pool', MSET(cwin[:, :, 0:15], 0.0), writes=['cwin'])
            if chi < hi:
                P.op('pool', MSET(cwin[:, :, T + 15:T + 30], 0.0), writes=['cwin'])
            P.dma('sp', cwin[:, :, clo - lo:chi - lo], CTv[:, :, clo:chi], 'ldc',
                  reads=[('CT', j) for j in range(max(i - 1, 0), min(i + 2, NT))], writes=['cwin'])
            if i == mid:
                P.op('dve', TS(cwin[:, :, 0:15], cwin[:, :, 0:15], flag[:, 0:1], ALU.mult),
                     reads=['cwin', 'flag'], writes=['cwin'])
            if i == mid - 1:
                P.op('dve', TS(cwin[:, :, T + 15:T + 30], cwin[:, :, T + 15:T + 30], flag[:, 0:1], ALU.mult),
                     reads=['cwin', 'flag'], writes=['cwin'])

            def na_head(hd):
                    hs = hrot.next()
                    P.dma('sp', Kw[hs][:, 0:nwin], KTv[:, hd, w0 * GW:w1 * GW], f'ldk{hs}',
                          reads=[('KT', j) for j in ktiles], writes=[('Kw', hs)])
                    P.dma('sp', Vw[hs][:, 0:(w1 - w0) // 2, :],
                          VV[w0 * GW:w1 * GW, hd * 128:(hd + 1) * 128].rearrange("(c p) d -> p c d", p=128),
                          f'ldv{hs}', reads=[('VV', j) for j in ktiles], writes=[('Vw', hs)])
                    P.dma('sp', Gp[hs][0:GW, :, :], gtv[l, tg['pack'], hd], f'ldg{hs}', reads=['gtb'],
                          writes=[('Gp', hs)])
                    ob, db = OB.next(), DBk.next()
                    def row_score(rl):
                        r = 8 * i + rl
                        start, nch, _ = geo['rows'][r]
                        sc = SC.next()
                        pp = prot.next()
                        mms = []
                        for ci in range(nch):
                            ko = (start + 2 * ci - w0) * GW
                            o_ = ps[:, sc, ci * 64:(ci + 1) * 64]
                            mms.append(MM(o_, Kw[hs][:, ko:ko + 128], qt[:, hd, rl * 64:(rl + 1) * 64], True, False))
                            mms.append(MM(o_, Gp[hs][0:GW, tg['slotmap'][(rl, ci)], :], ident_b[0:GW, 0:GW],
                                          False, True))
                        P.group('pe', mms, reads=[('Kw', hs), ('Gp', hs), 'qt', 'ident_b'], writes=[('ps', sc)])
                        P.op('act', ACTF(Pt[pp][:, 0:nch, :],
                                         ps[:, sc, 0:nch * 64].rearrange("p (a b) -> p a b", a=nch), AF.Exp),
                             reads=[('ps', sc)], writes=[('Pt', pp)])
                        return (rl, start, nch, pp)

                    def row_pv(st):
                        rl, start, nch, pp = st
                        mms = []
                        for ci in range(nch):
                            vi = (start + 2 * ci - w0) // 2
                            mms.append(MM(ps[:, ob, rl * 64:(rl + 1) * 64], Vw[hs][:, vi, :], Pt[pp][:, ci, :],
                                          ci == 0, ci == nch - 1))
                        for ci in range(nch):
                            mms.append(MM(ps[:, db, rl * 64:(rl + 1) * 64], ones_b, Pt[pp][:, ci, :],
                                          ci == 0, ci == nch - 1))
                        P.group('pe', mms, reads=[('Vw', hs), ('Pt', pp), 'ones_b'],
                                writes=[('ps', ob), ('ps', db)])

                    prev = None
                    for rl in range(8):
                        cur = row_score(rl)
                        if prev is not None:
                            row_pv(prev)
                        prev = cur
                    row_pv(prev)
                    rr = r2.next()
                    P.op('dve', RCP(rden[rr], ps[:, db, :]), reads=[('ps', db)], writes=[('rden', rr)])
                    P.op('dve', TT(cat[:, hd, :], ps[:, ob, :], rden[rr], ALU.mult),
                         reads=[('ps', ob), ('rden', rr)], writes=[('cat', hd)])


            def conv_part():
                for j in range(4):
                    cbk = P2.next()
                    for k in range(31):
                        ds = dgr.next()
                        P.op('dve', TS(dg[ds], ident_b, col(P_CW + j * 31 + k), ALU.mult),
                             reads=['ident_b', 'pv'], writes=[('dg', ds)])
                        P.op('pe', MM(ps[:, cbk, :], dg[ds], cwin[:, j, k:k + T], k == 0, k == 30),
                             reads=[('dg', ds), 'cwin'], writes=[('ps', cbk)])
                    P.op('act', ACTF(cv[:, j, :], ps[:, cbk, :], AF.Identity, bias=col(P_CB + j)),
                         reads=[('ps', cbk), 'pv'], writes=[('cv', j)])

            def ln_part1():
                mb = P2.next()
                P.group('pe', [MM(ps[:, mb, :], ones_f, cv[:, j, :], j == 0, j == 3) for j in range(4)],
                        reads=[('cv', j) for j in range(4)] + ['ones_f'], writes=[('ps', mb)])
                for j in range(4):
                    P.op('dve', STT(cv[:, j, :], ps[:, mb, :], -1.0 / CC, cv[:, j, :], ALU.mult, ALU.add),
                         reads=[('ps', mb), ('cv', j)], writes=[('cv', j)])

            def ln_part2():
                vb = P2.next()
                for j in range(4):
                    s2 = j % 2
                    P.op('act', ACTF(sqf[s2], cv[:, j, :], AF.Square), reads=[('cv', j)], writes=[('sqf', s2)])
                    P.op('pe', MM(ps[:, vb, :], ones_f, sqf[s2], j == 0, j == 3),
                         reads=[('sqf', s2), 'ones_f'], writes=[('ps', vb)])
                P.op('act', ACTF(msb, ps[:, vb, :], AF.Sqrt, scale=1.0 / CC, bias=epsc),
                     reads=[('ps', vb), 'epsc'], writes=['msB'])
                P.op('dve', RCP(rsb, msb), reads=['msB'], writes=['rsB'])
                for j in range(4):
                    P.op('dve', TT(cv[:, j, :], cv[:, j, :], rsb, ALU.mult), reads=[('cv', j), 'rsB'],
                         writes=[('cv', j)])
                    P.op('dve', TS(cv[:, j, :], cv[:, j, :], col(P_LNG + j), ALU.mult, col(P_LNB + j), ALU.add),
                         reads=[('cv', j), 'pv'], writes=[('cv', j)])
                    s2 = j % 2
                    P.op('act', ACTF(thb[s2], cv[:, j, :], AF.Tanh), reads=[('cv', j)], writes=[('thB', s2)])
                    P.op('dve', STT(cat[:, 12 + j, :], thb[s2], 1.0, cv[:, j, :], ALU.add, ALU.mult),
                         reads=[('thB', s2), ('cv', j)], writes=[('cat', 12 + j)])

            conv_part()
            na_head(0)
            na_head(1)
            ln_part1()
            na_head(2)
            na_head(3)
            ln_part2()
            for hd in range(4, 8):
                na_head(hd)

            mi = 0 if i < mid else 1

            def mem_score(hm):
                par = hm % 2
                for ch in range(2):
                    sc = SC.next()
                    P.group('pe', [MM(ps[:, sc, :], Km[:, mi, hm, ch * 128:(ch + 1) * 128], qmt[:, hm, :], True, True)],
                            reads=[('Km', hm), 'qmt'], writes=[('ps', sc)])
                    P.op('act', ACTF(Ptm[par][ch], ps[:, sc, :], AF.Exp), reads=[('ps', sc)],
                         writes=[('Ptm', par, ch)])

            def mem_pv(hm):
                par = hm % 2
                ob, db = OB.next(), DBk.next()
                mms = [MM(ps[:, ob, :], Vm[:, mi, ch, hm * 128:(hm + 1) * 128], Ptm[par][ch], ch == 0, ch == 1)
                       for ch in range(2)]
                mms += [MM(ps[:, db, :], ones_b, Ptm[par][ch], ch == 0, ch == 1) for ch in range(2)]
                P.group('pe', mms, reads=[('Vm', s_) for s_ in range(4)] + [('Ptm', par, 0), ('Ptm', par, 1), 'ones_b'],
                        writes=[('ps', ob), ('ps', db)])
                rr = r2.next()
                P.op('dve', RCP(rden[rr], ps[:, db, :]), reads=[('ps', db)], writes=[('rden', rr)])
                P.op('dve', TT(cat[:, 8 + hm, :], ps[:, ob, :], rden[rr], ALU.mult),
                     reads=[('ps', ob), ('rden', rr)], writes=[('cat', 8 + hm)])

            for hm in range(4):
                mem_score(hm)
                if hm > 0:
                    mem_pv(hm - 1)
            mem_pv(3)

            catkeys = [('cat', k) for k in range(KC)]
            sb = P2.next()
            pend_ss = []

            def flush_ss():
                while pend_ss:
                    oc_, xs_ = pend_ss.pop(0)
                    P.op('pe', MM(ps[:, sb, :], ones_b, sq[xs_], oc_ == 0, oc_ == KC - 1),
                         reads=[('sqB', xs_), 'ones_b'], writes=[('ps', sb)])
            for b in range(8):
                ws = wrot.next()
                P.dma('sp', wblk[ws], wv[:, :, b * 256:(b + 1) * 256], f'wB{ws}', reads=[('wb_out', l)],
                      writes=[('wB', ws)])
                for o in range(2):
                    oc = b * 2 + o
                    xs = oc % 2
                    P.dma('sp', xch[xs], fm(Xsrc)[:, oc, t0:t1], f'ldx{xs}', reads=[('XT', i) if l == 0 else ('X1T', i, oc)], writes=[('xch', xs)])
                    bank = SC.next()
                    P.group('pe', [MM(ps[:, bank, :], wblk[ws][:, kc, o * 128:(o + 1) * 128], cat[:, kc, :],
                                      kc == 0, kc == KC - 1) for kc in range(KC)],
                            reads=catkeys + [('wB', ws)], writes=[('ps', bank)])
                    flush_ss()
                    P.op('dve', TT(xmst[xs], ps[:, bank, :], xch[xs], ALU.add),
                         reads=[('ps', bank), ('xch', xs)], writes=[('xmst', xs)])
                    P.dma('pool', fm(XM)[:, oc, t0:t1], xmst[xs], f'stxm{xs}', reads=[('xmst', xs)], writes=[('XM', i, oc)])
                    P.op('act', ACTF(sq[xs], xmst[xs], AF.Square), reads=[('xmst', xs)], writes=[('sqB', xs)])
                    pend_ss.append((oc, xs))
                    P.op('pool', TS(h2[:, oc, :], xmst[xs], col(P_G2 + oc), ALU.mult, 0.0, ALU.add),
                         reads=[('xmst', xs), 'pv'], writes=[('h2', oc)])
            flush_ss()
            P.op('act', ACTF(msb, ps[:, sb, :], AF.Sqrt, scale=1.0 / D, bias=epsc),
                 reads=[('ps', sb), 'epsc'], writes=['msB'])
            P.op('dve', RCP(rsb, msb), reads=['msB'], writes=['rsB'])
            for oc in range(KC):
                P.op('dve', TT(h2[:, oc, :], h2[:, oc, :], rsb, ALU.mult), reads=[('h2', oc), 'rsB'],
                     writes=[('h2', oc)])
            h2keys = [('h2', k) for k in range(KC)]
            P.op('dve', CP(h2halo[:, :, 2 * i:2 * i + 1], h2[:, :, 0:1]), reads=h2keys, writes=[('h2halo', 2 * i)])
            P.op('dve', CP(h2halo[:, :, 2 * i + 1:2 * i + 2], h2[:, :, T - 1:T]), reads=h2keys,
                 writes=[('h2halo', 2 * i + 1)])
            P.dma('pool', fm(H2T)[:, :, t0:t1], h2, 'sth2', reads=h2keys, writes=[('H2T', i)])

    def sweepC(l):
        A.off = base_off
        P.barrier()
        h2 = A.bf16(KC, T)
        hid = A.bf16(HC, T)
        wu = [(A.bf16(KC, 256), A.bf16(KC, 256)) for _ in range(2)]
        wd = [A.bf16(HC, 128) for _ in range(3)]
        ag = [A.f32(T), A.f32(T)]
        av = [A.f32(T), A.f32(T)]
        th = [A.f32(T), A.f32(T)]
        xmc = [A.f32(T), A.f32(T)]
        xo = [A.f32(T), A.f32(T)]
        uph = A.bf16(88, 2 * NT)
        edge = A.f32(88, 2)
        GBk = Rot([0, 1])
        VBk = Rot([2, 3])
        OP = Rot([4, 5])
        HB = Rot([6, 7])
        wurot = Rot([0, 1])
        wdrot = Rot([0, 1, 2])
        r2 = Rot([0, 1])
        wuv = wb_up[l].rearrange("(c p) n -> p c n", p=128)
        wdv = wb_down[l].rearrange("(c p) n -> p c n", p=128)
        mid = NT // 2
        Xdst = X1T
        hkeys = [('h2c', k) for k in range(KC)]
        halokeys = [('h2halo', k) for k in range(2 * NT)]
        fwc = lambda c, k: col(P_FW + c * 3 + k)

        for b in range(2 * DFF // 256):
            ws = wurot.next()
            P.dma('sp', wu[ws][0], wuv[:, :, b * 256:(b + 1) * 256], f'wu{ws}', reads=[('wb_up', l)],
                  writes=[('wu', ws)])
            for o in range(2):
                c = b * 2 + o
                hb = HB.next()
                P.group('pe', [MM(ps[:, hb, 0:2 * NT], wu[ws][0][:, kc, o * 128:(o + 1) * 128], h2halo[:, kc, :],
                                  kc == 0, kc == KC - 1) for kc in range(KC)],
                        reads=halokeys + [('wu', ws)], writes=[('ps', hb)])
                P.op('act', ACTF(uph[:, c, :], ps[:, hb, 0:2 * NT], AF.Copy), reads=[('ps', hb)], writes=['uph'])
        P.op('dve', TS(uph[:, :, 2 * mid - 1:2 * mid + 1], uph[:, :, 2 * mid - 1:2 * mid + 1], flag[:, 0:1], ALU.mult),
             reads=['uph', 'flag'], writes=['uph'])

        for i in range(NT):
            t0, t1 = i * T, (i + 1) * T
            P.dma('sp', h2, fm(H2T)[:, :, t0:t1], 'ldh2', reads=[('H2T', i)], writes=hkeys)
            fw3 = pv[:, P_FW:P_FW + 264].rearrange("p (c k) -> p c k", k=3)
            if i > 0:
                P.op('dve', TT(edge[:, :, 0:1], uph[:, :, 2 * i - 1:2 * i], fw3[:, :, 0:1], ALU.mult),
                     reads=['uph', 'pv'], writes=['edge'])
            else:
                P.op('dve', MSET(edge[:, :, 0:1], 0.0), writes=['edge'])
            if i < NT - 1:
                P.op('dve', TT(edge[:, :, 1:2], uph[:, :, 2 * i + 2:2 * i + 3], fw3[:, :, 2:3], ALU.mult),
                     reads=['uph', 'pv'], writes=['edge'])
            else:
                P.op('dve', MSET(edge[:, :, 1:2], 0.0), writes=['edge'])

            for jb in range(HC // 2):
                ws = wurot.next()
                P.dma('sp', wu[ws][0], wuv[:, :, jb * 256:(jb + 1) * 256], f'wu{ws}', reads=[('wb_up', l)],
                      writes=[('wu', ws)])
                P.dma('sp', wu[ws][1], wuv[:, :, DFF + jb * 256:DFF + (jb + 1) * 256], f'wu{ws}',
                      reads=[('wb_up', l)], writes=[('wu', ws)])
                for o in range(2):
                    j = jb * 2 + o
                    gb, vb = GBk.next(), VBk.next()
                    P.group('pe', [MM(ps[:, gb, :], wu[ws][0][:, kc, o * 128:(o + 1) * 128], h2[:, kc, :],
                                      kc == 0, kc == KC - 1) for kc in range(KC)],
                            reads=hkeys + [('wu', ws)], writes=[('ps', gb)])
                    P.group('pe', [MM(ps[:, vb, :], wu[ws][1][:, kc, o * 128:(o + 1) * 128], h2[:, kc, :],
                                      kc == 0, kc == KC - 1) for kc in range(KC)],
                            reads=hkeys + [('wu', ws)], writes=[('ps', vb)])
                    s2 = r2.next()
                    for (bank, dst, c, key) in ((gb, ag[s2], j, 'ag'), (vb, av[s2], HC + j, 'av')):
                        P.op('act', ACTF(dst, ps[:, bank, :], AF.Identity, scale=fwc(c, 1), bias=col(P_FB + c)),
                             reads=[('ps', bank), 'pv'], writes=[(key, s2)])
                        P.op('dve', STT(dst[:, 1:T], ps[:, bank, 0:T - 1], fwc(c, 0), dst[:, 1:T], ALU.mult, ALU.add),
                             reads=[('ps', bank), 'pv', (key, s2)], writes=[(key, s2)])
                        P.op('dve', STT(dst[:, 0:T - 1], ps[:, bank, 1:T], fwc(c, 2), dst[:, 0:T - 1], ALU.mult, ALU.add),
                             reads=[('ps', bank), 'pv', (key, s2)], writes=[(key, s2)])
                        P.op('pool', TT(dst[:, 0:1], dst[:, 0:1], edge[:, c, 0:1], ALU.add),
                             reads=[(key, s2), 'edge'], writes=[(key, s2)])
                        P.op('pool', TT(dst[:, T - 1:T], dst[:, T - 1:T], edge[:, c, 1:2], ALU.add),
                             reads=[(key, s2), 'edge'], writes=[(key, s2)])
                    P.op('act', ACTF(th[s2], ag[s2], AF.Tanh, scale=0.5), reads=[('ag', s2)], writes=[('thC', s2)])
                    P.op('dve', STT(th[s2], th[s2], 1.0, ag[s2], ALU.add, ALU.mult),
                         reads=[('thC', s2), ('ag', s2)], writes=[('thC', s2)])
                    P.op('pool', TT(hid[:, j, :], th[s2], av[s2], ALU.mult),
                         reads=[('thC', s2), ('av', s2)], writes=[('hid', j)])
            hidkeys = [('hid', j) for j in range(HC)]
            for oc in range(KC):
                ws = wdrot.next()
                P.dma('sp', wd[ws], wdv[:, :, oc * 128:(oc + 1) * 128], f'wd{ws}', reads=[('wb_down', l)],
                      writes=[('wd', ws)])
                xs = oc % 2
                P.dma('sp', xmc[xs], fm(XM)[:, oc, t0:t1], f'ldxm{xs}', reads=[('XM', i, oc)], writes=[('xmc', xs)])
                bank = OP.next()
                P.group('pe', [MM(ps[:, bank, :], wd[ws][:, k, :], hid[:, k, :], k == 0, k == HC - 1)
                               for k in range(HC)],
                        reads=hidkeys + [('wd', ws)], writes=[('ps', bank)])
                P.op('dve', TT(xo[xs], ps[:, bank, :], xmc[xs], ALU.add), reads=[('ps', bank), ('xmc', xs)],
                     writes=[('xo', xs)])
                P.dma('pool', fm(Xdst)[:, oc, t0:t1], xo[xs], f'stxo{xs}', reads=[('xo', xs)], writes=[('X1T', i, oc)])

    def sweepD():
        A.off = base_off
        P.barrier()
        xT = [A.f32(KC, T), A.f32(KC, T)]
        yt = [A.f32(D), A.f32(D)]
        TB = Rot(range(8))
        for i in range(NT):
            t0, t1 = i * T, (i + 1) * T
            xs = i % 2
            P.dma('sp', xT[xs], fm(X1T)[:, :, t0:t1], f'ldD{xs}', reads=[('X1T', i, oc_) for oc_ in range(KC)], writes=[('xTD', xs)])
            for s in range(4):
                ys = s % 2
                for g in range(4):
                    b = TB.next()
                    P.group('pe', [TR(ps[:, b, q * 128:(q + 1) * 128], xT[xs][:, g * 4 + q, s * 128:(s + 1) * 128], ident_f)
                                   for q in range(4)], reads=[('xTD', xs), 'ident_f'], writes=[('ps', b)])
                    eng = 'dve' if g % 2 == 0 else 'act'
                    dst = yt[ys][:, g * 512:(g + 1) * 512]
                    fn = CP(dst, ps[:, b, :]) if eng == 'dve' else ACTF(dst, ps[:, b, :], AF.Copy)
                    P.op(eng, fn, reads=[('ps', b)], writes=[('yt', ys)])
                P.dma('sp', yout[t0 + s * 128:t0 + (s + 1) * 128, :], yt[ys], f'stD{ys}', reads=[('yt', ys)],
                      writes=[('yout', i, s)])

    for l in range(layers):
        P.dma('sp', pv, pvec[l], 'pv', writes=['pv'])
        sc = 128.0 ** -0.5
        for (off, n, c) in ((P_GQ, 1, sc), (P_GQM, 1, sc), (P_CW, 124, 0.5), (P_LNG, 8, 0.5),
                            (P_FW + HC * 3, HC * 3, 0.5), (P_FB + HC, HC, 0.5)):
            P.op('dve', TS(col(off, n), col(off, n), c, ALU.mult), reads=['pv'], writes=['pv'])
        if stop_after == ('P', l):
            break
        sweepA(l, 'mem')
        if l == 0:
            issue_casts(n_second)
        if stop_after == ('M', l):
            break
        sweepA(l, 'x0' if l == 0 else 'x1')
        if stop_after == ('A', l):
            break
        issue_casts(len(cast_jobs))
        sweepB(l)
        if stop_after == ('B', l):
            break
        sweepC(l)
    if stop_after is None:
        sweepD()
        P.wait_all('sp', [('yout', i, s) for i in range(NT) for s in range(4)])
    else:
        P.wait_all('sp', list(P.lastw.keys()))

    semnames = ENGS + sorted(P.dmasems)
    ctxs = [nc.semaphore(f"s_{n}") for n in semnames]
    sems = {n: c.__enter__() for n, c in zip(semnames, ctxs)}
    with nc.Block() as block:
        for e in ENGS:
            ops = P.ops[e]

            def body(eng, ops=ops, e=e):
                for o in ops:
                    if o[0] == 'wait':
                        eng.wait_ge(sems[o[1]], o[2])
                    elif o[0] == 'op':
                        ins = o[1](eng)
                        if o[2]:
                            ins.then_inc(sems[e], 1)
                    else:
                        eng.dma_start(out=o[1], in_=o[2], **o[4]).then_inc(sems[o[3]], 16)
            getattr(block, ENGATTR[e])(body)
    for c in reversed(ctxs):
        c.__exit__(None, None, None)
    ctx_psum.__exit__(None, None, None)
    ctx_arena.__exit__(None, None, None)
    stats = {e: len(P.ops[e]) for e in ENGS}
    stats['nops'] = P.nops
    return nc, geo, stats


def core_inputs(xseq, mem2, ty, shared, NT, geo, inp):
    m = dict(shared)
    m['xin'] = np.ascontiguousarray(xseq, dtype=np.float32)
    m['memin'] = np.ascontiguousarray(mem2.reshape(2 * NMEM, D), dtype=np.float32)
    g = build_gtab(inp['na_rpb'], ty, NT, geo)
    flat = g.reshape(-1)
    ngr = -(-flat.size // (2048 * 128)) * 128
    buf = np.zeros((ngr * 2048,), np.float32)
    buf[:flat.size] = flat
    m['gtab'] = buf.reshape(ngr, 2048)
    m['flagin'] = np.full((128, 1), 1.0 if ty == 'A' else 0.0, np.float32)
    return m


def shared_inputs(inp):
    return {
        'w_in': np.ascontiguousarray(inp['w_in'], dtype=np.float32),
        'w_kv': np.ascontiguousarray(inp['w_mem_kv'], dtype=np.float32),
        'w_out': np.ascontiguousarray(inp['w_out'], dtype=np.float32),
        'w_up': np.ascontiguousarray(inp['w_up'], dtype=np.float32),
        'w_down': np.ascontiguousarray(inp['w_down'], dtype=np.float32),
        'pvec': build_pvec(inp),
        'identin': np.eye(128, dtype=np.float32),
    }


_CACHE = {}


def kernel(**inputs):
    inp = {k: np.asarray(v) for k, v in inputs.items()}
    NT = 16
    if NT not in _CACHE:
        _CACHE[NT] = build_program(NT)
    nc, geo, _ = _CACHE[NT]
    shared = shared_inputs(inp)
    xp, xs, mp, ms = inp['x_prompt'], inp['x_sample'], inp['mem_prompt'], inp['mem_sample']
    maps = []
    for b in range(2):
        maps.append(core_inputs(xs[b], np.stack([ms[b], ms[b]]), 'A', shared, NT, geo, inp))
    for k in range(4):
        maps.append(core_inputs(np.concatenate([xp[2 * k], xp[2 * k + 1]], axis=0),
                                np.stack([mp[2 * k], mp[2 * k + 1]]), 'B', shared, NT, geo, inp))
    for k in range(2):
        maps.append(core_inputs(np.zeros((NT * T, D), np.float32), np.zeros((2, NMEM, D), np.float32), 'B',
                                shared, NT, geo, inp))
    res = run_bass_kernel_spmd(nc, maps, core_ids=list(range(8)))
    outs = [np.asarray(r['yout']) for r in res.results]
    y_sample = np.stack([outs[0], outs[1]]).astype(np.float32)
    y_prompt = np.stack([outs[2 + k // 2][(k % 2) * 4096:(k % 2 + 1) * 4096] for k in range(8)]).astype(np.float32)
    return (y_prompt, y_sample)
```

```python
import math
from collections import defaultdict
import numpy as np
import concourse.bass as bass
import concourse.mybir as mybir
from concourse.bass_utils import run_bass_kernel_spmd

F32 = mybir.dt.float32
BF16 = mybir.dt.bfloat16
AF = mybir.ActivationFunctionType
ALU = mybir.AluOpType

D = 2048
KC = 16
T = 512
NAW = 1024
MW = 512
CC = 512
DFF = 5632
HC = 44
INW = 4608
NMEM = 256
EPS = 1e-6
GW = 64
NEG = -30000.0
L = 2
NP = 540
P_G1, P_GM, P_G2, P_GQ, P_GK, P_GQM, P_GKM, P_CB, P_LNG, P_LNB, P_CW, P_FW, P_FB = \
    0, 16, 32, 48, 49, 50, 51, 52, 56, 60, 64, 188, 452
ARENA_WORDS = 43520


def _tw(r, ty, R):
    if ty == 'A':
        return int(np.clip(r - 4, 0, R - 8))
    half = R // 2
    base = 0 if r < half else half
    return base + int(np.clip(r - base - 4, 0, half - 8))


def geometry(NT):
    R = 8 * NT
    rows = []
    for r in range(R):
        a, b = _tw(r, 'A', R), _tw(r, 'B', R)
        lo, hi = min(a, b), max(a, b) + 8
        start = lo - (lo % 2)
        nch = (hi - start + 1) // 2
        assert start + 2 * nch <= R
        rows.append((start, nch, a == b))
    tiles = []
    packs = {}
    for i in range(NT):
        keys = []
        slotmap = {}
        for rl in range(8):
            r = 8 * i + rl
            start, nch, same = rows[r]
            rs = _tw(r, 'A', R)
            for ci in range(nch):
                kr = start + 2 * ci
                if same:
                    key = ('g', kr - r + 7, rs <= kr < rs + 8, rs <= kr + 1 < rs + 8)
                else:
                    key = ('s', rl, ci)
                if key not in keys:
                    keys.append(key)
                slotmap[(rl, ci)] = keys.index(key)
        sig = tuple(keys)
        if sig not in packs:
            packs[sig] = (len(packs), i)
        w0 = min(rows[8 * i + rl][0] for rl in range(8))
        w1 = max(rows[8 * i + rl][0] + 2 * rows[8 * i + rl][1] for rl in range(8))
        assert (w1 - w0) <= 16 and w0 % 2 == 0
        tiles.append(dict(keys=keys, slotmap=slotmap, pack=packs[sig][0], w0=w0, w1=w1))
    nslot = max(len(t['keys']) for t in tiles)
    packlist = sorted(packs.values())
    return dict(R=R, rows=rows, tiles=tiles, nslot=nslot, npack=len(packlist),
                pack_rep=[p[1] for p in packlist])


def build_gtab(rpb, ty, NT, geo):
    R = geo['R']
    qc = np.arange(GW)[:, None]
    kc = np.arange(GW)[None, :]
    cs = np.clip(qc - 8, 0, GW - 16)
    valid = (kc >= cs) & (kc < cs + 16)
    dc = np.clip(kc - qc + 15, 0, 30)
    Tf = np.where(valid[None, None, None], rpb[:, :, :, dc], np.float32(NEG)).astype(np.float32)
    negt = np.full((rpb.shape[0], 8, GW, GW), NEG, np.float32)
    out = np.full((rpb.shape[0], geo['npack'], 8, geo['nslot'], GW, 2 * GW), NEG, np.float32)
    for p, i in enumerate(geo['pack_rep']):
        t = geo['tiles'][i]
        for k, key in enumerate(t['keys']):
            halves = []
            if key[0] == 'g':
                _, dA, vA, vB = key
                for d, v in ((dA, vA), (dA + 1, vB)):
                    halves.append(Tf[:, :, d] if (v and 0 <= d <= 14) else negt)
            else:
                _, rl, ci = key
                r = 8 * i + rl
                start = geo['rows'][r][0]
                rs = _tw(r, ty, R)
                for hf in range(2):
                    kr = start + 2 * ci + hf
                    d = kr - r + 7
                    v = (rs <= kr < rs + 8)
                    halves.append(Tf[:, :, d] if (v and 0 <= d <= 14) else negt)
            out[:, p, :, k, :, 0:GW] = halves[0]
            out[:, p, :, k, :, GW:] = halves[1]
    return out


def build_pvec(inp):
    pv = np.zeros((L, 128, NP), np.float32)
    fm = lambda a, n: a.reshape(L, n, 128).transpose(0, 2, 1)
    pv[:, :, P_G1:P_G1 + 16] = fm(inp['norm_mix_g'], 16)
    pv[:, :, P_GM:P_GM + 16] = fm(inp['mem_norm_g'], 16)
    pv[:, :, P_G2:P_G2 + 16] = fm(inp['norm_ffn_g'], 16)
    pv[:, :, P_GQ] = inp['na_q_norm_g']
    pv[:, :, P_GK] = inp['na_k_norm_g']
    pv[:, :, P_GQM] = inp['mem_q_norm_g']
    pv[:, :, P_GKM] = inp['mem_k_norm_g']
    pv[:, :, P_CB:P_CB + 4] = fm(inp['conv_dw_b'], 4)
    pv[:, :, P_LNG:P_LNG + 4] = fm(inp['conv_ln_g'], 4)
    pv[:, :, P_LNB:P_LNB + 4] = fm(inp['conv_ln_b'], 4)
    cw = inp['conv_dw_w'].reshape(L, 31, 4, 128).transpose(0, 3, 2, 1)
    pv[:, :, P_CW:P_CW + 124] = cw.reshape(L, 128, 124)
    fw = inp['ffn_dw_w'].reshape(L, 3, 88, 128).transpose(0, 3, 2, 1)
    pv[:, :, P_FW:P_FW + 264] = fw.reshape(L, 128, 264)
    pv[:, :, P_FB:P_FB + 88] = fm(inp['ffn_dw_b'], 88)
    return pv


ENGS = ['pe', 'act', 'dve', 'pool', 'sp']
ENGATTR = {'pe': 'tensor', 'act': 'scalar', 'dve': 'vector', 'pool': 'gpsimd', 'sp': 'sync'}


class Prog:
    def __init__(self):
        self.cnt = defaultdict(int)
        self.ops = {e: [] for e in ENGS}
        self.seen = {e: defaultdict(int) for e in ENGS}
        self.lastw = {}
        self.readers = defaultdict(dict)
        self.dmasems = set()
        self.nobarrier = set()
        import os
        self.limit = int(os.environ.get('KLIMIT', '0')) or None
        self.nops = 0

    def _skip(self):
        self.nops += 1
        return self.limit is not None and self.nops > self.limit

    def _deps(self, reads, writes):
        need = {}

        def add(ev):
            s, v = ev
            if need.get(s, 0) < v:
                need[s] = v
        for k in reads:
            if k in self.lastw:
                add(self.lastw[k])
        for k in writes:
            if k in self.lastw:
                add(self.lastw[k])
            for s, v in self.readers[k].items():
                add((s, v))
        return need

    def _commit(self, ev, reads, writes):
        s, v = ev
        for k in reads:
            rd = self.readers[k]
            if rd.get(s, 0) < v:
                rd[s] = v
        for k in writes:
            self.lastw[k] = ev
            self.readers[k] = {}

    def _waits(self, eng, need):
        for s, v in need.items():
            if eng == 'pe' and s == 'pe':
                continue
            if self.seen[eng][s] >= v:
                continue
            self.seen[eng][s] = v
            self.ops[eng].append(('wait', s, v))

    def op(self, eng, fn, reads=(), writes=()):
        if self._skip():
            return None
        self._waits(eng, self._deps(reads, writes))
        self.cnt[eng] += 1
        ev = (eng, self.cnt[eng])
        self.ops[eng].append(('op', fn, True))
        self._commit(ev, reads, writes)
        return ev

    def group(self, eng, fns, reads=(), writes=()):
        if self._skip():
            return None
        self._waits(eng, self._deps(reads, writes))
        for f in fns[:-1]:
            self.ops[eng].append(('op', f, False))
        self.cnt[eng] += 1
        ev = (eng, self.cnt[eng])
        self.ops[eng].append(('op', fns[-1], True))
        self._commit(ev, reads, writes)
        return ev

    def dma(self, q, out, in_, sem, reads=(), writes=(), **kw):
        if self._skip():
            return None
        self._waits(q, self._deps(reads, writes))
        self.dmasems.add(sem)
        self.cnt[sem] += 16
        ev = (sem, self.cnt[sem])
        self.ops[q].append(('dma', out, in_, sem, kw))
        self._commit(ev, reads, writes)
        return ev

    def barrier(self):
        if self.limit is not None and self.nops > self.limit:
            return
        for e in ENGS:
            self._waits(e, {s_: c for s_, c in self.cnt.items() if c > 0 and s_ not in self.nobarrier})

    def wait_all(self, eng, keys):
        self._waits(eng, self._deps(keys, ()))


class Rot:
    def __init__(self, items):
        self.items = list(items)
        self.i = 0

    def next(self):
        v = self.items[self.i % len(self.items)]
        self.i += 1
        return v


class Arena:
    def __init__(self, ap, nwords):
        self.ap = ap
        self.n = nwords
        self.off = 0

    def _shape(self, v, shape):
        if len(shape) == 1:
            return v
        if len(shape) == 2:
            return v.rearrange("p (a b) -> p a b", a=shape[0])
        if len(shape) == 3:
            return v.rearrange("p (a b c) -> p a b c", a=shape[0], b=shape[1])
        raise ValueError

    def f32(self, *shape):
        n = int(np.prod(shape))
        v = self.ap[:, self.off:self.off + n]
        self.off += n
        assert self.off <= self.n, (self.off, self.n)
        return self._shape(v, shape)

    def bf16(self, *shape):
        n = int(np.prod(shape))
        w = (n + 1) // 2
        w += w % 2
        v = self.ap[:, self.off:self.off + w].bitcast(BF16)[:, 0:n]
        self.off += w
        assert self.off <= self.n, (self.off, self.n)
        return self._shape(v, shape)


def MM(out, lhsT, rhs, start, stop):
    return lambda e: e.matmul(out, lhsT, rhs, start=start, stop=stop)


def TR(out, in_, ident):
    return lambda e: e.transpose(out, in_, ident)


def ACTF(out, in_, func, scale=None, bias=None):
    kw = {}
    if scale is not None:
        kw['scale'] = scale
    if bias is not None:
        kw['bias'] = bias
    return lambda e: e.activation(out=out, in_=in_, func=func, **kw)


def TT(out, a, b, op):
    return lambda e: e.tensor_tensor(out=out, in0=a, in1=b, op=op)


def TS(out, a, s1, op0, s2=None, op1=None):
    if op1 is None:
        return lambda e: e.tensor_scalar(out=out, in0=a, scalar1=s1, scalar2=None, op0=op0)
    return lambda e: e.tensor_scalar(out=out, in0=a, scalar1=s1, scalar2=s2, op0=op0, op1=op1)


def STT(out, a, s, b, op0, op1):
    return lambda e: e.scalar_tensor_tensor(out=out, in0=a, scalar=s, in1=b, op0=op0, op1=op1)


def CP(out, in_):
    return lambda e: e.tensor_copy(out=out, in_=in_)


def RCP(out, in_):
    return lambda e: e.reciprocal(out=out, in_=in_)


def MSET(ap, c):
    return lambda e: e.memset(ap, c)


def build_program(NT, layers=L, stop_after=None, debug=False):
    geo = geometry(NT)
    LT = NT * T
    NSLOT, NPACK = geo['nslot'], geo['npack']
    nc = bass.Bass("TRN2", target_bir_lowering=False)

    def dt_(name, shape, dtype, kind):
        return nc.dram_tensor(name, list(shape), dtype, kind=kind).ap()

    xin = dt_("xin", [LT, D], F32, "ExternalInput")
    memin = dt_("memin", [2 * NMEM, D], F32, "ExternalInput")
    w_in = dt_("w_in", [L, D, INW], F32, "ExternalInput")
    w_kv = dt_("w_kv", [L, D, 2 * MW], F32, "ExternalInput")
    w_out = dt_("w_out", [L, D, D], F32, "ExternalInput")
    w_up = dt_("w_up", [L, D, 2 * DFF], F32, "ExternalInput")
    w_down = dt_("w_down", [L, DFF, D], F32, "ExternalInput")
    pvec = dt_("pvec", [L, 128, NP], F32, "ExternalInput")
    NGR = -(-(L * NPACK * 8 * NSLOT * GW * 2 * GW) // (2048 * 128)) * 128
    gtab = dt_("gtab", [NGR, 2048], F32, "ExternalInput")
    flagin = dt_("flagin", [128, 1], F32, "ExternalInput")
    identin = dt_("identin", [128, 128], F32, "ExternalInput")
    yout = dt_("yout", [LT, D], F32, "ExternalOutput")

    wb_in = dt_("wb_in", [L, D, INW], BF16, "Internal")
    wb_kv = dt_("wb_kv", [L, D, 2 * MW], BF16, "Internal")
    wb_out = dt_("wb_out", [L, D, D], BF16, "Internal")
    wb_up = dt_("wb_up", [L, D, 2 * DFF], BF16, "Internal")
    wb_down = dt_("wb_down", [L, DFF, D], BF16, "Internal")
    gtb = dt_("gtb", [NGR, 2048], BF16, "Internal")
    IK = "ExternalOutput" if debug else "Internal"
    XT = dt_("XT", [D, LT], F32, IK)
    QT = dt_("QT", [NAW, LT], BF16, IK)
    KT = dt_("KT", [NAW, LT], BF16, IK)
    VV = dt_("VV", [LT, NAW], BF16, IK)
    QMT = dt_("QMT", [MW, LT], BF16, IK)
    CT = dt_("CT", [CC, LT], BF16, IK)
    XM = dt_("XM", [D, LT], F32, IK)
    H2T = dt_("H2T", [D, LT], BF16, IK)
    X1T = dt_("X1T", [D, LT], F32, IK)

    P = Prog()
    fm = lambda ap: ap.rearrange("(c p) t -> p c t", p=128)

    ctx_arena = nc.sbuf_tensor("arena", [128, ARENA_WORDS], F32)
    ctx_psum = nc.psum_tensor("psum", [128, 8, 512], F32)
    arena_t = ctx_arena.__enter__()
    ps = ctx_psum.__enter__()
    A = Arena(arena_t, ARENA_WORDS)

    pv = A.f32(NP)
    ident_f = A.f32(128)
    ones_f = A.f32(128)
    neghalf = A.f32(512)
    flag = A.f32(1)
    epsc = A.f32(1)
    ident_b = A.bf16(128)
    ones_b = A.bf16(128)
    Km = A.bf16(2, 4, NMEM)
    Vm = A.bf16(2, 2, MW)
    h2halo = A.bf16(KC, 2 * NT)
    base_off = A.off

    def col(off, n=1):
        return pv[:, off:off + n]

    P.dma('sp', ident_f, identin, 'c0', writes=['ident_f'])
    P.dma('sp', flag, flagin, 'c1', writes=['flag'])
    P.op('dve', CP(ident_b, ident_f), reads=['ident_f'], writes=['ident_b'])
    P.op('pool', MSET(ones_b, 1.0), writes=['ones_b'])
    P.op('pool', MSET(ones_f, 1.0), writes=['ones_f'])
    P.op('pool', MSET(neghalf, -0.5), writes=['neghalf'])
    P.op('pool', MSET(epsc, EPS), writes=['epsc'])

    cast_jobs = []

    def plan_cast(dst2d, src2d, key, sem):
        rows, cols = src2d.shape
        assert rows % 128 == 0
        for r in range(0, rows, 128):
            cast_jobs.append((dst2d[r:r + 128, :], src2d[r:r + 128, :], sem))
        P.lastw[key] = (sem, 16 * (rows // 128))
        P.nobarrier.add(sem)

    def issue_casts(n):
        for _ in range(min(n, len(cast_jobs))):
            d_, s_, sem = cast_jobs.pop(0)
            P.dma('pool', d_, s_, sem, writes=[], max_dma_last_dim=4096)

    plan_cast(wb_kv[0], w_kv[0], ('wb_kv', 0), 'pkv0')
    plan_cast(wb_in[0], w_in[0], ('wb_in', 0), 'pin0')
    n_first = len(cast_jobs)
    plan_cast(wb_out[0], w_out[0], ('wb_out', 0), 'pout0')
    plan_cast(gtb, gtab, 'gtb', 'pgtb')
    n_second = len(cast_jobs) - n_first
    plan_cast(wb_up[0], w_up[0], ('wb_up', 0), 'pup0')
    plan_cast(wb_down[0], w_down[0], ('wb_down', 0), 'pdn0')
    if layers > 1:
        plan_cast(wb_kv[1], w_kv[1], ('wb_kv', 1), 'pkv1')
        plan_cast(wb_in[1], w_in[1], ('wb_in', 1), 'pin1')
        plan_cast(wb_out[1], w_out[1], ('wb_out', 1), 'pout1')
        plan_cast(wb_up[1], w_up[1], ('wb_up', 1), 'pup1')
        plan_cast(wb_down[1], w_down[1], ('wb_down', 1), 'pdn1')
    n_rest = len(cast_jobs) - n_first - n_second
    issue_casts(n_first)

    def rstd_chain(sumbank, inv_n, msb, rsb, kms, krs):
        P.op('act', ACTF(msb, ps[:, sumbank, :], AF.Sqrt, scale=inv_n, bias=epsc),
             reads=[('ps', sumbank), 'epsc'], writes=[kms])
        P.op('dve', RCP(rsb, msb), reads=[kms], writes=[krs])

    def sweepA(l, mode):
        A.off = base_off
        P.barrier()
        xtok = [A.f32(D), A.f32(D)] if mode != 'x1' else None
        xT = A.f32(KC, T)
        sq = [A.bf16(T), A.bf16(T), A.bf16(T)]
        msb = [A.f32(T), A.f32(T), A.f32(T)]
        rsb = [A.f32(T), A.f32(T), A.f32(T)]
        pfs = [A.f32(T), A.f32(T), A.f32(T)]
        h = A.bf16(KC, T)
        wblk = [A.bf16(KC, 256), A.bf16(KC, 256), A.bf16(KC, 256)]
        qstage = A.bf16(8, T)
        kstage = A.bf16(8, T)
        qmstage = A.bf16(4, T)
        vstage = A.bf16(4, NAW)
        ustage = A.f32(4, T)
        cstage = A.bf16(4, T)
        thb = [A.f32(T), A.f32(T)]
        PB = Rot([0, 1, 2, 3])
        SB = Rot([4, 5])
        VB = Rot([6, 7])
        tag = f"A{l}{mode}"
        ntile = NT if mode != 'mem' else 1
        gcol = P_GM if mode == 'mem' else P_G1
        wsrc = (wb_kv if mode == 'mem' else wb_in)[l]
        wkey = ('wb_kv', l) if mode == 'mem' else ('wb_in', l)
        nblk = (2 * MW if mode == 'mem' else INW) // 256
        wv = wsrc.rearrange("(c p) n -> p c n", p=128)
        wrot = Rot([0, 1, 2])
        slot2 = Rot([0, 1, 2])
        thr = Rot([0, 1])

        for i in range(ntile):
            t0, t1 = i * T, (i + 1) * T
            if mode == 'x1':
                P.dma('sp', xT, fm(X1T)[:, :, t0:t1], 'xT', reads=[('X1T', i, oc_) for oc_ in range(KC)],
                      writes=[('xT', kc) for kc in range(KC)])
            else:
                src = xin if mode == 'x0' else memin
                for s in range(4):
                    sl = s % 2
                    P.dma('sp', xtok[sl], src[t0 + s * 128:t0 + (s + 1) * 128, :], f'xtok{sl}',
                          writes=[('xtok', sl)])
                    for g in range(4):
                        b = VB.next()
                        P.group('pe', [TR(ps[:, b, q * 128:(q + 1) * 128],
                                          xtok[sl][:, (g * 4 + q) * 128:(g * 4 + q + 1) * 128], ident_f)
                                       for q in range(4)],
                                reads=[('xtok', sl), 'ident_f'], writes=[('ps', b)])
                        P.op('dve', CP(xT[:, g * 4:(g + 1) * 4, s * 128:(s + 1) * 128],
                                       ps[:, b, :].rearrange("p (a b) -> p a b", a=4)),
                             reads=[('ps', b)], writes=[('xT', g * 4 + q) for q in range(4)])
                if mode == 'x0':
                    P.dma('pool', fm(XT)[:, :, t0:t1], xT, 'stXT', reads=[('xT', kc) for kc in range(KC)],
                          writes=[('XT', i)])
            sb = SB.next()
            for kc in range(KC):
                sl = kc % 2
                P.op('act', ACTF(sq[sl], xT[:, kc, :], AF.Square), reads=[('xT', kc)], writes=[('sq', sl)])
                P.op('pe', MM(ps[:, sb, :], ones_b, sq[sl], kc == 0, kc == KC - 1),
                     reads=[('sq', sl), 'ones_b'], writes=[('ps', sb)])
            rstd_chain(sb, 1.0 / D, msb[0], rsb[0], ('ms', 0), ('rs', 0))
            for kc in range(KC):
                P.op('dve', STT(h[:, kc, :], xT[:, kc, :], col(gcol + kc), rsb[0], ALU.mult, ALU.mult),
                     reads=[('xT', kc), ('rs', 0), 'pv'], writes=[('h', kc)])
            hkeys = [('h', kc) for kc in range(KC)]

            pending = []

            def flush():
                for pa, pb in pending:
                    pa()
                    pb()
                pending.clear()

            def qk_stage1(bank, dst, gofs):
                s2 = slot2.next()
                prev = list(pending)
                pending.clear()
                for pa, _ in prev:
                    pa()
                P.op('dve', CP(pfs[s2], ps[:, bank, :]), reads=[('ps', bank)], writes=[('pf', s2)])
                P.op('act', ACTF(sq[s2], pfs[s2], AF.Square), reads=[('pf', s2)], writes=[('sq', s2)])
                for _, pb in prev:
                    pb()
                sb2 = SB.next()

                def part_a():
                    P.op('pe', MM(ps[:, sb2, :], ones_b, sq[s2], True, True),
                         reads=[('sq', s2), 'ones_b'], writes=[('ps', sb2)])
                    P.op('act', ACTF(msb[s2], ps[:, sb2, :], AF.Sqrt, scale=1.0 / 128, bias=epsc),
                         reads=[('ps', sb2), 'epsc'], writes=[('ms', s2)])

                def part_b():
                    P.op('dve', RCP(rsb[s2], msb[s2]), reads=[('ms', s2)], writes=[('rs', s2)])
                    a_, b_ = pfs[s2], rsb[s2]
                    if len(dst[0].shape) == 3:
                        a_ = a_.rearrange("p (a b) -> p a b", a=2)
                        b_ = b_.rearrange("p (a b) -> p a b", a=2)
                    P.op('dve', STT(dst[0], a_, col(gofs), b_, ALU.mult, ALU.mult),
                         reads=[('pf', s2), ('rs', s2), 'pv'], writes=[dst[1]])
                pending.append((part_a, part_b))

            for b in range(nblk):
                ws = wrot.next()
                P.dma('sp', wblk[ws], wv[:, :, b * 256:(b + 1) * 256], f'wA{ws}', reads=[wkey],
                      writes=[('wA', ws)])
                col0 = b * 256
                if mode == 'mem':
                    kind = 'km' if col0 < MW else 'vm'
                else:
                    kind = ('q' if col0 < 1024 else 'k' if col0 < 2048 else 'v' if col0 < 3072
                            else 'qm' if col0 < 3584 else 'u' if col0 < 4096 else 'g')
                if kind in ('v', 'vm'):
                    for s in range(4):
                        bank = VB.next()
                        P.group('pe', [MM(ps[:, bank, 0:256], h[:, kc, s * 128:(s + 1) * 128], wblk[ws][:, kc, :],
                                          kc == 0, kc == KC - 1) for kc in range(KC)],
                                reads=hkeys + [('wA', ws)], writes=[('ps', bank)])
                        flush()
                        if kind == 'v':
                            vc = col0 - 2048
                            dst, dkey = vstage[:, s, vc:vc + 256], ('vst', s)
                        else:
                            vc = col0 - MW
                            dst, dkey = Vm[:, s // 2, s % 2, vc:vc + 256], ('Vm', s)
                        eng = 'dve' if (s % 2 == 0) else 'act'
                        fn = CP(dst, ps[:, bank, 0:256]) if eng == 'dve' else ACTF(dst, ps[:, bank, 0:256], AF.Copy)
                        P.op(eng, fn, reads=[('ps', bank)], writes=[dkey])
                    continue
                for o in range(2):
                    oc = (col0 + o * 128) // 128
                    bank = PB.next()
                    P.group('pe', [MM(ps[:, bank, :], wblk[ws][:, kc, o * 128:(o + 1) * 128], h[:, kc, :],
                                      kc == 0, kc == KC - 1) for kc in range(KC)],
                            reads=hkeys + [('wA', ws)], writes=[('ps', bank)])
                    if kind not in ('q', 'k', 'qm', 'km'):
                        flush()
                    if kind == 'q':
                        qk_stage1(bank, (qstage[:, oc, :], ('qst', oc)), P_GQ)
                    elif kind == 'k':
                        qk_stage1(bank, (kstage[:, oc - 8, :], ('kst', oc - 8)), P_GK)
                    elif kind == 'qm':
                        qk_stage1(bank, (qmstage[:, oc - 24, :], ('qmst', oc - 24)), P_GQM)
                    elif kind == 'km':
                        qk_stage1(bank, (Km[:, :, oc, :], ('Km', oc)), P_GKM)
                    elif kind == 'u':
                        j = oc - 28
                        P.op('dve', CP(ustage[:, j, :], ps[:, bank, :]), reads=[('ps', bank)], writes=[('ust', j)])
                    elif kind == 'g':
                        j = oc - 32
                        s2 = thr.next()
                        P.op('act', ACTF(thb[s2], ps[:, bank, :], AF.Tanh, scale=0.5),
                             reads=[('ps', bank)], writes=[('th', s2)])
                        P.op('dve', STT(cstage[:, j, :], thb[s2], 1.0, ustage[:, j, :], ALU.add, ALU.mult),
                             reads=[('th', s2), ('ust', j)], writes=[('cst', j)])
            flush()
            if mode != 'mem':
                P.dma('pool', fm(QT)[:, :, t0:t1], qstage, 'stQ', reads=[('qst', j) for j in range(8)],
                      writes=[('QT', i)])
                P.dma('pool', fm(KT)[:, :, t0:t1], kstage, 'stK', reads=[('kst', j) for j in range(8)],
                      writes=[('KT', i)])
                P.dma('pool', fm(QMT)[:, :, t0:t1], qmstage, 'stQM', reads=[('qmst', j) for j in range(4)],
                      writes=[('QMT', i)])
                P.dma('pool', VV[t0:t1, :].rearrange("(s p) c -> p s c", p=128), vstage, 'stV',
                      reads=[('vst', s) for s in range(4)], writes=[('VV', i)])
                P.dma('pool', fm(CT)[:, :, t0:t1], cstage, 'stC', reads=[('cst', j) for j in range(4)],
                      writes=[('CT', i)])
            if mode == 'x0':
                issue_casts(-(-n_rest // NT))

    def sweepB(l):
        A.off = base_off
        P.barrier()
        qt = A.bf16(8, T)
        qmt = A.bf16(4, T)
        Kw = [A.bf16(1024), A.bf16(1024)]
        Vw = [A.bf16(8, 128), A.bf16(8, 128)]
        Gp = [A.bf16(NSLOT, 128), A.bf16(NSLOT, 128)]
        Pt = [A.bf16(6, 64), A.bf16(6, 64), A.bf16(6, 64)]
        Ptm = [[A.bf16(T), A.bf16(T)], [A.bf16(T), A.bf16(T)]]
        rden = [A.f32(T), A.f32(T)]
        cat = A.bf16(KC, T)
        cwin = A.bf16(4, T + 30)
        dg = [A.bf16(128) for _ in range(8)]
        cv = A.f32(4, T)
        sqf = [A.f32(T), A.f32(T)]
        msb = A.f32(T)
        rsb = A.f32(T)
        thb = [A.f32(T), A.f32(T)]
        xch = [A.f32(T), A.f32(T)]
        xmst = [A.f32(T), A.f32(T)]
        sq = [A.bf16(T), A.bf16(T)]
        h2 = A.bf16(KC, T)
        wblk = [A.bf16(KC, 256), A.bf16(KC, 256), A.bf16(KC, 256)]
        SC = Rot([0, 1, 6])
        OB = Rot([2, 3])
        DBk = Rot([4, 5])
        P2 = Rot([7])
        hrot = Rot([0, 1])
        prot = Rot([0, 1, 2])
        wrot = Rot([0, 1, 2])
        r2 = Rot([0, 1])
        dgr = Rot(range(8))
        wv = wb_out[l].rearrange("(c p) n -> p c n", p=128)
        KTv, QTv, QMTv, CTv = fm(KT), fm(QT), fm(QMT), fm(CT)
        gtv = gtb.rearrange("a b -> (a b)")[0:L * NPACK * 8 * NSLOT * GW * 2 * GW].rearrange(
            "(l k h s q c) -> l k h q s c", l=L, k=NPACK, h=8, s=NSLOT, q=GW)
        Xsrc = XT if l == 0 else X1T
        xkey = 'XT' if l == 0 else 'X1T'
        mid = NT // 2

        for i in range(NT):
            t0, t1 = i * T, (i + 1) * T
            tg = geo['tiles'][i]
            w0, w1 = tg['w0'], tg['w1']
            nwin = (w1 - w0) * GW
            ktiles = sorted(set(range((w0 * GW) // T, ((w1 * GW) - 1) // T + 1)))
            P.dma('sp', qt, QTv[:, :, t0:t1], 'ldq', reads=[('QT', i)], writes=['qt'])
            P.dma('sp', qmt, QMTv[:, :, t0:t1], 'ldqm', reads=[('QMT', i)], writes=['qmt'])
            lo, hi = t0 - 15, t1 + 15
            clo, chi = max(lo, 0), min(hi, LT)
            if clo > lo:
                P.op('pool', MSET(cwin[:, :, 0:15], 0.0), writes=['cwin'])
            if chi < hi:
                P.op('pool', MSET(cwin[:, :, T + 15:T + 30], 0.0), writes=['cwin'])
            P.dma('sp', cwin[:, :, clo - lo:chi - lo], CTv[:, :, clo:chi], 'ldc',
                  reads=[('CT', j) for j in range(max(i - 1, 0), min(i + 2, NT))], writes=['cwin'])
            if i == mid:
                P.op('dve', TS(cwin[:, :, 0:15], cwin[:, :, 0:15], flag[:, 0:1], ALU.mult),
                     reads=['cwin', 'flag'], writes=['cwin'])
            if i == mid - 1:
                P.op('dve', TS(cwin[:, :, T + 15:T + 30], cwin[:, :, T + 15:T + 30], flag[:, 0:1], ALU.mult),
                     reads=['cwin', 'flag'], writes=['cwin'])

            def na_head(hd):
                    hs = hrot.next()
                    P.dma('sp', Kw[hs][:, 0:nwin], KTv[:, hd, w0 * GW:w1 * GW], f'ldk{hs}',
                          reads=[('KT', j) for j in ktiles], writes=[('Kw', hs)])
                    P.dma('sp', Vw[hs][:, 0:(w1 - w0) // 2, :],
                          VV[w0 * GW:w1 * GW, hd * 128:(hd + 1) * 128].rearrange("(c p) d -> p c d", p=128),
                          f'ldv{hs}', reads=[('VV', j) for j in ktiles], writes=[('Vw', hs)])
                    P.dma('sp', Gp[hs][0:GW, :, :], gtv[l, tg['pack'], hd], f'ldg{hs}', reads=['gtb'],
                          writes=[('Gp', hs)])
                    ob, db = OB.next(), DBk.next()
                    def row_score(rl):
                        r = 8 * i + rl
                        start, nch, _ = geo['rows'][r]
                        sc = SC.next()
                        pp = prot.next()
                        mms = []
                        for ci in range(nch):
                            ko = (start + 2 * ci - w0) * GW
                            o_ = ps[:, sc, ci * 64:(ci + 1) * 64]
                            mms.append(MM(o_, Kw[hs][:, ko:ko + 128], qt[:, hd, rl * 64:(rl + 1) * 64], True, False))
                            mms.append(MM(o_, Gp[hs][0:GW, tg['slotmap'][(rl, ci)], :], ident_b[0:GW, 0:GW],
                                          False, True))
                        P.group('pe', mms, reads=[('Kw', hs), ('Gp', hs), 'qt', 'ident_b'], writes=[('ps', sc)])
                        P.op('act', ACTF(Pt[pp][:, 0:nch, :],
                                         ps[:, sc, 0:nch * 64].rearrange("p (a b) -> p a b", a=nch), AF.Exp),
                             reads=[('ps', sc)], writes=[('Pt', pp)])
                        return (rl, start, nch, pp)

                    def row_pv(st):
                        rl, start, nch, pp = st
                        mms = []
                        for ci in range(nch):
                            vi = (start + 2 * ci - w0) // 2
                            mms.append(MM(ps[:, ob, rl * 64:(rl + 1) * 64], Vw[hs][:, vi, :], Pt[pp][:, ci, :],
                                          ci == 0, ci == nch - 1))
                        for ci in range(nch):
                            mms.append(MM(ps[:, db, rl * 64:(rl + 1) * 64], ones_b, Pt[pp][:, ci, :],
                                          ci == 0, ci == nch - 1))
                        P.group('pe', mms, reads=[('Vw', hs), ('Pt', pp), 'ones_b'],
                                writes=[('ps', ob), ('ps', db)])

                    prev = None
                    for rl in range(8):
                        cur = row_score(rl)
                        if prev is not None:
                            row_pv(prev)
                        prev = cur
                    row_pv(prev)
                    rr = r2.next()
                    P.op('dve', RCP(rden[rr], ps[:, db, :]), reads=[('ps', db)], writes=[('rden', rr)])
                    P.op('dve', TT(cat[:, hd, :], ps[:, ob, :], rden[rr], ALU.mult),
                         reads=[('ps', ob), ('rden', rr)], writes=[('cat', hd)])


            def conv_part():
                for j in range(4):
                    cbk = P2.next()
                    for k in range(31):
                        ds = dgr.next()
                        P.op('dve', TS(dg[ds], ident_b, col(P_CW + j * 31 + k), ALU.mult),
                             reads=['ident_b', 'pv'], writes=[('dg', ds)])
                        P.op('pe', MM(ps[:, cbk, :], dg[ds], cwin[:, j, k:k + T], k == 0, k == 30),
                             reads=[('dg', ds), 'cwin'], writes=[('ps', cbk)])
                    P.op('act', ACTF(cv[:, j, :], ps[:, cbk, :], AF.Identity, bias=col(P_CB + j)),
                         reads=[('ps', cbk), 'pv'], writes=[('cv', j)])

            def ln_part1():
                mb = P2.next()
                P.group('pe', [MM(ps[:, mb, :], ones_f, cv[:, j, :], j == 0, j == 3) for j in range(4)],
                        reads=[('cv', j) for j in range(4)] + ['ones_f'], writes=[('ps', mb)])
                for j in range(4):
                    P.op('dve', STT(cv[:, j, :], ps[:, mb, :], -1.0 / CC, cv[:, j, :], ALU.mult, ALU.add),
                         reads=[('ps', mb), ('cv', j)], writes=[('cv', j)])

            def ln_part2():
                vb = P2.next()
                for j in range(4):
                    s2 = j % 2
                    P.op('act', ACTF(sqf[s2], cv[:, j, :], AF.Square), reads=[('cv', j)], writes=[('sqf', s2)])
                    P.op('pe', MM(ps[:, vb, :], ones_f, sqf[s2], j == 0, j == 3),
                         reads=[('sqf', s2), 'ones_f'], writes=[('ps', vb)])
                P.op('act', ACTF(msb, ps[:, vb, :], AF.Sqrt, scale=1.0 / CC, bias=epsc),
                     reads=[('ps', vb), 'epsc'], writes=['msB'])
                P.op('dve', RCP(rsb, msb), reads=['msB'], writes=['rsB'])
                for j in range(4):
                    P.op('dve', TT(cv[:, j, :], cv[:, j, :], rsb, ALU.mult), reads=[('cv', j), 'rsB'],
                         writes=[('cv', j)])
                    P.op('dve', TS(cv[:, j, :], cv[:, j, :], col(P_LNG + j), ALU.mult, col(P_LNB + j), ALU.add),
                         reads=[('cv', j), 'pv'], writes=[('cv', j)])
                    s2 = j % 2
                    P.op('act', ACTF(thb[s2], cv[:, j, :], AF.Tanh), reads=[('cv', j)], writes=[('thB', s2)])
                    P.op('dve', STT(cat[:, 12 + j, :], thb[s2], 1.0, cv[:, j, :], ALU.add, ALU.mult),
                         reads=[('thB', s2), ('cv', j)], writes=[('cat', 12 + j)])

            conv_part()
            na_head(0)
            na_head(1)
            ln_part1()
            na_head(2)
            na_head(3)
            ln_part2()
            for hd in range(4, 8):
                na_head(hd)

            mi = 0 if i < mid else 1

            def mem_score(hm):
                par = hm % 2
                for ch in range(2):
                    sc = SC.next()
                    P.group('pe', [MM(ps[:, sc, :], Km[:, mi, hm, ch * 128:(ch + 1) * 128], qmt[:, hm, :], True, True)],
                            reads=[('Km', hm), 'qmt'], writes=[('ps', sc)])
                    P.op('act', ACTF(Ptm[par][ch], ps[:, sc, :], AF.Exp), reads=[('ps', sc)],
                         writes=[('Ptm', par, ch)])

            def mem_pv(hm):
                par = hm % 2
                ob, db = OB.next(), DBk.next()
                mms = [MM(ps[:, ob, :], Vm[:, mi, ch, hm * 128:(hm + 1) * 128], Ptm[par][ch], ch == 0, ch == 1)
                       for ch in range(2)]
                mms += [MM(ps[:, db, :], ones_b, Ptm[par][ch], ch == 0, ch == 1) for ch in range(2)]
                P.group('pe', mms, reads=[('Vm', s_) for s_ in range(4)] + [('Ptm', par, 0), ('Ptm', par, 1), 'ones_b'],
                        writes=[('ps', ob), ('ps', db)])
                rr = r2.next()
                P.op('dve', RCP(rden[rr], ps[:, db, :]), reads=[('ps', db)], writes=[('rden', rr)])
                P.op('dve', TT(cat[:, 8 + hm, :], ps[:, ob, :], rden[rr], ALU.mult),
                     reads=[('ps', ob), ('rden', rr)], writes=[('cat', 8 + hm)])

            for hm in range(4):
                mem_score(hm)
                if hm > 0:
                    mem_pv(hm - 1)
            mem_pv(3)

            catkeys = [('cat', k) for k in range(KC)]
            sb = P2.next()
            pend_ss = []

            def flush_ss():
                while pend_ss:
                    oc_, xs_ = pend_ss.pop(0)
                    P.op('pe', MM(ps[:, sb, :], ones_b, sq[xs_], oc_ == 0, oc_ == KC - 1),
                         reads=[('sqB', xs_), 'ones_b'], writes=[('ps', sb)])
            for b in range(8):
                ws = wrot.next()
                P.dma('sp', wblk[ws], wv[:, :, b * 256:(b + 1) * 256], f'wB{ws}', reads=[('wb_out', l)],
                      writes=[('wB', ws)])
                for o in range(2):
                    oc = b * 2 + o
                    xs = oc % 2
                    P.dma('sp', xch[xs], fm(Xsrc)[:, oc, t0:t1], f'ldx{xs}', reads=[('XT', i) if l == 0 else ('X1T', i, oc)], writes=[('xch', xs)])
                    bank = SC.next()
                    P.group('pe', [MM(ps[:, bank, :], wblk[ws][:, kc, o * 128:(o + 1) * 128], cat[:, kc, :],
                                      kc == 0, kc == KC - 1) for kc in range(KC)],
                            reads=catkeys + [('wB', ws)], writes=[('ps', bank)])
                    flush_ss()
                    P.op('dve', TT(xmst[xs], ps[:, bank, :], xch[xs], ALU.add),
                         reads=[('ps', bank), ('xch', xs)], writes=[('xmst', xs)])
                    P.dma('pool', fm(XM)[:, oc, t0:t1], xmst[xs], f'stxm{xs}', reads=[('xmst', xs)], writes=[('XM', i, oc)])
                    P.op('act', ACTF(sq[xs], xmst[xs], AF.Square), reads=[('xmst', xs)], writes=[('sqB', xs)])
                    pend_ss.append((oc, xs))
                    P.op('pool', TS(h2[:, oc, :], xmst[xs], col(P_G2 + oc), ALU.mult, 0.0, ALU.add),
                         reads=[('xmst', xs), 'pv'], writes=[('h2', oc)])
            flush_ss()
            P.op('act', ACTF(msb, ps[:, sb, :], AF.Sqrt, scale=1.0 / D, bias=epsc),
                 reads=[('ps', sb), 'epsc'], writes=['msB'])
            P.op('dve', RCP(rsb, msb), reads=['msB'], writes=['rsB'])
            for oc in range(KC):
                P.op('dve', TT(h2[:, oc, :], h2[:, oc, :], rsb, ALU.mult), reads=[('h2', oc), 'rsB'],
                     writes=[('h2', oc)])
            h2keys = [('h2', k) for k in range(KC)]
            P.op('dve', CP(h2halo[:, :, 2 * i:2 * i + 1], h2[:, :, 0:1]), reads=h2keys, writes=[('h2halo', 2 * i)])
            P.op('dve', CP(h2halo[:, :, 2 * i + 1:2 * i + 2], h2[:, :, T - 1:T]), reads=h2keys,
                 writes=[('h2halo', 2 * i + 1)])
            P.dma('pool', fm(H2T)[:, :, t0:t1], h2, 'sth2', reads=h2keys, writes=[('H2T', i)])

    def sweepC(l):
        A.off = base_off
        P.barrier()
        h2 = A.bf16(KC, T)
        hid = A.bf16(HC, T)
        wu = [(A.bf16(KC, 256), A.bf16(KC, 256)) for _ in range(2)]
        wd = [A.bf16(HC, 128) for _ in range(3)]
        ag = [A.f32(T), A.f32(T)]
        av = [A.f32(T), A.f32(T)]
        th = [A.f32(T), A.f32(T)]
        xmc = [A.f32(T), A.f32(T)]
        xo = [A.f32(T), A.f32(T)]
        uph = A.bf16(88, 2 * NT)
        edge = A.f32(88, 2)
        GBk = Rot([0, 1])
        VBk = Rot([2, 3])
        OP = Rot([4, 5])
        HB = Rot([6, 7])
        wurot = Rot([0, 1])
        wdrot = Rot([0, 1, 2])
        r2 = Rot([0, 1])
        wuv = wb_up[l].rearrange("(c p) n -> p c n", p=128)
        wdv = wb_down[l].rearrange("(c p) n -> p c n", p=128)
        mid = NT // 2
        Xdst = X1T
        hkeys = [('h2c', k) for k in range(KC)]
        halokeys = [('h2halo', k) for k in range(2 * NT)]
        fwc = lambda c, k: col(P_FW + c * 3 + k)

        for b in range(2 * DFF // 256):
            ws = wurot.next()
            P.dma('sp', wu[ws][0], wuv[:, :, b * 256:(b + 1) * 256], f'wu{ws}', reads=[('wb_up', l)],
                  writes=[('wu', ws)])
            for o in range(2):
                c = b * 2 + o
                hb = HB.next()
                P.group('pe', [MM(ps[:, hb, 0:2 * NT], wu[ws][0][:, kc, o * 128:(o + 1) * 128], h2halo[:, kc, :],
                                  kc == 0, kc == KC - 1) for kc in range(KC)],
                        reads=halokeys + [('wu', ws)], writes=[('ps', hb)])
                P.op('act', ACTF(uph[:, c, :], ps[:, hb, 0:2 * NT], AF.Copy), reads=[('ps', hb)], writes=['uph'])
        P.op('dve', TS(uph[:, :, 2 * mid - 1:2 * mid + 1], uph[:, :, 2 * mid - 1:2 * mid + 1], flag[:, 0:1], ALU.mult),
             reads=['uph', 'flag'], writes=['uph'])

        for i in range(NT):
            t0, t1 = i * T, (i + 1) * T
            P.dma('sp', h2, fm(H2T)[:, :, t0:t1], 'ldh2', reads=[('H2T', i)], writes=hkeys)
            fw3 = pv[:, P_FW:P_FW + 264].rearrange("p (c k) -> p c k", k=3)
            if i > 0:
                P.op('dve', TT(edge[:, :, 0:1], uph[:, :, 2 * i - 1:2 * i], fw3[:, :, 0:1], ALU.mult),
                     reads=['uph', 'pv'], writes=['edge'])
            else:
                P.op('dve', MSET(edge[:, :, 0:1], 0.0), writes=['edge'])
            if i < NT - 1:
                P.op('dve', TT(edge[:, :, 1:2], uph[:, :, 2 * i + 2:2 * i + 3], fw3[:, :, 2:3], ALU.mult),
                     reads=['uph', 'pv'], writes=['edge'])
            else:
                P.op('dve', MSET(edge[:, :, 1:2], 0.0), writes=['edge'])

            for jb in range(HC // 2):
                ws = wurot.next()
                P.dma('sp', wu[ws][0], wuv[:, :, jb * 256:(jb + 1) * 256], f'wu{ws}', reads=[('wb_up', l)],
                      writes=[('wu', ws)])
                P.dma('sp', wu[ws][1], wuv[:, :, DFF + jb * 256:DFF + (jb + 1) * 256], f'wu{ws}',
                      reads=[('wb_up', l)], writes=[('wu', ws)])
                for o in range(2):
                    j = jb * 2 + o
                    gb, vb = GBk.next(), VBk.next()
                    P.group('pe', [MM(ps[:, gb, :], wu[ws][0][:, kc, o * 128:(o + 1) * 128], h2[:, kc, :],
                                      kc == 0, kc == KC - 1) for kc in range(KC)],
                            reads=hkeys + [('wu', ws)], writes=[('ps', gb)])
                    P.group('pe', [MM(ps[:, vb, :], wu[ws][1][:, kc, o * 128:(o + 1) * 128], h2[:, kc, :],
                                      kc == 0, kc == KC - 1) for kc in range(KC)],
                            reads=hkeys + [('wu', ws)], writes=[('ps', vb)])
                    s2 = r2.next()
                    for (bank, dst, c, key) in ((gb, ag[s2], j, 'ag'), (vb, av[s2], HC + j, 'av')):
                        P.op('act', ACTF(dst, ps[:, bank, :], AF.Identity, scale=fwc(c, 1), bias=col(P_FB + c)),
                             reads=[('ps', bank), 'pv'], writes=[(key, s2)])
                        P.op('dve', STT(dst[:, 1:T], ps[:, bank, 0:T - 1], fwc(c, 0), dst[:, 1:T], ALU.mult, ALU.add),
                             reads=[('ps', bank), 'pv', (key, s2)], writes=[(key, s2)])
                        P.op('dve', STT(dst[:, 0:T - 1], ps[:, bank, 1:T], fwc(c, 2), dst[:, 0:T - 1], ALU.mult, ALU.add),
                             reads=[('ps', bank), 'pv', (key, s2)], writes=[(key, s2)])
                        P.op('pool', TT(dst[:, 0:1], dst[:, 0:1], edge[:, c, 0:1], ALU.add),
                             reads=[(key, s2), 'edge'], writes=[(key, s2)])
                        P.op('pool', TT(dst[:, T - 1:T], dst[:, T - 1:T], edge[:, c, 1:2], ALU.add),
                             reads=[(key, s2), 'edge'], writes=[(key, s2)])
                    P.op('act', ACTF(th[s2], ag[s2], AF.Tanh, scale=0.5), reads=[('ag', s2)], writes=[('thC', s2)])
                    P.op('dve', STT(th[s2], th[s2], 1.0, ag[s2], ALU.add, ALU.mult),
                         reads=[('thC', s2), ('ag', s2)], writes=[('thC', s2)])
                    P.op('pool', TT(hid[:, j, :], th[s2], av[s2], ALU.mult),
                         reads=[('thC', s2), ('av', s2)], writes=[('hid', j)])
            hidkeys = [('hid', j) for j in range(HC)]
            for oc in range(KC):
                ws = wdrot.next()
                P.dma('sp', wd[ws], wdv[:, :, oc * 128:(oc + 1) * 128], f'wd{ws}', reads=[('wb_down', l)],
                      writes=[('wd', ws)])
                xs = oc % 2
                P.dma('sp', xmc[xs], fm(XM)[:, oc, t0:t1], f'ldxm{xs}', reads=[('XM', i, oc)], writes=[('xmc', xs)])
                bank = OP.next()
                P.group('pe', [MM(ps[:, bank, :], wd[ws][:, k, :], hid[:, k, :], k == 0, k == HC - 1)
                               for k in range(HC)],
                        reads=hidkeys + [('wd', ws)], writes=[('ps', bank)])
                P.op('dve', TT(xo[xs], ps[:, bank, :], xmc[xs], ALU.add), reads=[('ps', bank), ('xmc', xs)],
                     writes=[('xo', xs)])
                P.dma('pool', fm(Xdst)[:, oc, t0:t1], xo[xs], f'stxo{xs}', reads=[('xo', xs)], writes=[('X1T', i, oc)])

    def sweepD():
        A.off = base_off
        P.barrier()
        xT = [A.f32(KC, T), A.f32(KC, T)]
        yt = [A.f32(D), A.f32(D)]
        TB = Rot(range(8))
        for i in range(NT):
            t0, t1 = i * T, (i + 1) * T
            xs = i % 2
            P.dma('sp', xT[xs], fm(X1T)[:, :, t0:t1], f'ldD{xs}', reads=[('X1T', i, oc_) for oc_ in range(KC)], writes=[('xTD', xs)])
            for s in range(4):
                ys = s % 2
                for g in range(4):
                    b = TB.next()
                    P.group('pe', [TR(ps[:, b, q * 128:(q + 1) * 128], xT[xs][:, g * 4 + q, s * 128:(s + 1) * 128], ident_f)
                                   for q in range(4)], reads=[('xTD', xs), 'ident_f'], writes=[('ps', b)])
                    eng = 'dve' if g % 2 == 0 else 'act'
                    dst = yt[ys][:, g * 512:(g + 1) * 512]
                    fn = CP(dst, ps[:, b, :]) if eng == 'dve' else ACTF(dst, ps[:, b, :], AF.Copy)
                    P.op(eng, fn, reads=[('ps', b)], writes=[('yt', ys)])
                P.dma('sp', yout[t0 + s * 128:t0 + (s + 1) * 128, :], yt[ys], f'stD{ys}', reads=[('yt', ys)],
                      writes=[('yout', i, s)])

    for l in range(layers):
        P.dma('sp', pv, pvec[l], 'pv', writes=['pv'])
        sc = 128.0 ** -0.5
        for (off, n, c) in ((P_GQ, 1, sc), (P_GQM, 1, sc), (P_CW, 124, 0.5), (P_LNG, 8, 0.5),
                            (P_FW + HC * 3, HC * 3, 0.5), (P_FB + HC, HC, 0.5)):
            P.op('dve', TS(col(off, n), col(off, n), c, ALU.mult), reads=['pv'], writes=['pv'])
        if stop_after == ('P', l):
            break
        sweepA(l, 'mem')
        if l == 0:
            issue_casts(n_second)
        if stop_after == ('M', l):
            break
        sweepA(l, 'x0' if l == 0 else 'x1')
        if stop_after == ('A', l):
            break
        issue_casts(len(cast_jobs))
        sweepB(l)
        if stop_after == ('B', l):
            break
        sweepC(l)
    if stop_after is None:
        sweepD()
        P.wait_all('sp', [('yout', i, s) for i in range(NT) for s in range(4)])
    else:
        P.wait_all('sp', list(P.lastw.keys()))

    semnames = ENGS + sorted(P.dmasems)
    ctxs = [nc.semaphore(f"s_{n}") for n in semnames]
    sems = {n: c.__enter__() for n, c in zip(semnames, ctxs)}
    with nc.Block() as block:
        for e in ENGS:
            ops = P.ops[e]

            def body(eng, ops=ops, e=e):
                for o in ops:
                    if o[0] == 'wait':
                        eng.wait_ge(sems[o[1]], o[2])
                    elif o[0] == 'op':
                        ins = o[1](eng)
                        if o[2]:
                            ins.then_inc(sems[e], 1)
                    else:
                        eng.dma_start(out=o[1], in_=o[2], **o[4]).then_inc(sems[o[3]], 16)
            getattr(block, ENGATTR[e])(body)
    for c in reversed(ctxs):
        c.__exit__(None, None, None)
    ctx_psum.__exit__(None, None, None)
    ctx_arena.__exit__(None, None, None)
    stats = {e: len(P.ops[e]) for e in ENGS}
    stats['nops'] = P.nops
    return nc, geo, stats


def core_inputs(xseq, mem2, ty, shared, NT, geo, inp):
    m = dict(shared)
    m['xin'] = np.ascontiguousarray(xseq, dtype=np.float32)
    m['memin'] = np.ascontiguousarray(mem2.reshape(2 * NMEM, D), dtype=np.float32)
    g = build_gtab(inp['na_rpb'], ty, NT, geo)
    flat = g.reshape(-1)
    ngr = -(-flat.size // (2048 * 128)) * 128
    buf = np.zeros((ngr * 2048,), np.float32)
    buf[:flat.size] = flat
    m['gtab'] = buf.reshape(ngr, 2048)
    m['flagin'] = np.full((128, 1), 1.0 if ty == 'A' else 0.0, np.float32)
    return m


def shared_inputs(inp):
    return {
        'w_in': np.ascontiguousarray(inp['w_in'], dtype=np.float32),
        'w_kv': np.ascontiguousarray(inp['w_mem_kv'], dtype=np.float32),
        'w_out': np.ascontiguousarray(inp['w_out'], dtype=np.float32),
        'w_up': np.ascontiguousarray(inp['w_up'], dtype=np.float32),
        'w_down': np.ascontiguousarray(inp['w_down'], dtype=np.float32),
        'pvec': build_pvec(inp),
        'identin': np.eye(128, dtype=np.float32),
    }


_CACHE = {}


def kernel(**inputs):
    inp = {k: np.asarray(v) for k, v in inputs.items()}
    NT = 16
    if NT not in _CACHE:
        _CACHE[NT] = build_program(NT)
    nc, geo, _ = _CACHE[NT]
    shared = shared_inputs(inp)
    xp, xs, mp, ms = inp['x_prompt'], inp['x_sample'], inp['mem_prompt'], inp['mem_sample']
    maps = []
    for b in range(2):
        maps.append(core_inputs(xs[b], np.stack([ms[b], ms[b]]), 'A', shared, NT, geo, inp))
    for k in range(4):
        maps.append(core_inputs(np.concatenate([xp[2 * k], xp[2 * k + 1]], axis=0),
                                np.stack([mp[2 * k], mp[2 * k + 1]]), 'B', shared, NT, geo, inp))
    for k in range(2):
        maps.append(core_inputs(np.zeros((NT * T, D), np.float32), np.zeros((2, NMEM, D), np.float32), 'B',
                                shared, NT, geo, inp))
    res = run_bass_kernel_spmd(nc, maps, core_ids=list(range(8)))
    outs = [np.asarray(r['yout']) for r in res.results]
    y_sample = np.stack([outs[0], outs[1]]).astype(np.float32)
    y_prompt = np.stack([outs[2 + k // 2][(k % 2) * 4096:(k % 2 + 1) * 4096] for k in range(8)]).astype(np.float32)
    return (y_prompt, y_sample)
```

```python
import math
from collections import defaultdict
import numpy as np
import concourse.bass as bass
import concourse.mybir as mybir
from concourse.bass_utils import run_bass_kernel_spmd

F32 = mybir.dt.float32
BF16 = mybir.dt.bfloat16
AF = mybir.ActivationFunctionType
ALU = mybir.AluOpType

D = 2048
KC = 16
T = 512
NAW = 1024
MW = 512
CC = 512
DFF = 5632
HC = 44
INW = 4608
NMEM = 256
EPS = 1e-6
GW = 64
NEG = -30000.0
L = 2
NP = 540
P_G1, P_GM, P_G2, P_GQ, P_GK, P_GQM, P_GKM, P_CB, P_LNG, P_LNB, P_CW, P_FW, P_FB = \
    0, 16, 32, 48, 49, 50, 51, 52, 56, 60, 64, 188, 452
ARENA_WORDS = 43520


def _tw(r, ty, R):
    if ty == 'A':
        return int(np.clip(r - 4, 0, R - 8))
    half = R // 2
    base = 0 if r < half else half
    return base + int(np.clip(r - base - 4, 0, half - 8))


def geometry(NT):
    R = 8 * NT
    rows = []
    for r in range(R):
        a, b = _tw(r, 'A', R), _tw(r, 'B', R)
        lo, hi = min(a, b), max(a, b) + 8
        start = lo - (lo % 2)
        nch = (hi - start + 1) // 2
        assert start + 2 * nch <= R
        rows.append((start, nch, a == b))
    tiles = []
    packs = {}
    for i in range(NT):
        keys = []
        slotmap = {}
        for rl in range(8):
            r = 8 * i + rl
            start, nch, same = rows[r]
            rs = _tw(r, 'A', R)
            for ci in range(nch):
                kr = start + 2 * ci
                if same:
                    key = ('g', kr - r + 7, rs <= kr < rs + 8, rs <= kr + 1 < rs + 8)
                else:
                    key = ('s', rl, ci)
                if key not in keys:
                    keys.append(key)
                slotmap[(rl, ci)] = keys.index(key)
        sig = tuple(keys)
        if sig not in packs:
            packs[sig] = (len(packs), i)
        w0 = min(rows[8 * i + rl][0] for rl in range(8))
        w1 = max(rows[8 * i + rl][0] + 2 * rows[8 * i + rl][1] for rl in range(8))
        assert (w1 - w0) <= 16 and w0 % 2 == 0
        tiles.append(dict(keys=keys, slotmap=slotmap, pack=packs[sig][0], w0=w0, w1=w1))
    nslot = max(len(t['keys']) for t in tiles)
    packlist = sorted(packs.values())
    return dict(R=R, rows=rows, tiles=tiles, nslot=nslot, npack=len(packlist),
                pack_rep=[p[1] for p in packlist])


def build_gtab(rpb, ty, NT, geo):
    R = geo['R']
    qc = np.arange(GW)[:, None]
    kc = np.arange(GW)[None, :]
    cs = np.clip(qc - 8, 0, GW - 16)
    valid = (kc >= cs) & (kc < cs + 16)
    dc = np.clip(kc - qc + 15, 0, 30)
    Tf = np.where(valid[None, None, None], rpb[:, :, :, dc], np.float32(NEG)).astype(np.float32)
    negt = np.full((rpb.shape[0], 8, GW, GW), NEG, np.float32)
    out = np.full((rpb.shape[0], geo['npack'], 8, geo['nslot'], GW, 2 * GW), NEG, np.float32)
    for p, i in enumerate(geo['pack_rep']):
        t = geo['tiles'][i]
        for k, key in enumerate(t['keys']):
            halves = []
            if key[0] == 'g':
                _, dA, vA, vB = key
                for d, v in ((dA, vA), (dA + 1, vB)):
                    halves.append(Tf[:, :, d] if (v and 0 <= d <= 14) else negt)
            else:
                _, rl, ci = key
                r = 8 * i + rl
                start = geo['rows'][r][0]
                rs = _tw(r, ty, R)
                for hf in range(2):
                    kr = start + 2 * ci + hf
                    d = kr - r + 7
                    v = (rs <= kr < rs + 8)
                    halves.append(Tf[:, :, d] if (v and 0 <= d <= 14) else negt)
            out[:, p, :, k, :, 0:GW] = halves[0]
            out[:, p, :, k, :, GW:] = halves[1]
    return out


def build_pvec(inp):
    pv = np.zeros((L, 128, NP), np.float32)
    fm = lambda a, n: a.reshape(L, n, 128).transpose(0, 2, 1)
    pv[:, :, P_G1:P_G1 + 16] = fm(inp['norm_mix_g'], 16)
    pv[:, :, P_GM:P_GM + 16] = fm(inp['mem_norm_g'], 16)
    pv[:, :, P_G2:P_G2 + 16] = fm(inp['norm_ffn_g'], 16)
    pv[:, :, P_GQ] = inp['na_q_norm_g']
    pv[:, :, P_GK] = inp['na_k_norm_g']
    pv[:, :, P_GQM] = inp['mem_q_norm_g']
    pv[:, :, P_GKM] = inp['mem_k_norm_g']
    pv[:, :, P_CB:P_CB + 4] = fm(inp['conv_dw_b'], 4)
    pv[:, :, P_LNG:P_LNG + 4] = fm(inp['conv_ln_g'], 4)
    pv[:, :, P_LNB:P_LNB + 4] = fm(inp['conv_ln_b'], 4)
    cw = inp['conv_dw_w'].reshape(L, 31, 4, 128).transpose(0, 3, 2, 1)
    pv[:, :, P_CW:P_CW + 124] = cw.reshape(L, 128, 124)
    fw = inp['ffn_dw_w'].reshape(L, 3, 88, 128).transpose(0, 3, 2, 1)
    pv[:, :, P_FW:P_FW + 264] = fw.reshape(L, 128, 264)
    pv[:, :, P_FB:P_FB + 88] = fm(inp['ffn_dw_b'], 88)
    return pv


ENGS = ['pe', 'act', 'dve', 'pool', 'sp']
ENGATTR = {'pe': 'tensor', 'act': 'scalar', 'dve': 'vector', 'pool': 'gpsimd', 'sp': 'sync'}


class Prog:
    def __init__(self):
        self.cnt = defaultdict(int)
        self.ops = {e: [] for e in ENGS}
        self.seen = {e: defaultdict(int) for e in ENGS}
        self.lastw = {}
        self.readers = defaultdict(dict)
        self.dmasems = set()
        self.nobarrier = set()
        import os
        self.limit = int(os.environ.get('KLIMIT', '0')) or None
        self.nops = 0

    def _skip(self):
        self.nops += 1
        return self.limit is not None and self.nops > self.limit

    def _deps(self, reads, writes):
        need = {}

        def add(ev):
            s, v = ev
            if need.get(s, 0) < v:
                need[s] = v
        for k in reads:
            if k in self.lastw:
                add(self.lastw[k])
        for k in writes:
            if k in self.lastw:
                add(self.lastw[k])
            for s, v in self.readers[k].items():
                add((s, v))
        return need

    def _commit(self, ev, reads, writes):
        s, v = ev
        for k in reads:
            rd = self.readers[k]
            if rd.get(s, 0) < v:
                rd[s] = v
        for k in writes:
            self.lastw[k] = ev
            self.readers[k] = {}

    def _waits(self, eng, need):
        for s, v in need.items():
            if eng == 'pe' and s == 'pe':
                continue
            if self.seen[eng][s] >= v:
                continue
            self.seen[eng][s] = v
            self.ops[eng].append(('wait', s, v))

    def op(self, eng, fn, reads=(), writes=()):
        if self._skip():
            return None
        self._waits(eng, self._deps(reads, writes))
        self.cnt[eng] += 1
        ev = (eng, self.cnt[eng])
        self.ops[eng].append(('op', fn, True))
        self._commit(ev, reads, writes)
        return ev

    def group(self, eng, fns, reads=(), writes=()):
        if self._skip():
            return None
        self._waits(eng, self._deps(reads, writes))
        for f in fns[:-1]:
            self.ops[eng].append(('op', f, False))
        self.cnt[eng] += 1
        ev = (eng, self.cnt[eng])
        self.ops[eng].append(('op', fns[-1], True))
        self._commit(ev, reads, writes)
        return ev

    def dma(self, q, out, in_, sem, reads=(), writes=(), **kw):
        if self._skip():
            return None
        self._waits(q, self._deps(reads, writes))
        self.dmasems.add(sem)
        self.cnt[sem] += 16
        ev = (sem, self.cnt[sem])
        self.ops[q].append(('dma', out, in_, sem, kw))
        self._commit(ev, reads, writes)
        return ev

    def barrier(self):
        if self.limit is not None and self.nops > self.limit:
            return
        for e in ENGS:
            self._waits(e, {s_: c for s_, c in self.cnt.items() if c > 0 and s_ not in self.nobarrier})

    def wait_all(self, eng, keys):
        self._waits(eng, self._deps(keys, ()))


class Rot:
    def __init__(self, items):
        self.items = list(items)
        self.i = 0

    def next(self):
        v = self.items[self.i % len(self.items)]
        self.i += 1
        return v


class Arena:
    def __init__(self, ap, nwords):
        self.ap = ap
        self.n = nwords
        self.off = 0

    def _shape(self, v, shape):
        if len(shape) == 1:
            return v
        if len(shape) == 2:
            return v.rearrange("p (a b) -> p a b", a=shape[0])
        if len(shape) == 3:
            return v.rearrange("p (a b c) -> p a b c", a=shape[0], b=shape[1])
        raise ValueError

    def f32(self, *shape):
        n = int(np.prod(shape))
        v = self.ap[:, self.off:self.off + n]
        self.off += n
        assert self.off <= self.n, (self.off, self.n)
        return self._shape(v, shape)

    def bf16(self, *shape):
        n = int(np.prod(shape))
        w = (n + 1) // 2
        w += w % 2
        v = self.ap[:, self.off:self.off + w].bitcast(BF16)[:, 0:n]
        self.off += w
        assert self.off <= self.n, (self.off, self.n)
        return self._shape(v, shape)


def MM(out, lhsT, rhs, start, stop):
    return lambda e: e.matmul(out, lhsT, rhs, start=start, stop=stop)


def TR(out, in_, ident):
    return lambda e: e.transpose(out, in_, ident)


def ACTF(out, in_, func, scale=None, bias=None):
    kw = {}
    if scale is not None:
        kw['scale'] = scale
    if bias is not None:
        kw['bias'] = bias
    return lambda e: e.activation(out=out, in_=in_, func=func, **kw)


def TT(out, a, b, op):
    return lambda e: e.tensor_tensor(out=out, in0=a, in1=b, op=op)


def TS(out, a, s1, op0, s2=None, op1=None):
    if op1 is None:
        return lambda e: e.tensor_scalar(out=out, in0=a, scalar1=s1, scalar2=None, op0=op0)
    return lambda e: e.tensor_scalar(out=out, in0=a, scalar1=s1, scalar2=s2, op0=op0, op1=op1)


def STT(out, a, s, b, op0, op1):
    return lambda e: e.scalar_tensor_tensor(out=out, in0=a, scalar=s, in1=b, op0=op0, op1=op1)


def CP(out, in_):
    return lambda e: e.tensor_copy(out=out, in_=in_)


def RCP(out, in_):
    return lambda e: e.reciprocal(out=out, in_=in_)


def MSET(ap, c):
    return lambda e: e.memset(ap, c)


def build_program(NT, layers=L, stop_after=None, debug=False):
    geo = geometry(NT)
    LT = NT * T
    NSLOT, NPACK = geo['nslot'], geo['npack']
    nc = bass.Bass("TRN2", target_bir_lowering=False)

    def dt_(name, shape, dtype, kind):
        return nc.dram_tensor(name, list(shape), dtype, kind=kind).ap()

    xin = dt_("xin", [LT, D], F32, "ExternalInput")
    memin = dt_("memin", [2 * NMEM, D], F32, "ExternalInput")
    w_in = dt_("w_in", [L, D, INW], F32, "ExternalInput")
    w_kv = dt_("w_kv", [L, D, 2 * MW], F32, "ExternalInput")
    w_out = dt_("w_out", [L, D, D], F32, "ExternalInput")
    w_up = dt_("w_up", [L, D, 2 * DFF], F32, "ExternalInput")
    w_down = dt_("w_down", [L, DFF, D], F32, "ExternalInput")
    pvec = dt_("pvec", [L, 128, NP], F32, "ExternalInput")
    NGR = -(-(L * NPACK * 8 * NSLOT * GW * 2 * GW) // (2048 * 128)) * 128
    gtab = dt_("gtab", [NGR, 2048], F32, "ExternalInput")
    flagin = dt_("flagin", [128, 1], F32, "ExternalInput")
    identin = dt_("identin", [128, 128], F32, "ExternalInput")
    yout = dt_("yout", [LT, D], F32, "ExternalOutput")

    wb_in = dt_("wb_in", [L, D, INW], BF16, "Internal")
    wb_kv = dt_("wb_kv", [L, D, 2 * MW], BF16, "Internal")
    wb_out = dt_("wb_out", [L, D, D], BF16, "Internal")
    wb_up = dt_("wb_up", [L, D, 2 * DFF], BF16, "Internal")
    wb_down = dt_("wb_down", [L, DFF, D], BF16, "Internal")
    gtb = dt_("gtb", [NGR, 2048], BF16, "Internal")
    IK = "ExternalOutput" if debug else "Internal"
    XT = dt_("XT", [D, LT], F32, IK)
    QT = dt_("QT", [NAW, LT], BF16, IK)
    KT = dt_("KT", [NAW, LT], BF16, IK)
    VV = dt_("VV", [LT, NAW], BF16, IK)
    QMT = dt_("QMT", [MW, LT], BF16, IK)
    CT = dt_("CT", [CC, LT], BF16, IK)
    XM = dt_("XM", [D, LT], F32, IK)
    H2T = dt_("H2T", [D, LT], BF16, IK)
    X1T = dt_("X1T", [D, LT], F32, IK)

    P = Prog()
    fm = lambda ap: ap.rearrange("(c p) t -> p c t", p=128)

    ctx_arena = nc.sbuf_tensor("arena", [128, ARENA_WORDS], F32)
    ctx_psum = nc.psum_tensor("psum", [128, 8, 512], F32)
    arena_t = ctx_arena.__enter__()
    ps = ctx_psum.__enter__()
    A = Arena(arena_t, ARENA_WORDS)

    pv = A.f32(NP)
    ident_f = A.f32(128)
    ones_f = A.f32(128)
    neghalf = A.f32(512)
    flag = A.f32(1)
    epsc = A.f32(1)
    ident_b = A.bf16(128)
    ones_b = A.bf16(128)
    Km = A.bf16(2, 4, NMEM)
    Vm = A.bf16(2, 2, MW)
    h2halo = A.bf16(KC, 2 * NT)
    base_off = A.off

    def col(off, n=1):
        return pv[:, off:off + n]

    P.dma('sp', ident_f, identin, 'c0', writes=['ident_f'])
    P.dma('sp', flag, flagin, 'c1', writes=['flag'])
    P.op('dve', CP(ident_b, ident_f), reads=['ident_f'], writes=['ident_b'])
    P.op('pool', MSET(ones_b, 1.0), writes=['ones_b'])
    P.op('pool', MSET(ones_f, 1.0), writes=['ones_f'])
    P.op('pool', MSET(neghalf, -0.5), writes=['neghalf'])
    P.op('pool', MSET(epsc, EPS), writes=['epsc'])

    cast_jobs = []

    def plan_cast(dst2d, src2d, key, sem):
        rows, cols = src2d.shape
        assert rows % 128 == 0
        for r in range(0, rows, 128):
            cast_jobs.append((dst2d[r:r + 128, :], src2d[r:r + 128, :], sem))
        P.lastw[key] = (sem, 16 * (rows // 128))
        P.nobarrier.add(sem)

    def issue_casts(n):
        for _ in range(min(n, len(cast_jobs))):
            d_, s_, sem = cast_jobs.pop(0)
            P.dma('pool', d_, s_, sem, writes=[], max_dma_last_dim=4096)

    plan_cast(wb_kv[0], w_kv[0], ('wb_kv', 0), 'pkv0')
    plan_cast(wb_in[0], w_in[0], ('wb_in', 0), 'pin0')
    n_first = len(cast_jobs)
    plan_cast(wb_out[0], w_out[0], ('wb_out', 0), 'pout0')
    plan_cast(gtb, gtab, 'gtb', 'pgtb')
    n_second = len(cast_jobs) - n_first
    plan_cast(wb_up[0], w_up[0], ('wb_up', 0), 'pup0')
    plan_cast(wb_down[0], w_down[0], ('wb_down', 0), 'pdn0')
    if layers > 1:
        plan_cast(wb_kv[1], w_kv[1], ('wb_kv', 1), 'pkv1')
        plan_cast(wb_in[1], w_in[1], ('wb_in', 1), 'pin1')
        plan_cast(wb_out[1], w_out[1], ('wb_out', 1), 'pout1')
        plan_cast(wb_up[1], w_up[1], ('wb_up', 1), 'pup1')
        plan_cast(wb_down[1], w_down[1], ('wb_down', 1), 'pdn1')
    n_rest = len(cast_jobs) - n_first - n_second
    issue_casts(n_first)

    def rstd_chain(sumbank, inv_n, msb, rsb, kms, krs):
        P.op('act', ACTF(msb, ps[:, sumbank, :], AF.Sqrt, scale=inv_n, bias=epsc),
             reads=[('ps', sumbank), 'epsc'], writes=[kms])
        P.op('dve', RCP(rsb, msb), reads=[kms], writes=[krs])

    def sweepA(l, mode):
        A.off = base_off
        P.barrier()
        xtok = [A.f32(D), A.f32(D)] if mode != 'x1' else None
        xT = A.f32(KC, T)
        sq = [A.bf16(T), A.bf16(T), A.bf16(T)]
        msb = [A.f32(T), A.f32(T), A.f32(T)]
        rsb = [A.f32(T), A.f32(T), A.f32(T)]
        pfs = [A.f32(T), A.f32(T), A.f32(T)]
        h = A.bf16(KC, T)
        wblk = [A.bf16(KC, 256), A.bf16(KC, 256), A.bf16(KC, 256)]
        qstage = A.bf16(8, T)
        kstage = A.bf16(8, T)
        qmstage = A.bf16(4, T)
        vstage = A.bf16(4, NAW)
        ustage = A.f32(4, T)
        cstage = A.bf16(4, T)
        thb = [A.f32(T), A.f32(T)]
        PB = Rot([0, 1, 2, 3])
        SB = Rot([4, 5])
        VB = Rot([6, 7])
        tag = f"A{l}{mode}"
        ntile = NT if mode != 'mem' else 1
        gcol = P_GM if mode == 'mem' else P_G1
        wsrc = (wb_kv if mode == 'mem' else wb_in)[l]
        wkey = ('wb_kv', l) if mode == 'mem' else ('wb_in', l)
        nblk = (2 * MW if mode == 'mem' else INW) // 256
        wv = wsrc.rearrange("(c p) n -> p c n", p=128)
        wrot = Rot([0, 1, 2])
        slot2 = Rot([0, 1, 2])
        thr = Rot([0, 1])

        for i in range(ntile):
            t0, t1 = i * T, (i + 1) * T
            if mode == 'x1':
                P.dma('sp', xT, fm(X1T)[:, :, t0:t1], 'xT', reads=[('X1T', i, oc_) for oc_ in range(KC)],
                      writes=[('xT', kc) for kc in range(KC)])
            else:
                src = xin if mode == 'x0' else memin
                for s in range(4):
                    sl = s % 2
                    P.dma('sp', xtok[sl], src[t0 + s * 128:t0 + (s + 1) * 128, :], f'xtok{sl}',
                          writes=[('xtok', sl)])
                    for g in range(4):
                        b = VB.next()
                        P.group('pe', [TR(ps[:, b, q * 128:(q + 1) * 128],
                                          xtok[sl][:, (g * 4 + q) * 128:(g * 4 + q + 1) * 128], ident_f)
                                       for q in range(4)],
                                reads=[('xtok', sl), 'ident_f'], writes=[('ps', b)])
                        P.op('dve', CP(xT[:, g * 4:(g + 1) * 4, s * 128:(s + 1) * 128],
                                       ps[:, b, :].rearrange("p (a b) -> p a b", a=4)),
                             reads=[('ps', b)], writes=[('xT', g * 4 + q) for q in range(4)])
                if mode == 'x0':
                    P.dma('pool', fm(XT)[:, :, t0:t1], xT, 'stXT', reads=[('xT', kc) for kc in range(KC)],
                          writes=[('XT', i)])
            sb = SB.next()
            for kc in range(KC):
                sl = kc % 2
                P.op('act', ACTF(sq[sl], xT[:, kc, :], AF.Square), reads=[('xT', kc)], writes=[('sq', sl)])
                P.op('pe', MM(ps[:, sb, :], ones_b, sq[sl], kc == 0, kc == KC - 1),
                     reads=[('sq', sl), 'ones_b'], writes=[('ps', sb)])
            rstd_chain(sb, 1.0 / D, msb[0], rsb[0], ('ms', 0), ('rs', 0))
            for kc in range(KC):
                P.op('dve', STT(h[:, kc, :], xT[:, kc, :], col(gcol + kc), rsb[0], ALU.mult, ALU.mult),
                     reads=[('xT', kc), ('rs', 0), 'pv'], writes=[('h', kc)])
            hkeys = [('h', kc) for kc in range(KC)]

            pending = []

            def flush():
                for pa, pb in pending:
                    pa()
                    pb()
                pending.clear()

            def qk_stage1(bank, dst, gofs):
                s2 = slot2.next()
                prev = list(pending)
                pending.clear()
                for pa, _ in prev:
                    pa()
                P.op('dve', CP(pfs[s2], ps[:, bank, :]), reads=[('ps', bank)], writes=[('pf', s2)])
                P.op('act', ACTF(sq[s2], pfs[s2], AF.Square), reads=[('pf', s2)], writes=[('sq', s2)])
                for _, pb in prev:
                    pb()
                sb2 = SB.next()

                def part_a():
                    P.op('pe', MM(ps[:, sb2, :], ones_b, sq[s2], True, True),
                         reads=[('sq', s2), 'ones_b'], writes=[('ps', sb2)])
                    P.op('act', ACTF(msb[s2], ps[:, sb2, :], AF.Sqrt, scale=1.0 / 128, bias=epsc),
                         reads=[('ps', sb2), 'epsc'], writes=[('ms', s2)])

                def part_b():
                    P.op('dve', RCP(rsb[s2], msb[s2]), reads=[('ms', s2)], writes=[('rs', s2)])
                    a_, b_ = pfs[s2], rsb[s2]
                    if len(dst[0].shape) == 3:
                        a_ = a_.rearrange("p (a b) -> p a b", a=2)
                        b_ = b_.rearrange("p (a b) -> p a b", a=2)
                    P.op('dve', STT(dst[0], a_, col(gofs), b_, ALU.mult, ALU.mult),
                         reads=[('pf', s2), ('rs', s2), 'pv'], writes=[dst[1]])
                pending.append((part_a, part_b))

            for b in range(nblk):
                ws = wrot.next()
                P.dma('sp', wblk[ws], wv[:, :, b * 256:(b + 1) * 256], f'wA{ws}', reads=[wkey],
                      writes=[('wA', ws)])
                col0 = b * 256
                if mode == 'mem':
                    kind = 'km' if col0 < MW else 'vm'
                else:
                    kind = ('q' if col0 < 1024 else 'k' if col0 < 2048 else 'v' if col0 < 3072
                            else 'qm' if col0 < 3584 else 'u' if col0 < 4096 else 'g')
                if kind in ('v', 'vm'):
                    for s in range(4):
                        bank = VB.next()
                        P.group('pe', [MM(ps[:, bank, 0:256], h[:, kc, s * 128:(s + 1) * 128], wblk[ws][:, kc, :],
                                          kc == 0, kc == KC - 1) for kc in range(KC)],
                                reads=hkeys + [('wA', ws)], writes=[('ps', bank)])
                        flush()
                        if kind == 'v':
                            vc = col0 - 2048
                            dst, dkey = vstage[:, s, vc:vc + 256], ('vst', s)
                        else:
                            vc = col0 - MW
                            dst, dkey = Vm[:, s // 2, s % 2, vc:vc + 256], ('Vm', s)
                        eng = 'dve' if (s % 2 == 0) else 'act'
                        fn = CP(dst, ps[:, bank, 0:256]) if eng == 'dve' else ACTF(dst, ps[:, bank, 0:256], AF.Copy)
                        P.op(eng, fn, reads=[('ps', bank)], writes=[dkey])
                    continue
                for o in range(2):
                    oc = (col0 + o * 128) // 128
                    bank = PB.next()
                    P.group('pe', [MM(ps[:, bank, :], wblk[ws][:, kc, o * 128:(o + 1) * 128], h[:, kc, :],
                                      kc == 0, kc == KC - 1) for kc in range(KC)],
                            reads=hkeys + [('wA', ws)], writes=[('ps', bank)])
                    if kind not in ('q', 'k', 'qm', 'km'):
                        flush()
                    if kind == 'q':
                        qk_stage1(bank, (qstage[:, oc, :], ('qst', oc)), P_GQ)
                    elif kind == 'k':
                        qk_stage1(bank, (kstage[:, oc - 8, :], ('kst', oc - 8)), P_GK)
                    elif kind == 'qm':
                        qk_stage1(bank, (qmstage[:, oc - 24, :], ('qmst', oc - 24)), P_GQM)
                    elif kind == 'km':
                        qk_stage1(bank, (Km[:, :, oc, :], ('Km', oc)), P_GKM)
                    elif kind == 'u':
                        j = oc - 28
                        P.op('dve', CP(ustage[:, j, :], ps[:, bank, :]), reads=[('ps', bank)], writes=[('ust', j)])
                    elif kind == 'g':
                        j = oc - 32
                        s2 = thr.next()
                        P.op('act', ACTF(thb[s2], ps[:, bank, :], AF.Tanh, scale=0.5),
                             reads=[('ps', bank)], writes=[('th', s2)])
                        P.op('dve', STT(cstage[:, j, :], thb[s2], 1.0, ustage[:, j, :], ALU.add, ALU.mult),
                             reads=[('th', s2), ('ust', j)], writes=[('cst', j)])
            flush()
            if mode != 'mem':
                P.dma('pool', fm(QT)[:, :, t0:t1], qstage, 'stQ', reads=[('qst', j) for j in range(8)],
                      writes=[('QT', i)])
                P.dma('pool', fm(KT)[:, :, t0:t1], kstage, 'stK', reads=[('kst', j) for j in range(8)],
                      writes=[('KT', i)])
                P.dma('pool', fm(QMT)[:, :, t0:t1], qmstage, 'stQM', reads=[('qmst', j) for j in range(4)],
                      writes=[('QMT', i)])
                P.dma('pool', VV[t0:t1, :].rearrange("(s p) c -> p s c", p=128), vstage, 'stV',
                      reads=[('vst', s) for s in range(4)], writes=[('VV', i)])
                P.dma('pool', fm(CT)[:, :, t0:t1], cstage, 'stC', reads=[('cst', j) for j in range(4)],
                      writes=[('CT', i)])
            if mode == 'x0':
                issue_casts(-(-(n_rest + n_second) // NT))

    def sweepB(l):
        A.off = base_off
        P.barrier()
        qt = A.bf16(8, T)
        qmt = A.bf16(4, T)
        Kw = [A.bf16(1024), A.bf16(1024)]
        Vw = [A.bf16(8, 128), A.bf16(8, 128)]
        Gp = [A.bf16(NSLOT, 128), A.bf16(NSLOT, 128)]
        Pt = [A.bf16(6, 64), A.bf16(6, 64), A.bf16(6, 64)]
        Ptm = [[A.bf16(T), A.bf16(T)], [A.bf16(T), A.bf16(T)]]
        rden = [A.f32(T), A.f32(T)]
        cat = A.bf16(KC, T)
        cwin = A.bf16(4, T + 30)
        dg = [A.bf16(128) for _ in range(8)]
        cv = A.f32(4, T)
        sqf = [A.f32(T), A.f32(T)]
        msb = A.f32(T)
        rsb = A.f32(T)
        thb = [A.f32(T), A.f32(T)]
        xch = [A.f32(T), A.f32(T)]
        xmst = [A.f32(T), A.f32(T)]
        sq = [A.bf16(T), A.bf16(T)]
        h2 = A.bf16(KC, T)
        wblk = [A.bf16(KC, 256), A.bf16(KC, 256), A.bf16(KC, 256)]
        SC = Rot([0, 1, 6])
        OB = Rot([2, 3])
        DBk = Rot([4, 5])
        P2 = Rot([7])
        hrot = Rot([0, 1])
        prot = Rot([0, 1, 2])
        wrot = Rot([0, 1, 2])
        r2 = Rot([0, 1])
        dgr = Rot(range(8))
        wv = wb_out[l].rearrange("(c p) n -> p c n", p=128)
        KTv, QTv, QMTv, CTv = fm(KT), fm(QT), fm(QMT), fm(CT)
        gtv = gtb.rearrange("a b -> (a b)")[0:L * NPACK * 8 * NSLOT * GW * 2 * GW].rearrange(
            "(l k h s q c) -> l k h q s c", l=L, k=NPACK, h=8, s=NSLOT, q=GW)
        Xsrc = XT if l == 0 else X1T
        xkey = 'XT' if l == 0 else 'X1T'
        mid = NT // 2

        for i in range(NT):
            t0, t1 = i * T, (i + 1) * T
            tg = geo['tiles'][i]
            w0, w1 = tg['w0'], tg['w1']
            nwin = (w1 - w0) * GW
            ktiles = sorted(set(range((w0 * GW) // T, ((w1 * GW) - 1) // T + 1)))
            P.dma('sp', qt, QTv[:, :, t0:t1], 'ldq', reads=[('QT', i)], writes=['qt'])
            P.dma('sp', qmt, QMTv[:, :, t0:t1], 'ldqm', reads=[('QMT', i)], writes=['qmt'])
            lo, hi = t0 - 15, t1 + 15
            clo, chi = max(lo, 0), min(hi, LT)
            if clo > lo:
                P.op('pool', MSET(cwin[:, :, 0:15], 0.0), writes=['cwin'])
            if chi < hi:
                P.op('pool', MSET(cwin[:, :, T + 15:T + 30], 0.0), writes=['cwin'])
            P.dma('sp', cwin[:, :, clo - lo:chi - lo], CTv[:, :, clo:chi], 'ldc',
                  reads=[('CT', j) for j in range(max(i - 1, 0), min(i + 2, NT))], writes=['cwin'])
            if i == mid:
                P.op('dve', TS(cwin[:, :, 0:15], cwin[:, :, 0:15], flag[:, 0:1], ALU.mult),
                     reads=['cwin', 'flag'], writes=['cwin'])
            if i == mid - 1:
                P.op('dve', TS(cwin[:, :, T + 15:T + 30], cwin[:, :, T + 15:T + 30], flag[:, 0:1], ALU.mult),
                     reads=['cwin', 'flag'], writes=['cwin'])

            def na_head(hd):
                    hs = hrot.next()
                    P.dma('sp', Kw[hs][:, 0:nwin], KTv[:, hd, w0 * GW:w1 * GW], f'ldk{hs}',
                          reads=[('KT', j) for j in ktiles], writes=[('Kw', hs)])
                    P.dma('sp', Vw[hs][:, 0:(w1 - w0) // 2, :],
                          VV[w0 * GW:w1 * GW, hd * 128:(hd + 1) * 128].rearrange("(c p) d -> p c d", p=128),
                          f'ldv{hs}', reads=[('VV', j) for j in ktiles], writes=[('Vw', hs)])
                    P.dma('sp', Gp[hs][0:GW, :, :], gtv[l, tg['pack'], hd], f'ldg{hs}', reads=['gtb'],
                          writes=[('Gp', hs)])
                    ob, db = OB.next(), DBk.next()
                    def row_score(rl):
                        r = 8 * i + rl
                        start, nch, _ = geo['rows'][r]
                        sc = SC.next()
                        pp = prot.next()
                        mms = []
                        for ci in range(nch):
                            ko = (start + 2 * ci - w0) * GW
                            o_ = ps[:, sc, ci * 64:(ci + 1) * 64]
                            mms.append(MM(o_, Kw[hs][:, ko:ko + 128], qt[:, hd, rl * 64:(rl + 1) * 64], True, False))
                            mms.append(MM(o_, Gp[hs][0:GW, tg['slotmap'][(rl, ci)], :], ident_b[0:GW, 0:GW],
                                          False, True))
                        P.group('pe', mms, reads=[('Kw', hs), ('Gp', hs), 'qt', 'ident_b'], writes=[('ps', sc)])
                        P.op('act', ACTF(Pt[pp][:, 0:nch, :],
                                         ps[:, sc, 0:nch * 64].rearrange("p (a b) -> p a b", a=nch), AF.Exp),
                             reads=[('ps', sc)], writes=[('Pt', pp)])
                        return (rl, start, nch, pp)

                    def row_pv(st):
                        rl, start, nch, pp = st
                        mms = []
                        for ci in range(nch):
                            vi = (start + 2 * ci - w0) // 2
                            mms.append(MM(ps[:, ob, rl * 64:(rl + 1) * 64], Vw[hs][:, vi, :], Pt[pp][:, ci, :],
                                          ci == 0, ci == nch - 1))
                        for ci in range(nch):
                            mms.append(MM(ps[:, db, rl * 64:(rl + 1) * 64], ones_b, Pt[pp][:, ci, :],
                                          ci == 0, ci == nch - 1))
                        P.group('pe', mms, reads=[('Vw', hs), ('Pt', pp), 'ones_b'],
                                writes=[('ps', ob), ('ps', db)])

                    prev = None
                    for rl in range(8):
                        cur = row_score(rl)
                        if prev is not None:
                            row_pv(prev)
                        prev = cur
                    row_pv(prev)
                    rr = r2.next()
                    P.op('dve', RCP(rden[rr], ps[:, db, :]), reads=[('ps', db)], writes=[('rden', rr)])
                    P.op('dve', TT(cat[:, hd, :], ps[:, ob, :], rden[rr], ALU.mult),
                         reads=[('ps', ob), ('rden', rr)], writes=[('cat', hd)])


            def conv_part():
                for j in range(4):
                    cbk = P2.next()
                    for k in range(31):
                        ds = dgr.next()
                        P.op('dve', TS(dg[ds], ident_b, col(P_CW + j * 31 + k), ALU.mult),
                             reads=['ident_b', 'pv'], writes=[('dg', ds)])
                        P.op('pe', MM(ps[:, cbk, :], dg[ds], cwin[:, j, k:k + T], k == 0, k == 30),
                             reads=[('dg', ds), 'cwin'], writes=[('ps', cbk)])
                    P.op('act', ACTF(cv[:, j, :], ps[:, cbk, :], AF.Identity, bias=col(P_CB + j)),
                         reads=[('ps', cbk), 'pv'], writes=[('cv', j)])

            def ln_part1():
                mb = P2.next()
                P.group('pe', [MM(ps[:, mb, :], ones_f, cv[:, j, :], j == 0, j == 3) for j in range(4)],
                        reads=[('cv', j) for j in range(4)] + ['ones_f'], writes=[('ps', mb)])
                for j in range(4):
                    P.op('dve', STT(cv[:, j, :], ps[:, mb, :], -1.0 / CC, cv[:, j, :], ALU.mult, ALU.add),
                         reads=[('ps', mb), ('cv', j)], writes=[('cv', j)])

            def ln_part2():
                vb = P2.next()
                for j in range(4):
                    s2 = j % 2
                    P.op('act', ACTF(sqf[s2], cv[:, j, :], AF.Square), reads=[('cv', j)], writes=[('sqf', s2)])
                    P.op('pe', MM(ps[:, vb, :], ones_f, sqf[s2], j == 0, j == 3),
                         reads=[('sqf', s2), 'ones_f'], writes=[('ps', vb)])
                P.op('act', ACTF(msb, ps[:, vb, :], AF.Sqrt, scale=1.0 / CC, bias=epsc),
                     reads=[('ps', vb), 'epsc'], writes=['msB'])
                P.op('dve', RCP(rsb, msb), reads=['msB'], writes=['rsB'])
                for j in range(4):
                    P.op('dve', TT(cv[:, j, :], cv[:, j, :], rsb, ALU.mult), reads=[('cv', j), 'rsB'],
                         writes=[('cv', j)])
                    P.op('dve', TS(cv[:, j, :], cv[:, j, :], col(P_LNG + j), ALU.mult, col(P_LNB + j), ALU.add),
                         reads=[('cv', j), 'pv'], writes=[('cv', j)])
                    s2 = j % 2
                    P.op('act', ACTF(thb[s2], cv[:, j, :], AF.Tanh), reads=[('cv', j)], writes=[('thB', s2)])
                    P.op('dve', STT(cat[:, 12 + j, :], thb[s2], 1.0, cv[:, j, :], ALU.add, ALU.mult),
                         reads=[('thB', s2), ('cv', j)], writes=[('cat', 12 + j)])

            conv_part()
            na_head(0)
            na_head(1)
            ln_part1()
            na_head(2)
            na_head(3)
            ln_part2()
            for hd in range(4, 8):
                na_head(hd)

            mi = 0 if i < mid else 1

            def mem_score(hm):
                par = hm % 2
                for ch in range(2):
                    sc = SC.next()
                    P.group('pe', [MM(ps[:, sc, :], Km[:, mi, hm, ch * 128:(ch + 1) * 128], qmt[:, hm, :], True, True)],
                            reads=[('Km', hm), 'qmt'], writes=[('ps', sc)])
                    P.op('act', ACTF(Ptm[par][ch], ps[:, sc, :], AF.Exp), reads=[('ps', sc)],
                         writes=[('Ptm', par, ch)])

            def mem_pv(hm):
                par = hm % 2
                ob, db = OB.next(), DBk.next()
                mms = [MM(ps[:, ob, :], Vm[:, mi, ch, hm * 128:(hm + 1) * 128], Ptm[par][ch], ch == 0, ch == 1)
                       for ch in range(2)]
                mms += [MM(ps[:, db, :], ones_b, Ptm[par][ch], ch == 0, ch == 1) for ch in range(2)]
                P.group('pe', mms, reads=[('Vm', s_) for s_ in range(4)] + [('Ptm', par, 0), ('Ptm', par, 1), 'ones_b'],
                        writes=[('ps', ob), ('ps', db)])
                rr = r2.next()
                P.op('dve', RCP(rden[rr], ps[:, db, :]), reads=[('ps', db)], writes=[('rden', rr)])
                P.op('dve', TT(cat[:, 8 + hm, :], ps[:, ob, :], rden[rr], ALU.mult),
                     reads=[('ps', ob), ('rden', rr)], writes=[('cat', 8 + hm)])

            for hm in range(4):
                mem_score(hm)
                if hm > 0:
                    mem_pv(hm - 1)
            mem_pv(3)

            catkeys = [('cat', k) for k in range(KC)]
            sb = P2.next()
            pend_ss = []

            def flush_ss():
                while pend_ss:
                    oc_, xs_ = pend_ss.pop(0)
                    P.op('pe', MM(ps[:, sb, :], ones_b, sq[xs_], oc_ == 0, oc_ == KC - 1),
                         reads=[('sqB', xs_), 'ones_b'], writes=[('ps', sb)])
            for b in range(8):
                ws = wrot.next()
                P.dma('sp', wblk[ws], wv[:, :, b * 256:(b + 1) * 256], f'wB{ws}', reads=[('wb_out', l)],
                      writes=[('wB', ws)])
                for o in range(2):
                    oc = b * 2 + o
                    xs = oc % 2
                    P.dma('sp', xch[xs], fm(Xsrc)[:, oc, t0:t1], f'ldx{xs}', reads=[('XT', i) if l == 0 else ('X1T', i, oc)], writes=[('xch', xs)])
                    bank = SC.next()
                    P.group('pe', [MM(ps[:, bank, :], wblk[ws][:, kc, o * 128:(o + 1) * 128], cat[:, kc, :],
                                      kc == 0, kc == KC - 1) for kc in range(KC)],
                            reads=catkeys + [('wB', ws)], writes=[('ps', bank)])
                    flush_ss()
                    P.op('dve', TT(xmst[xs], ps[:, bank, :], xch[xs], ALU.add),
                         reads=[('ps', bank), ('xch', xs)], writes=[('xmst', xs)])
                    P.dma('pool', fm(XM)[:, oc, t0:t1], xmst[xs], f'stxm{xs}', reads=[('xmst', xs)], writes=[('XM', i, oc)])
                    P.op('act', ACTF(sq[xs], xmst[xs], AF.Square), reads=[('xmst', xs)], writes=[('sqB', xs)])
                    pend_ss.append((oc, xs))
                    P.op('pool', TS(h2[:, oc, :], xmst[xs], col(P_G2 + oc), ALU.mult, 0.0, ALU.add),
                         reads=[('xmst', xs), 'pv'], writes=[('h2', oc)])
            flush_ss()
            P.op('act', ACTF(msb, ps[:, sb, :], AF.Sqrt, scale=1.0 / D, bias=epsc),
                 reads=[('ps', sb), 'epsc'], writes=['msB'])
            P.op('dve', RCP(rsb, msb), reads=['msB'], writes=['rsB'])
            for oc in range(KC):
                P.op('dve', TT(h2[:, oc, :], h2[:, oc, :], rsb, ALU.mult), reads=[('h2', oc), 'rsB'],
                     writes=[('h2', oc)])
            h2keys = [('h2', k) for k in range(KC)]
            P.op('dve', CP(h2halo[:, :, 2 * i:2 * i + 1], h2[:, :, 0:1]), reads=h2keys, writes=[('h2halo', 2 * i)])
            P.op('dve', CP(h2halo[:, :, 2 * i + 1:2 * i + 2], h2[:, :, T - 1:T]), reads=h2keys,
                 writes=[('h2halo', 2 * i + 1)])
            P.dma('pool', fm(H2T)[:, :, t0:t1], h2, 'sth2', reads=h2keys, writes=[('H2T', i)])

    def sweepC(l):
        A.off = base_off
        P.barrier()
        h2 = A.bf16(KC, T)
        hid = A.bf16(HC, T)
        wu = [(A.bf16(KC, 256), A.bf16(KC, 256)) for _ in range(2)]
        wd = [A.bf16(HC, 128) for _ in range(3)]
        ag = [A.f32(T), A.f32(T)]
        av = [A.f32(T), A.f32(T)]
        th = [A.f32(T), A.f32(T)]
        xmc = [A.f32(T), A.f32(T)]
        xo = [A.f32(T), A.f32(T)]
        uph = A.bf16(88, 2 * NT)
        edge = A.f32(88, 2)
        GBk = Rot([0, 1])
        VBk = Rot([2, 3])
        OP = Rot([4, 5])
        HB = Rot([6, 7])
        wurot = Rot([0, 1])
        wdrot = Rot([0, 1, 2])
        r2 = Rot([0, 1])
        wuv = wb_up[l].rearrange("(c p) n -> p c n", p=128)
        wdv = wb_down[l].rearrange("(c p) n -> p c n", p=128)
        mid = NT // 2
        Xdst = X1T
        hkeys = [('h2c', k) for k in range(KC)]
        halokeys = [('h2halo', k) for k in range(2 * NT)]
        fwc = lambda c, k: col(P_FW + c * 3 + k)

        for b in range(2 * DFF // 256):
            ws = wurot.next()
            P.dma('sp', wu[ws][0], wuv[:, :, b * 256:(b + 1) * 256], f'wu{ws}', reads=[('wb_up', l)],
                  writes=[('wu', ws)])
            for o in range(2):
                c = b * 2 + o
                hb = HB.next()
                P.group('pe', [MM(ps[:, hb, 0:2 * NT], wu[ws][0][:, kc, o * 128:(o + 1) * 128], h2halo[:, kc, :],
                                  kc == 0, kc == KC - 1) for kc in range(KC)],
                        reads=halokeys + [('wu', ws)], writes=[('ps', hb)])
                P.op('act', ACTF(uph[:, c, :], ps[:, hb, 0:2 * NT], AF.Copy), reads=[('ps', hb)], writes=['uph'])
        P.op('dve', TS(uph[:, :, 2 * mid - 1:2 * mid + 1], uph[:, :, 2 * mid - 1:2 * mid + 1], flag[:, 0:1], ALU.mult),
             reads=['uph', 'flag'], writes=['uph'])

        for i in range(NT):
            t0, t1 = i * T, (i + 1) * T
            P.dma('sp', h2, fm(H2T)[:, :, t0:t1], 'ldh2', reads=[('H2T', i)], writes=hkeys)
            fw3 = pv[:, P_FW:P_FW + 264].rearrange("p (c k) -> p c k", k=3)
            if i > 0:
                P.op('dve', TT(edge[:, :, 0:1], uph[:, :, 2 * i - 1:2 * i], fw3[:, :, 0:1], ALU.mult),
                     reads=['uph', 'pv'], writes=['edge'])
            else:
                P.op('dve', MSET(edge[:, :, 0:1], 0.0), writes=['edge'])
            if i < NT - 1:
                P.op('dve', TT(edge[:, :, 1:2], uph[:, :, 2 * i + 2:2 * i + 3], fw3[:, :, 2:3], ALU.mult),
                     reads=['uph', 'pv'], writes=['edge'])
            else:
                P.op('dve', MSET(edge[:, :, 1:2], 0.0), writes=['edge'])

            for jb in range(HC // 2):
                ws = wurot.next()
                P.dma('sp', wu[ws][0], wuv[:, :, jb * 256:(jb + 1) * 256], f'wu{ws}', reads=[('wb_up', l)],
                      writes=[('wu', ws)])
                P.dma('sp', wu[ws][1], wuv[:, :, DFF + jb * 256:DFF + (jb + 1) * 256], f'wu{ws}',
                      reads=[('wb_up', l)], writes=[('wu', ws)])
                for o in range(2):
                    j = jb * 2 + o
                    gb, vb = GBk.next(), VBk.next()
                    P.group('pe', [MM(ps[:, gb, :], wu[ws][0][:, kc, o * 128:(o + 1) * 128], h2[:, kc, :],
                                      kc == 0, kc == KC - 1) for kc in range(KC)],
                            reads=hkeys + [('wu', ws)], writes=[('ps', gb)])
                    P.group('pe', [MM(ps[:, vb, :], wu[ws][1][:, kc, o * 128:(o + 1) * 128], h2[:, kc, :],
                                      kc == 0, kc == KC - 1) for kc in range(KC)],
                            reads=hkeys + [('wu', ws)], writes=[('ps', vb)])
                    s2 = r2.next()
                    for (bank, dst, c, key) in ((gb, ag[s2], j, 'ag'), (vb, av[s2], HC + j, 'av')):
                        P.op('act', ACTF(dst, ps[:, bank, :], AF.Identity, scale=fwc(c, 1), bias=col(P_FB + c)),
                             reads=[('ps', bank), 'pv'], writes=[(key, s2)])
                        P.op('dve', STT(dst[:, 1:T], ps[:, bank, 0:T - 1], fwc(c, 0), dst[:, 1:T], ALU.mult, ALU.add),
                             reads=[('ps', bank), 'pv', (key, s2)], writes=[(key, s2)])
                        P.op('dve', STT(dst[:, 0:T - 1], ps[:, bank, 1:T], fwc(c, 2), dst[:, 0:T - 1], ALU.mult, ALU.add),
                             reads=[('ps', bank), 'pv', (key, s2)], writes=[(key, s2)])
                        P.op('pool', TT(dst[:, 0:1], dst[:, 0:1], edge[:, c, 0:1], ALU.add),
                             reads=[(key, s2), 'edge'], writes=[(key, s2)])
                        P.op('pool', TT(dst[:, T - 1:T], dst[:, T - 1:T], edge[:, c, 1:2], ALU.add),
                             reads=[(key, s2), 'edge'], writes=[(key, s2)])
                    P.op('act', ACTF(th[s2], ag[s2], AF.Tanh, scale=0.5), reads=[('ag', s2)], writes=[('thC', s2)])
                    P.op('dve', STT(th[s2], th[s2], 1.0, ag[s2], ALU.add, ALU.mult),
                         reads=[('thC', s2), ('ag', s2)], writes=[('thC', s2)])
                    P.op('pool', TT(hid[:, j, :], th[s2], av[s2], ALU.mult),
                         reads=[('thC', s2), ('av', s2)], writes=[('hid', j)])
            hidkeys = [('hid', j) for j in range(HC)]
            for oc in range(KC):
                ws = wdrot.next()
                P.dma('sp', wd[ws], wdv[:, :, oc * 128:(oc + 1) * 128], f'wd{ws}', reads=[('wb_down', l)],
                      writes=[('wd', ws)])
                xs = oc % 2
                P.dma('sp', xmc[xs], fm(XM)[:, oc, t0:t1], f'ldxm{xs}', reads=[('XM', i, oc)], writes=[('xmc', xs)])
                bank = OP.next()
                P.group('pe', [MM(ps[:, bank, :], wd[ws][:, k, :], hid[:, k, :], k == 0, k == HC - 1)
                               for k in range(HC)],
                        reads=hidkeys + [('wd', ws)], writes=[('ps', bank)])
                P.op('dve', TT(xo[xs], ps[:, bank, :], xmc[xs], ALU.add), reads=[('ps', bank), ('xmc', xs)],
                     writes=[('xo', xs)])
                P.dma('pool', fm(Xdst)[:, oc, t0:t1], xo[xs], f'stxo{xs}', reads=[('xo', xs)], writes=[('X1T', i, oc)])

    def sweepD():
        A.off = base_off
        P.barrier()
        xT = [A.f32(KC, T), A.f32(KC, T)]
        yt = [A.f32(D), A.f32(D)]
        TB = Rot(range(8))
        for i in range(NT):
            t0, t1 = i * T, (i + 1) * T
            xs = i % 2
            P.dma('sp', xT[xs], fm(X1T)[:, :, t0:t1], f'ldD{xs}', reads=[('X1T', i, oc_) for oc_ in range(KC)], writes=[('xTD', xs)])
            for s in range(4):
                ys = s % 2
                for g in range(4):
                    b = TB.next()
                    P.group('pe', [TR(ps[:, b, q * 128:(q + 1) * 128], xT[xs][:, g * 4 + q, s * 128:(s + 1) * 128], ident_f)
                                   for q in range(4)], reads=[('xTD', xs), 'ident_f'], writes=[('ps', b)])
                    eng = 'dve' if g % 2 == 0 else 'act'
                    dst = yt[ys][:, g * 512:(g + 1) * 512]
                    fn = CP(dst, ps[:, b, :]) if eng == 'dve' else ACTF(dst, ps[:, b, :], AF.Copy)
                    P.op(eng, fn, reads=[('ps', b)], writes=[('yt', ys)])
                P.dma('sp', yout[t0 + s * 128:t0 + (s + 1) * 128, :], yt[ys], f'stD{ys}', reads=[('yt', ys)],
                      writes=[('yout', i, s)])

    for l in range(layers):
        P.dma('sp', pv, pvec[l], 'pv', writes=['pv'])
        sc = 128.0 ** -0.5
        for (off, n, c) in ((P_GQ, 1, sc), (P_GQM, 1, sc), (P_CW, 124, 0.5), (P_LNG, 8, 0.5),
                            (P_FW + HC * 3, HC * 3, 0.5), (P_FB + HC, HC, 0.5)):
            P.op('dve', TS(col(off, n), col(off, n), c, ALU.mult), reads=['pv'], writes=['pv'])
        if stop_after == ('P', l):
            break
        sweepA(l, 'mem')
        if stop_after == ('M', l):
            break
        sweepA(l, 'x0' if l == 0 else 'x1')
        if stop_after == ('A', l):
            break
        issue_casts(len(cast_jobs))
        sweepB(l)
        if stop_after == ('B', l):
            break
        sweepC(l)
    if stop_after is None:
        sweepD()
        P.wait_all('sp', [('yout', i, s) for i in range(NT) for s in range(4)])
    else:
        P.wait_all('sp', list(P.lastw.keys()))

    semnames = ENGS + sorted(P.dmasems)
    ctxs = [nc.semaphore(f"s_{n}") for n in semnames]
    sems = {n: c.__enter__() for n, c in zip(semnames, ctxs)}
    with nc.Block() as block:
        for e in ENGS:
            ops = P.ops[e]

            def body(eng, ops=ops, e=e):
                for o in ops:
                    if o[0] == 'wait':
                        eng.wait_ge(sems[o[1]], o[2])
                    elif o[0] == 'op':
                        ins = o[1](eng)
                        if o[2]:
                            ins.then_inc(sems[e], 1)
                    else:
                        eng.dma_start(out=o[1], in_=o[2], **o[4]).then_inc(sems[o[3]], 16)
            getattr(block, ENGATTR[e])(body)
    for c in reversed(ctxs):
        c.__exit__(None, None, None)
    ctx_psum.__exit__(None, None, None)
    ctx_arena.__exit__(None, None, None)
    stats = {e: len(P.ops[e]) for e in ENGS}
    stats['nops'] = P.nops
    return nc, geo, stats


def core_inputs(xseq, mem2, ty, shared, NT, geo, inp):
    m = dict(shared)
    m['xin'] = np.ascontiguousarray(xseq, dtype=np.float32)
    m['memin'] = np.ascontiguousarray(mem2.reshape(2 * NMEM, D), dtype=np.float32)
    g = build_gtab(inp['na_rpb'], ty, NT, geo)
    flat = g.reshape(-1)
    ngr = -(-flat.size // (2048 * 128)) * 128
    buf = np.zeros((ngr * 2048,), np.float32)
    buf[:flat.size] = flat
    m['gtab'] = buf.reshape(ngr, 2048)
    m['flagin'] = np.full((128, 1), 1.0 if ty == 'A' else 0.0, np.float32)
    return m


def shared_inputs(inp):
    return {
        'w_in': np.ascontiguousarray(inp['w_in'], dtype=np.float32),
        'w_kv': np.ascontiguousarray(inp['w_mem_kv'], dtype=np.float32),
        'w_out': np.ascontiguousarray(inp['w_out'], dtype=np.float32),
        'w_up': np.ascontiguousarray(inp['w_up'], dtype=np.float32),
        'w_down': np.ascontiguousarray(inp['w_down'], dtype=np.float32),
        'pvec': build_pvec(inp),
        'identin': np.eye(128, dtype=np.float32),
    }


_CACHE = {}


def kernel(**inputs):
    inp = {k: np.asarray(v) for k, v in inputs.items()}
    NT = 16
    if NT not in _CACHE:
        _CACHE[NT] = build_program(NT)
    nc, geo, _ = _CACHE[NT]
    shared = shared_inputs(inp)
    xp, xs, mp, ms = inp['x_prompt'], inp['x_sample'], inp['mem_prompt'], inp['mem_sample']
    maps = []
    for b in range(2):
        maps.append(core_inputs(xs[b], np.stack([ms[b], ms[b]]), 'A', shared, NT, geo, inp))
    for k in range(4):
        maps.append(core_inputs(np.concatenate([xp[2 * k], xp[2 * k + 1]], axis=0),
                                np.stack([mp[2 * k], mp[2 * k + 1]]), 'B', shared, NT, geo, inp))
    for k in range(2):
        maps.append(core_inputs(np.zeros((NT * T, D), np.float32), np.zeros((2, NMEM, D), np.float32), 'B',
                                shared, NT, geo, inp))
    res = run_bass_kernel_spmd(nc, maps, core_ids=list(range(8)))
    outs = [np.asarray(r['yout']) for r in res.results]
    y_sample = np.stack([outs[0], outs[1]]).astype(np.float32)
    y_prompt = np.stack([outs[2 + k // 2][(k % 2) * 4096:(k % 2 + 1) * 4096] for k in range(8)]).astype(np.float32)
    return (y_prompt, y_sample)
```
